# Optimizing a Trainium2 kernel written in Bass

```python
import functools
import jax
import jax.numpy as jnp
from jax import lax
import numpy as np

D_MODEL = 2048
BATCH = 4
SEQ = 2048
DEPTH = 1
DEC_BATCH = 128
DEC_SEQ = 4
PAST_LEN = 16384
PAGE_SIZE = 128

HEAD_DIM = 64
N_RWKV_HEADS = D_MODEL // HEAD_DIM
D_RWKV = N_RWKV_HEADS * HEAD_DIM
D_DECAY_LORA = max(32, int(round(1.8 * D_MODEL ** 0.5 / 32)) * 32)
D_A_LORA = max(32, int(round(1.8 * D_MODEL ** 0.5 / 32)) * 32)
D_GATE_LORA = max(32, int(round(0.6 * D_MODEL ** 0.8 / 32)) * 32)
R_COLS = 3 * D_RWKV + D_DECAY_LORA + D_A_LORA + D_GATE_LORA
N_Q_HEADS = D_MODEL // HEAD_DIM
N_KV_HEADS = 8
GQA_GROUP = N_Q_HEADS // N_KV_HEADS
D_SWA = N_Q_HEADS * HEAD_DIM
D_KV = N_KV_HEADS * HEAD_DIM
WINDOW = 128
BLOCK = WINDOW
D_FF = 4 * D_MODEL
C_IN = R_COLS + D_SWA + 2 * D_KV + 2 * D_MODEL
RWKV_SPLITS = (D_RWKV, 2 * D_RWKV, 3 * D_RWKV, 3 * D_RWKV + D_DECAY_LORA, 3 * D_RWKV + D_DECAY_LORA + D_A_LORA)
IN_SPLITS = (R_COLS, R_COLS + D_SWA, R_COLS + D_SWA + D_KV, R_COLS + D_SWA + 2 * D_KV, R_COLS + D_SWA + 2 * D_KV + D_MODEL)
RMS_EPS = 1e-5
GN_EPS = 64e-5

kernel_name = 'rwkv7_swa_sink_hybrid_step'


def rmsnorm(x, w):
    xf = x.astype(jnp.float32)
    y = xf * lax.rsqrt(jnp.mean(xf * xf, axis=-1, keepdims=True) + RMS_EPS)
    return (y * w.astype(jnp.float32)).astype(x.dtype)


def alibi_slopes():
    h = jnp.arange(N_Q_HEADS, dtype=jnp.float32)
    return jnp.exp2(-8.0 * (h + 1.0) / N_Q_HEADS).reshape(N_KV_HEADS, GQA_GROUP)


def window_attend(q, k, v, dist, valid, sinks):
    f32 = jnp.float32
    s = jnp.einsum('...qhgd,...khd->...hgqk', q.astype(f32), k.astype(f32)) * (HEAD_DIM ** -0.5)
    s = s - alibi_slopes()[:, :, None, None] * dist.astype(f32)
    s = jnp.where(valid, s, -jnp.inf)
    sink = sinks.astype(f32).reshape(N_KV_HEADS, GQA_GROUP)[:, :, None, None]
    m = jnp.maximum(jnp.max(s, axis=-1, keepdims=True), sink)
    p = jnp.exp(s - m)
    den = jnp.sum(p, axis=-1, keepdims=True) + jnp.exp(sink - m)
    o = jnp.einsum('...hgqk,...khd->...qhgd', p / den, v.astype(f32))
    return o.astype(q.dtype)


def swa_prompt(q, k, v, sinks):
    b, t, _ = q.shape
    nb = t // BLOCK
    qb = q.reshape(b, nb, BLOCK, N_KV_HEADS, GQA_GROUP, HEAD_DIM)
    kb = k.reshape(b, nb, BLOCK, N_KV_HEADS, HEAD_DIM)
    vb = v.reshape(b, nb, BLOCK, N_KV_HEADS, HEAD_DIM)

    def with_prev(u):
        prev = jnp.concatenate([jnp.zeros_like(u[:, :1]), u[:, :-1]], axis=1)
        return jnp.concatenate([prev, u], axis=2)

    qi = jnp.arange(BLOCK)[:, None]
    kj = jnp.arange(2 * BLOCK)[None, :]
    dist = qi - kj + BLOCK
    blk = jnp.arange(nb)[:, None, None]
    valid = (dist >= 0) & (dist <= WINDOW) & (blk * BLOCK - BLOCK + kj >= 0)
    o = window_attend(qb, with_prev(kb), with_prev(vb), dist, valid[:, None, None], sinks)
    k_win = k[:, t - WINDOW:].reshape(b, WINDOW, N_KV_HEADS, HEAD_DIM)
    v_win = v[:, t - WINDOW:].reshape(b, WINDOW, N_KV_HEADS, HEAD_DIM)
    return o.reshape(b, t, D_SWA), k_win, v_win


def swa_sample(q, k, v, k_past, v_past, sinks):
    b, t, _ = q.shape
    qh = q.reshape(b, t, N_KV_HEADS, GQA_GROUP, HEAD_DIM)
    kcat = jnp.concatenate([k_past.astype(k.dtype), k.reshape(b, t, N_KV_HEADS, HEAD_DIM)], axis=1)
    vcat = jnp.concatenate([v_past.astype(v.dtype), v.reshape(b, t, N_KV_HEADS, HEAD_DIM)], axis=1)
    qpos = PAST_LEN + jnp.arange(t)
    kpos = jnp.concatenate([PAST_LEN - WINDOW + jnp.arange(WINDOW), PAST_LEN + jnp.arange(t)])
    dist = qpos[:, None] - kpos[None, :]
    valid = (dist >= 0) & (dist <= WINDOW)
    o = window_attend(qh, kcat, vcat, dist, valid, sinks)
    return o.reshape(b, t, D_SWA), kcat[:, -WINDOW:], vcat[:, -WINDOW:]


def wkv_scan(r, w, k, v, a, bvec, s0):
    def step(s, inp):
        r_t, w_t, k_t, v_t, a_t, b_t = inp
        sa = jnp.einsum('bhij,bhj->bhi', s, a_t)
        s = s * w_t[:, :, None, :] + sa[..., None] * b_t[:, :, None, :] + v_t[..., None] * k_t[:, :, None, :]
        return s, jnp.einsum('bhij,bhj->bhi', s, r_t)

    xs = tuple(jnp.swapaxes(u, 0, 1) for u in (r, w, k, v, a, bvec))
    s_fin, ys = lax.scan(step, s0.astype(jnp.float32), xs)
    return jnp.swapaxes(ys, 0, 1), s_fin


def rwkv_branch(p_rwkv, shift_prev, wkv0, tshift_mu, w0, w_lora, a0, a_lora, g_lora, k_k, k_a, r_k, ln_x_w, ln_x_b):
    f32 = jnp.float32
    b, t, _ = p_rwkv.shape
    p_prev = jnp.concatenate([shift_prev[:, None, :].astype(p_rwkv.dtype), p_rwkv[:, :-1]], axis=1)
    ps = p_rwkv + tshift_mu * (p_prev - p_rwkv)
    r, k, v, dw, da, dg = jnp.split(ps, RWKV_SPLITS, axis=-1)
    w_log = -jax.nn.softplus(-(w0 + jnp.tanh(dw) @ w_lora).astype(f32)) - 0.5
    decay = jnp.exp(-jnp.exp(w_log))
    a = jax.nn.sigmoid((a0 + da @ a_lora).astype(f32))
    g = jax.nn.sigmoid(dg) @ g_lora

    def heads(u):
        return u.astype(f32).reshape(b, t, N_RWKV_HEADS, HEAD_DIM)

    kk = heads(k * k_k)
    kk = kk / jnp.maximum(jnp.sqrt(jnp.sum(kk * kk, axis=-1, keepdims=True)), 1e-12)
    a_h = heads(a)
    k_h = heads(k.astype(f32) * (1.0 + (a - 1.0) * k_a.astype(f32)))
    r_h = heads(r)
    v_h = heads(v)
    y, s_fin = wkv_scan(r_h, heads(decay), k_h, v_h, -kk, kk * a_h, wkv0)
    mu = jnp.mean(y, axis=-1, keepdims=True)
    var = jnp.mean(jnp.square(y - mu), axis=-1, keepdims=True)
    yn = ((y - mu) * lax.rsqrt(var + GN_EPS)).reshape(b, t, D_RWKV) * ln_x_w.astype(f32) + ln_x_b.astype(f32)
    bonus = jnp.sum(r_h * k_h * r_k.astype(f32), axis=-1, keepdims=True) * v_h
    out = (yn + bonus.reshape(b, t, D_RWKV)) * g.astype(f32)
    return out.astype(p_rwkv.dtype), p_rwkv[:, -1], s_fin.astype(wkv0.dtype)


def trunk_layer(x, shift_prev, wkv0, swa_fn, norm_mix_w, w_in, tshift_mu, w0, w_lora, a0, a_lora, g_lora,
                k_k, k_a, r_k, ln_x_w, ln_x_b, w_out, norm_mlp_w, w_up, w_down):
    xn = rmsnorm(x, norm_mix_w)
    proj = jnp.einsum('btd,dc->btc', xn, w_in)
    p_rwkv, q, k, v, gate_a, gate_b = jnp.split(proj, IN_SPLITS, axis=-1)
    y_a, shift_new, wkv_new = rwkv_branch(p_rwkv, shift_prev, wkv0, tshift_mu, w0, w_lora, a0, a_lora, g_lora,
                                          k_k, k_a, r_k, ln_x_w, ln_x_b)
    y_b, k_win, v_win = swa_fn(q, k, v)
    mixed = jax.nn.sigmoid(gate_a) * y_a + jax.nn.sigmoid(gate_b) * y_b
    h = x + jnp.einsum('btc,cd->btd', mixed, w_out)
    hn = rmsnorm(h, norm_mlp_w)
    u = jnp.square(jax.nn.relu(jnp.einsum('btd,df->btf', hn, w_up)))
    h = h + jnp.einsum('btf,fd->btd', u, w_down)
    return h, shift_new, wkv_new, k_win, v_win


def setup_inputs(seed: int = 0) -> dict:
    key = jax.random.key(seed)
    ks = jax.random.split(key, 26)
    nrm = jax.random.normal
    f32 = jnp.float32
    L = DEPTH
    return {
        'x_prompt': nrm(ks[0], (BATCH, SEQ, D_MODEL), f32),
        'x_sample': nrm(ks[1], (DEC_BATCH, DEC_SEQ, D_MODEL), f32),
        'state_shift': nrm(ks[2], (L, DEC_BATCH, R_COLS), f32),
        'state_wkv': 0.5 * nrm(ks[3], (L, DEC_BATCH, N_RWKV_HEADS, HEAD_DIM, HEAD_DIM), f32),
        'cache_k_win': nrm(ks[4], (L, DEC_BATCH, WINDOW, N_KV_HEADS, HEAD_DIM), f32),
        'cache_v_win': nrm(ks[5], (L, DEC_BATCH, WINDOW, N_KV_HEADS, HEAD_DIM), f32),
        'norm_mix_w': 1.0 + 0.02 * nrm(ks[6], (L, D_MODEL), f32),
        'w_in': nrm(ks[7], (L, D_MODEL, C_IN), f32) * D_MODEL ** -0.5,
        'tshift_mu': jax.random.uniform(ks[8], (L, R_COLS), f32),
        'w0': jax.random.uniform(ks[9], (L, D_RWKV), f32, -6.0, 0.0),
        'w_lora': 0.5 * nrm(ks[10], (L, D_DECAY_LORA, D_RWKV), f32) * D_DECAY_LORA ** -0.5,
        'a0': 0.1 * nrm(ks[11], (L, D_RWKV), f32),
        'a_lora': 0.5 * nrm(ks[12], (L, D_A_LORA, D_RWKV), f32) * D_A_LORA ** -0.5,
        'g_lora': nrm(ks[13], (L, D_GATE_LORA, D_RWKV), f32) * D_GATE_LORA ** -0.5,
        'k_k': 0.85 + 0.05 * nrm(ks[14], (L, D_RWKV), f32),
        'k_a': 1.0 + 0.05 * nrm(ks[15], (L, D_RWKV), f32),
        'r_k': 0.1 * nrm(ks[16], (L, N_RWKV_HEADS, HEAD_DIM), f32),
        'ln_x_w': 1.0 + 0.02 * nrm(ks[17], (L, D_RWKV), f32),
        'ln_x_b': 0.02 * nrm(ks[18], (L, D_RWKV), f32),
        'attn_sinks': nrm(ks[19], (L, N_Q_HEADS), f32),
        'w_out': nrm(ks[20], (L, D_MODEL, D_MODEL), f32) * D_MODEL ** -0.5,
        'norm_mlp_w': 1.0 + 0.02 * nrm(ks[21], (L, D_MODEL), f32),
        'w_up': nrm(ks[22], (L, D_MODEL, D_FF), f32) * D_MODEL ** -0.5,
        'w_down': nrm(ks[23], (L, D_FF, D_MODEL), f32) * D_FF ** -0.5,
        'norm_final_w': 1.0 + 0.02 * nrm(ks[24], (D_MODEL,), f32),
    }


def reference(x_prompt, x_sample, state_shift, state_wkv, cache_k_win, cache_v_win, norm_mix_w, w_in, tshift_mu,
              w0, w_lora, a0, a_lora, g_lora, k_k, k_a, r_k, ln_x_w, ln_x_b, attn_sinks, w_out, norm_mlp_w,
              w_up, w_down, norm_final_w):
    hp, hs = x_prompt, x_sample
    bp = x_prompt.shape[0]
    sp_l, wp_l, kp_l, vp_l = [], [], [], []
    ss_l, ws_l, ksm_l, vsm_l = [], [], [], []
    for l in range(DEPTH):
        lw = (norm_mix_w[l], w_in[l], tshift_mu[l], w0[l], w_lora[l], a0[l], a_lora[l], g_lora[l], k_k[l], k_a[l],
              r_k[l], ln_x_w[l], ln_x_b[l], w_out[l], norm_mlp_w[l], w_up[l], w_down[l])
        hp, sp, wp, kp, vp = trunk_layer(
            hp, jnp.zeros((bp, R_COLS), hp.dtype), jnp.zeros((bp, N_RWKV_HEADS, HEAD_DIM, HEAD_DIM), hp.dtype),
            functools.partial(swa_prompt, sinks=attn_sinks[l]), *lw)
        hs, ss, ws, ksm, vsm = trunk_layer(
            hs, state_shift[l], state_wkv[l],
            functools.partial(swa_sample, k_past=cache_k_win[l], v_past=cache_v_win[l], sinks=attn_sinks[l]), *lw)
        sp_l.append(sp); wp_l.append(wp); kp_l.append(kp); vp_l.append(vp)
        ss_l.append(ss); ws_l.append(ws); ksm_l.append(ksm); vsm_l.append(vsm)
    y_prompt = rmsnorm(hp, norm_final_w)
    y_sample = rmsnorm(hs, norm_final_w)
    return (y_prompt, y_sample, jnp.stack(sp_l), jnp.stack(wp_l), jnp.stack(kp_l), jnp.stack(vp_l),
            jnp.stack(ss_l), jnp.stack(ws_l), jnp.stack(ksm_l), jnp.stack(vsm_l))
```

```python
import contextlib
import math
import os
_DBG = int(os.environ.get('K_DBG', '0'))
import numpy as np
import concourse.bass as bass
import concourse.mybir as mybir
from concourse.bass_utils import run_bass_kernel_spmd

F32 = mybir.dt.float32
BF16 = mybir.dt.bfloat16
AF = mybir.ActivationFunctionType
ALU = mybir.AluOpType
AX = mybir.AxisListType

D = 2048
RC = 6592
CIN = 13760
DFF = 8192
NCORES = 8
Q0 = RC
KA0 = RC + 2048
VA0 = KA0 + 512
GA0 = VA0 + 512
GB0 = GA0 + 2048
RMS_EPS = 1e-5
GN_EPS = 64e-5
NEG = -1.0e9
SLOPES = [2.0 ** (-8.0 * (h + 1.0) / 32.0) for h in range(32)]
NCP = 164


class _Op:
    __slots__ = ("eng", "fn", "dma", "deps", "sig", "semi", "val")

    def __init__(self, eng, fn, dma):
        self.eng = eng
        self.fn = fn
        self.dma = dma
        self.deps = []
        self.sig = False
        self.semi = None
        self.val = 0


class Sched:
    ENGS = ("pe", "act", "dve", "pool", "sp")
    NDMA = 8

    def __init__(self, nc):
        self.nc = nc
        self.ops = {e: [] for e in self.ENGS}
        self.last_w = {}
        self.rd_c = {}
        self.rd_d = {}
        self.dma_hist = {e: [] for e in self.ENGS}
        self.out_dmas = []
        self.nops = 0

    def op(self, eng, fn, reads=(), writes=(), dma=False, is_out=False):
        o = _Op(eng, fn, dma)
        self.nops += 1
        deps = []
        for k in reads:
            w = self.last_w.get(k)
            if w is not None:
                deps.append(w)
        for k in writes:
            w = self.last_w.get(k)
            if w is not None:
                deps.append(w)
            c = self.rd_c.get(k)
            if c:
                deps.extend(c.values())
            dd = self.rd_d.get(k)
            if dd:
                deps.extend(dd)
        if dma:
            h = self.dma_hist[eng]
            if len(h) >= self.NDMA:
                deps.append(h[-self.NDMA])
            h.append(o)
        seen = set()
        for d in deps:
            if id(d) in seen or d is o:
                continue
            seen.add(id(d))
            if d.eng != eng or d.dma or eng != "pe":
                o.deps.append(d)
                d.sig = True
        for k in reads:
            if dma:
                self.rd_d.setdefault(k, []).append(o)
            else:
                self.rd_c.setdefault(k, {})[eng] = o
        for k in writes:
            self.last_w[k] = o
            self.rd_c[k] = {}
            self.rd_d[k] = []
        self.ops[eng].append(o)
        if is_out:
            self.out_dmas.append(o)
        return o

    def fence(self, fn):
        keys = set(self.last_w.keys()) | set(self.rd_c.keys()) | set(self.rd_d.keys())
        keys = sorted(keys)
        self.op("dve", fn, reads=keys, writes=keys)

    def emit(self):
        nc = self.nc
        fin = _Op("sp", None, False)
        for d in self.out_dmas:
            fin.deps.append(d)
            d.sig = True
        self.ops["sp"].append(fin)
        sem_names = ["c_" + e for e in self.ENGS]
        for e in self.ENGS:
            for i in range(self.NDMA):
                sem_names.append("d_%s_%d" % (e, i))
        with contextlib.ExitStack() as st:
            sems = {n: st.enter_context(nc.semaphore(n)) for n in sem_names}
            for e in self.ENGS:
                cnt = 0
                dcnt = [0] * self.NDMA
                di = 0
                for o in self.ops[e]:
                    if o.dma:
                        s = di % self.NDMA
                        di += 1
                        dcnt[s] += 16
                        o.semi = "d_%s_%d" % (e, s)
                        o.val = dcnt[s]
                        o.sig = True
                    elif o.sig:
                        cnt += 1
                        o.semi = "c_" + e
                        o.val = cnt
            block = st.enter_context(nc.Block())
            engmap = {"pe": block.tensor, "act": block.scalar, "dve": block.vector,
                      "pool": block.gpsimd, "sp": block.sync}

            def make(e):
                def body(eng):
                    waited = {}
                    for o in self.ops[e]:
                        for d in o.deps:
                            if waited.get(d.semi, 0) >= d.val:
                                continue
                            waited[d.semi] = d.val
                            eng.wait_ge(sems[d.semi], d.val)
                        if o.fn is None:
                            continue
                        ins = o.fn(eng)
                        if o.sig:
                            ins.then_inc(sems[o.semi], 16 if o.dma else 1)
                return body

            for e in self.ENGS:
                if self.ops[e]:
                    engmap[e](make(e))


class KB:
    def __init__(self):
        self.nc = bass.Bass("TRN2", target_bir_lowering=False)
        self.S = Sched(self.nc)
        self.st = contextlib.ExitStack()

    def din(self, name, shape):
        return self.nc.dram_tensor(name, list(shape), F32, kind="ExternalInput").ap()

    def dout(self, name, shape):
        return self.nc.dram_tensor(name, list(shape), F32, kind="ExternalOutput").ap()

    def sb(self, name, shape, dt=F32):
        return self.st.enter_context(self.nc.sbuf_tensor(name, list(shape), dt))

    def ps(self, name, shape, dt=F32):
        return self.st.enter_context(self.nc.psum_tensor(name, list(shape), dt))

    def mm(self, out, lhsT, rhs, start, stop, r, w):
        self.S.op("pe", lambda e: e.matmul(out, lhsT=lhsT, rhs=rhs, start=start, stop=stop), r, w)

    def tr(self, out, in_, ident, r, w):
        self.S.op("pe", lambda e: e.transpose(out, in_, ident), r, w)

    def act(self, out, in_, func, r, w, bias=0.0, scale=1.0, accum=None):
        if accum is None:
            self.S.op("act", lambda e: e.activation(out=out, in_=in_, func=func, bias=bias, scale=scale), r, w)
        else:
            self.S.op("act", lambda e: e.activation(out=out, in_=in_, func=func, bias=bias, scale=scale,
                                                    accum_out=accum), r, w)

    def tt(self, out, in0, in1, op, r, w, eng="dve"):
        self.S.op(eng, lambda e: e.tensor_tensor(out=out, in0=in0, in1=in1, op=op), r, w)

    def ts(self, out, in0, s1, s2, op0, op1, r, w, eng="dve"):
        if s2 is None:
            self.S.op(eng, lambda e: e.tensor_single_scalar(out=out, in_=in0, scalar=s1, op=op0), r, w)
        else:
            self.S.op(eng, lambda e: e.tensor_scalar(out=out, in0=in0, scalar1=s1, scalar2=s2, op0=op0, op1=op1), r, w)

    def stt(self, out, in0, scalar, in1, op0, op1, r, w):
        self.S.op("dve", lambda e: e.scalar_tensor_tensor(out=out, in0=in0, scalar=scalar, in1=in1,
                                                          op0=op0, op1=op1), r, w)

    def cp(self, out, in_, r, w, eng="dve"):
        if eng == "act" and not (_DBG & 8):
            eng = "dve"
        if eng == "act":
            self.S.op("act", lambda e: e.activation(out=out, in_=in_, func=AF.Copy), r, w)
        else:
            self.S.op(eng, lambda e: e.tensor_copy(out=out, in_=in_), r, w)

    def red(self, out, in_, op, r, w):
        self.S.op("dve", lambda e: e.tensor_reduce(out=out, in_=in_, axis=AX.X, op=op), r, w)

    def recip(self, out, in_, r, w):
        self.S.op("dve", lambda e: e.reciprocal(out=out, in_=in_), r, w)

    def memset(self, ap, val, w, eng="dve"):
        self.S.op(eng, lambda e: e.memset(ap, val), (), w)

    def dma(self, q, out, in_, r, w, is_out=False):
        self.S.op(q, lambda e: e.dma_start(out=out, in_=in_), r, w, dma=True, is_out=is_out)


def bc(ap, shape):
    return ap.broadcast_to(list(shape))


def build_program(npass=5, stop_phase=99, passes=None):
    k = KB()
    nc = k.nc
    xpre = k.din("xpre", [1024, D])
    xmain = k.din("xmain", [1024, D])
    xsmp = k.din("xsmp", [64, D])
    sshift = k.din("sshift", [16, RC])
    swkv = k.din("swkv", [16, 16, 128, 64])
    ckd = k.din("ck", [16, 128, 512])
    cvd = k.din("cv", [16, 128, 512])
    w_in = k.din("w_in", [D, CIN])
    w_out = k.din("w_out", [D, D])
    w_up = k.din("w_up", [D, DFF])
    w_down = k.din("w_down", [DFF, D])
    w_lora = k.din("w_lora", [96, D])
    a_lora = k.din("a_lora", [96, D])
    g_lora = k.din("g_lora", [256, D])
    nmix = k.din("nmix", [D])
    nmlp = k.din("nmlp", [D])
    nfin = k.din("nfin", [D])
    cpd = k.din("cp", [128, NCP])
    sinkbd = k.din("sinkb", [128, 32])
    sinktd = k.din("sinkt", [16, 8])
    identd = k.din("ident", [128, 128])
    mask64d = k.din("mask64", [64, 3, 64])
    mask4d = k.din("mask4", [4, 3, 4])
    bonesd = k.din("bones", [128, 128])
    dmd = k.din("dm", [128, 256])
    dm0d = k.din("dm0", [128, 256])
    biascd = k.din("biasc", [16, 8, 128])
    biasnd = k.din("biasn", [16, 8, 4])

    y_main = k.dout("y_main", [1024, D])
    y_smp = k.dout("y_smp", [64, D])
    shift_p = k.dout("shift_p", [RC, 1])
    wkv_p = k.dout("wkv_p", [16, 128, 64])
    kwin_p = k.dout("kwin_p", [128, 512])
    vwin_p = k.dout("vwin_p", [128, 512])
    shift_s = k.dout("shift_s", [16, RC])
    wkv_s = k.dout("wkv_s", [16, 16, 128, 64])
    kwin_s = k.dout("kwin_s", [16, 128, 512])
    vwin_s = k.dout("vwin_s", [16, 128, 512])

    XN = k.sb("XN", [128, 16, 512], BF16)
    MIX = k.sb("MIX", [128, 16, 512], BF16)
    H = k.sb("H", [128, 4, D], F32)
    WA = k.sb("WA", [128, 12288], BF16)
    NW = k.sb("NW", [128, D], F32)
    XNT = k.sb("XNT", [128, D], BF16)
    PEXT = k.sb("PEXT", [128, 520], F32)
    Rt = k.sb("Rt", [128, 512], F32)
    Kt = k.sb("Kt", [128, 512], F32)
    Vt = k.sb("Vt", [128, 512], F32)
    LW = k.sb("LW", [128, 512], F32)
    ASIG = k.sb("ASIG", [128, 512], F32)
    KK = k.sb("KK", [128, 512], F32)
    TMP1 = k.sb("TMP1", [128, 512], F32)
    TMP2 = k.sb("TMP2", [128, 512], F32)
    CUM = k.sb("CUM", [128, 512], F32)
    E1 = k.sb("E1", [128, 512], F32)
    Yt = k.sb("Yt", [128, 512], F32)
    ONES = k.sb("ONES", [128, 512], F32)
    MIXA = PEXT[:, 0:512]
    Gb = k.sb("Gb", [128, 512], BF16)
    SGA = k.sb("SGA", [128, 512], BF16)
    SGB = k.sb("SGB", [128, 512], BF16)
    TW = k.sb("TW", [128, 512], BF16)
    DAb = k.sb("DAb", [128, 512], BF16)
    SG = k.sb("SG", [128, 2, 512], BF16)
    AR = k.sb("AR", [128, 1024], BF16)
    KT = k.sb("KT", [128, 512], BF16)
    BT = k.sb("BT", [128, 512], BF16)
    VB = k.sb("VB", [128, 512], BF16)
    SQ = k.sb("SQ", [128, 512], BF16)
    WC = k.sb("WC", [128, 16], F32)
    BASE = k.sb("BASE", [128, 16], F32)
    TM3 = [k.sb("TM3_%d" % i, [64, 2, 3, 128], BF16) for i in range(2)]
    ABRM = [k.sb("ABRM%d" % i, [64, 4, 128], BF16) for i in range(2)]
    AKRM = [k.sb("AKRM%d" % i, [64, 4, 128], BF16) for i in range(2)]
    PTC = [[k.sb("PTC%d_%d" % (i, j), [64, 4, 2, 64], BF16) for j in range(2)] for i in range(2)]
    PT = [[k.sb("PT%d_%d" % (i, j), [64, 4, 64], BF16) for j in range(2)] for i in range(2)]
    ZB = k.sb("ZB", [64, 2, 64], BF16)
    UB = k.sb("UB", [64, 2, 64], BF16)
    SST = k.sb("SST", [128, 16, 64], F32)
    SBF = k.sb("SBF", [128, 16, 2, 64], BF16)
    KTP = k.sb("KTP", [128, 2, 512], BF16)
    BTP = k.sb("BTP", [128, 2, 512], BF16)
    SHIFT = k.sb("SHIFT", [128, 52], F32)
    CP = k.sb("CP", [128, NCP], F32)
    OMU = k.sb("OMU", [128, 52], F32)
    SINKB = k.sb("SINKB", [128, 32], F32)
    SINKT = k.sb("SINKT", [16, 8], F32)
    IDF = k.sb("IDF", [128, 128], F32)
    IDB = k.sb("IDB", [128, 128], BF16)
    BONES = k.sb("BONES", [128, 128], BF16)
    BONESF = k.sb("BONESF", [128, 128], F32)
    M64 = k.sb("M64", [64, 3, 64], F32)
    M4 = k.sb("M4", [4, 3, 4], F32)
    DM = k.sb("DM", [128, 256], F32)
    DM0 = k.sb("DM0", [128, 256], F32)
    EPSC = k.sb("EPSC", [128, 2], F32)
    LWT = k.sb("LWT", [128, 2, 4, 128], BF16)
    ST = k.sb("ST", [128, 16], F32)
    DUM = k.sb("DUM", [128, 4], F32)
    PA16 = k.sb("PA16", [128, 11264], BF16)
    TS = k.sb("TS", [128, 256], F32)
    BH = k.sb("BH", [128, 2, 256], F32)
    BH0 = k.sb("BH0", [128, 2, 256], F32)
    PB = k.sb("PB", [128, 256], BF16)
    PN = k.sb("PN", [128, 256], BF16)
    PNT = k.sb("PNT", [128, 256], BF16)
    KWO = CUM
    VWO = E1
    SWI = k.sb("SWI", [128, 128], F32)
    SOB = k.sb("SOB", [128, 128], F32)
    SOUT = [k.sb("SOUT%d" % i, [128, 64], F32) for i in range(2)]
    SS = [k.sb("SS%d" % i, [128, 64], F32) for i in range(2)]
    SSB = [k.sb("SSB%d" % i, [128, 2, 64], BF16) for i in range(2)]
    BIASC = k.sb("BIASC", [16, 8, 128], F32)
    BIASN = k.sb("BIASN", [16, 8, 4], F32)

    KC = PA16[:, 0:5120].rearrange("p (g t) -> p g t", g=8)
    VT = PA16[:, 5120:7680].rearrange("p (a c) -> p a c", a=5)
    QP = PA16[:, 7680:8704].rearrange("p (h t) -> p h t", h=2)
    QT = PA16[0:64, 0:2048]
    QC = PA16[0:64, 2048:4096].rearrange("p (s h t) -> p s h t", s=16, h=32)
    VNTb = PA16[0:64, 4096:4608].rearrange("p (g d) -> p g d", g=8)
    KNC = PA16[0:64, 4608:5120].rearrange("p (g t) -> p g t", g=8)
    CKb = PA16[:, 5120:5632].rearrange("p (g d) -> p g d", g=8)
    CVb = PA16[:, 5632:6144].rearrange("p (g d) -> p g d", g=8)
    CKC = PA16[0:64, 6144:7168].rearrange("p (g t) -> p g t", g=8)
    PNs = PA16[0:16, 7168:8224].rearrange("p (g t) -> p g t", g=8)
    PNF = PA16[0:16, 8224:8736].rearrange("p (g t) -> p g t", g=8)
    PNTc = PA16[:, 8736:8864].rearrange("p (g t) -> p g t", g=8)
    PNTn = PA16[0:64, 8864:8992].rearrange("p (g t) -> p g t", g=8)
    SGBS = PA16[:, 8992:10016].rearrange("p (a t) -> p a t", a=16)
    HA = H[:, 1:4, :].rearrange("p a b -> p (a b)")
    KNT = HA[0:64, 0:512]
    VNT = HA[0:64, 512:1024]
    CKf = HA[:, 1024:1536]
    CVf = HA[:, 1536:2048]
    TSs = HA[0:16, 2048:3104].rearrange("p (g t) -> p g t", g=8)
    YBS = HA[:, 3104:4128].rearrange("p (a t) -> p a t", a=16)
    SHS = HA[:, 4128:4960].rearrange("p (b s) -> p b s", b=52)
    SSR = HA[0:16, 4960:5984]

    psA = [k.ps("psA%d" % i, [128, 512]) for i in range(2)]
    psTb = k.ps("psTb", [128, 8, 128], BF16)
    psTf = k.ps("psTf", [128, 512])
    psB4 = k.ps("psB4", [128, 512])
    psB5 = k.ps("psB5", [128, 512])
    psB6 = k.ps("psB6", [128, 512])
    psB7 = k.ps("psB7", [128, 512])

    S = k.S
    mmctr = [0]

    k.dma("sp", CP[:], cpd, [], ["CP"])
    k.dma("sp", SINKB[:], sinkbd, [], ["SINKB"])
    k.dma("sp", SINKT[:], sinktd, [], ["SINKT"])
    k.dma("sp", IDF[:], identd, [], ["IDF"])
    k.dma("sp", BONESF[:], bonesd, [], ["BONESF"])
    k.dma("sp", M64[:], mask64d, [], ["M64"])
    k.dma("sp", M4[:], mask4d, [], ["M4"])
    k.dma("sp", DM[:], dmd, [], ["DM"])
    k.dma("sp", DM0[:], dm0d, [], ["DM0"])
    k.dma("sp", BIASC[:], biascd, [], ["BIASC"])
    k.dma("sp", BIASN[:], biasnd, [], ["BIASN"])
    k.cp(IDB[:], IDF[:], ["IDF"], ["IDB"])
    k.cp(BONES[:], BONESF[:], ["BONESF"], ["BONES"])
    k.ts(OMU[:], CP[:, 0:52], -1.0, 1.0, ALU.mult, ALU.add, ["CP"], ["OMU"])
    k.memset(ONES[:], 1.0, ["ONES"])
    k.memset(EPSC[:, 0:1], RMS_EPS, ["EPSC"])
    k.memset(EPSC[:, 1:2], GN_EPS, ["EPSC"])
    k.memset(SST[:], 0.0, ["SST"])
    k.memset(SBF[:], 0.0, ["SBF"])
    k.memset(SHIFT[:], 0.0, ["SHIFT"])
    k.memset(BASE[:], 0.0, ["BASE"])
    k.memset(PA16[:], 0.0, ["KC", "VT", "QP"])
    k.memset(SWI[:], 0.0, ["SWI"])
    k.memset(SOB[:], 0.0, ["SOB"])
    k.memset(DUM[:], 0.0, ["DUM"])
    k.memset(AR[:], 0.0, ["AR"])
    k.memset(KTP[:], 0.0, ["KTP"])
    k.memset(BTP[:], 0.0, ["BTP"])
    for i_ in range(2):
        k.memset(SSB[i_][:], 0.0, ["SS%db" % i_])

    MU = CP[:, 0:52]
    W0C = CP[:, 52:68]
    A0C = CP[:, 68:84]
    KKC = CP[:, 84:100]
    KAC = CP[:, 100:116]
    RKC = CP[:, 116:132]
    LNW = CP[:, 132:148]
    LNB = CP[:, 148:164]

    wslot = [0]

    def load_w_in(segs):
        s = wslot[0] % 6
        wslot[0] += 1
        key = "wa%d" % s
        v = WA[:, s * 2048:(s + 1) * 2048].rearrange("p (a b) -> p a b", a=16)
        for (c0, n, off) in segs:
            k.dma("pool", v[:, :, off:off + n], w_in[:, c0:c0 + n].rearrange("(kc p) c -> p kc c", p=128),
                  [], [key])
        return v, key

    def cm_matmul(v, key, ncols, T, t0=0):
        b = mmctr[0] % 2
        mmctr[0] += 1
        pk = "psA%d" % b
        for kc in range(16):
            k.mm(psA[b][0:ncols, 0:T], v[:, kc, 0:ncols], XN[:, kc, t0:t0 + T], kc == 0, kc == 15,
                 [key, "XN"], [pk])
        return psA[b], pk

    def run_pass(kind, xsrc, T, nseg, L, C, pidx, last_pre=False, first_main=False, last_main=False):
        full = kind != "pre"
        smp = kind == "smp"
        NCH = T // C
        ntile = max(1, T // 128)
        rows = min(128, T)
        MK = M64 if C == 64 else M4
        nst = int(round(math.log2(C)))

        def seg3(ap2):
            return ap2.rearrange("p (s l) -> p s l", s=nseg)

        def ch3(ap2):
            return ap2.rearrange("p (n c) -> p n c", n=NCH)

        k.dma("sp", NW[:], nmix.partition_broadcast(128), [], ["NW"])
        for ti in range(ntile):
            hk = "H%d" % ti
            k.dma("sp", H[0:rows, ti, :], xsrc[ti * 128:ti * 128 + rows, :], [], [hk])
            k.act(XNT[0:rows, :], H[0:rows, ti, :], AF.Square, [hk], ["XNT", "ST0"], accum=ST[0:rows, 0:1])
            k.act(ST[0:rows, 1:2], ST[0:rows, 0:1], AF.Sqrt, ["ST0", "EPSC"], ["ST1"], bias=EPSC[0:rows, 0:1],
                  scale=1.0 / D)
            k.recip(ST[0:rows, 1:2], ST[0:rows, 1:2], ["ST1"], ["ST1"])
            k.stt(XNT[0:rows, :], H[0:rows, ti, :], ST[0:rows, 1:2], NW[0:rows, :], ALU.mult, ALU.mult,
                  [hk, "ST1", "NW"], ["XNT"])
            for half in range(2):
                for j in range(8):
                    kc = half * 8 + j
                    k.tr(psTb[:, j, 0:rows], XNT[0:rows, kc * 128:(kc + 1) * 128], IDB[0:rows, 0:rows],
                         ["XNT", "IDB"], ["psTb"])
                k.cp(XN[:, half * 8:(half + 1) * 8, ti * 128:ti * 128 + rows], psTb[:, :, 0:rows],
                     ["psTb"], ["XN"], eng=("act" if half else "dve"))

        if stop_phase <= 1:
            return
        if smp:
            for c8 in range(6):
                k.dma("sp", SSR[:, :], sshift[:, c8 * 1024:(c8 + 1) * 1024], [], ["SSR"])
                for j in range(8):
                    k.tr(psTf[:, j * 16:(j + 1) * 16], SSR[:, j * 128:(j + 1) * 128], IDF[0:16, 0:16],
                         ["SSR", "IDF"], ["psTf"])
                k.cp(SHS[:, c8 * 8:(c8 + 1) * 8, :], psTf[:, 0:128].rearrange("p (b s) -> p b s", b=8),
                     ["psTf"], ["SHS"])
            k.dma("sp", SSR[:, 0:448], sshift[:, 6144:6592], [], ["SSR"])
            for j, (o, n) in enumerate(((0, 96), (96, 96), (192, 128), (320, 128))):
                k.tr(psTf[0:n, j * 16:(j + 1) * 16], SSR[:, o:o + n], IDF[0:16, 0:16], ["SSR", "IDF"], ["psTf"])
            k.memset(SHS[:, 48:50, :], 0.0, ["SHS"])
            k.cp(SHS[0:96, 48:50, :], psTf[0:96, 0:32].rearrange("p (b s) -> p b s", b=2), ["psTf"], ["SHS"])
            k.cp(SHS[:, 50:52, :], psTf[:, 32:64].rearrange("p (b s) -> p b s", b=2), ["psTf"], ["SHS"])

        def shift_block(ps, pk, blk, nr, out2, okeys):
            PX = PEXT[0:nr, 0:nseg * (L + 1)].rearrange("p (s l) -> p s l", s=nseg)
            if smp:
                prev = SHS[0:nr, blk, :]
                pkey = "SHS"
            else:
                prev = SHIFT[0:nr, blk:blk + 1]
                pkey = "SHIFT"
            k.cp(PX[:, :, 0], prev, [pkey], ["PEXT"])
            k.act(PX[:, :, 1:L + 1], seg3(ps[0:nr, 0:T]), AF.Copy, [pk], ["PEXT"])
            k.cp(prev, PX[:, :, L], ["PEXT"], [pkey])
            k.ts(seg3(TMP2[0:nr, 0:T]), PX[:, :, 0:L], MU[0:nr, blk:blk + 1], None, ALU.mult, None,
                 ["PEXT", "CP"], ["TMP2"])
            k.stt(seg3(out2), PX[:, :, 1:L + 1], OMU[0:nr, blk:blk + 1], seg3(TMP2[0:nr, 0:T]),
                  ALU.mult, ALU.add, ["PEXT", "OMU", "TMP2"], okeys)

        v, key = load_w_in([(6144, 96, 0)])
        ps, pk = cm_matmul(v, key, 96, T)
        shift_block(ps, pk, 48, 96, TMP1[0:96, 0:T], ["TMP1"])
        k.act(TW[0:96, 0:T], TMP1[0:96, 0:T], AF.Tanh, ["TMP1"], ["TW"])
        v, key = load_w_in([(6240, 96, 0)])
        ps, pk = cm_matmul(v, key, 96, T)
        shift_block(ps, pk, 49, 96, DAb[0:96, 0:T], ["DAb"])
        if full or last_pre:
            for j in range(2):
                v, key = load_w_in([(6336 + j * 128, 128, 0)])
                ps, pk = cm_matmul(v, key, 128, T)
                shift_block(ps, pk, 50 + j, 128, TMP1[:, 0:T], ["TMP1"])
                k.act(SG[:, j, 0:T], TMP1[:, 0:T], AF.Sigmoid, ["TMP1"], ["SG"])

        if stop_phase <= 2:
            return
        if kind == "main":
            k.cp(KC[:, :, 0:128], KC[:, :, 512:640], ["KC"], ["KC"])
            k.cp(VT[:, 0, :], VT[:, 4, :], ["VT"], ["VT"])
        if kind == "main" or last_pre:
            tl = T - 128
            for g in range(8):
                v, key = load_w_in([(KA0 + g * 64, 64, 0), (KA0 + g * 64, 64, 64)])
                if kind == "main":
                    ps, pk = cm_matmul(v, key, 128, T)
                    k.cp(KC[:, g, 128:640], ps[:, 0:T], [pk], ["KC"], eng="act")
                else:
                    ps, pk = cm_matmul(v, key, 128, 128, t0=tl)
                    k.cp(KC[:, g, 512:640], ps[:, 0:128], [pk], ["KC"], eng="act")
            for j in range(4):
                v, key = load_w_in([(VA0 + j * 128, 128, 0)])
                for ti in (range(ntile) if kind == "main" else [ntile - 1]):
                    b = mmctr[0] % 2
                    mmctr[0] += 1
                    pk = "psA%d" % b
                    for kc in range(16):
                        k.mm(psA[b][:, 0:128], XN[:, kc, ti * 128:(ti + 1) * 128], v[:, kc, :], kc == 0, kc == 15,
                             [key, "XN"], [pk])
                    k.cp(VT[:, 1 + ti, j * 128:(j + 1) * 128], psA[b][:, 0:128], [pk], ["VT"], eng="act")
                    if last_main and ti == ntile - 1:
                        k.cp(VWO[:, j * 128:(j + 1) * 128], psA[b][:, 0:128], [pk], ["E1"])
            if last_main and not (_DBG & 4):
                for j in range(4):
                    v, key = load_w_in([(KA0 + j * 128, 128, 0)])
                    b = mmctr[0] % 2
                    mmctr[0] += 1
                    pk = "psA%d" % b
                    for kc in range(16):
                        k.mm(psA[b][:, 0:128], XN[:, kc, tl:tl + 128], v[:, kc, :], kc == 0, kc == 15,
                             [key, "XN"], [pk])
                    k.cp(KWO[:, j * 128:(j + 1) * 128], psA[b][:, 0:128], [pk], ["CUM"])
                k.dma("sp", kwin_p, KWO[:], ["CUM"], ["o_kwin"], is_out=True)
                k.dma("sp", vwin_p, VWO[:], ["E1"], ["o_vwin"], is_out=True)
        if smp:
            for j in range(8):
                v, key = load_w_in([((KA0 if j < 4 else VA0) + (j % 4) * 128, 128, 0)])
                b = mmctr[0] % 2
                mmctr[0] += 1
                pk = "psA%d" % b
                for kc in range(16):
                    k.mm(psA[b][0:64, 0:128], XN[:, kc, 0:64], v[:, kc, :], kc == 0, kc == 15, [key, "XN"], [pk])
                if j < 4:
                    k.cp(KNT[:, j * 128:(j + 1) * 128], psA[b][0:64, 0:128], [pk], ["KNT"])
                else:
                    jj = j - 4
                    k.cp(VNT[:, jj * 128:(jj + 1) * 128], psA[b][0:64, 0:128], [pk], ["VNT"])
                    k.cp(VNTb[:, 2 * jj:2 * jj + 2, :], psA[b][0:64, 0:128].rearrange("p (g d) -> p g d", g=2),
                         [pk], ["VNTb"], eng="act")
            k.cp(QT[:, 0:512], KNT[:, :], ["KNT"], ["QT"])
            for g in range(8):
                k.tr(psTb[0:64, g, 0:64], QT[:, g * 64:(g + 1) * 64], IDB[0:64, 0:64], ["QT", "IDB"], ["psTb"])
            k.cp(KNC[:, :, :], psTb[0:64, :, 0:64], ["psTb"], ["KNC"])

        def wkv_pair(p, get_state, put_state):
            ARv = AR[:, 0:2 * T].rearrange("p (n a c) -> p n a c", n=NCH, a=2)
            ABR = psB4[0:C, 0:4 * 2 * C].rearrange("p (h c) -> p h c", h=4)
            AKR = psB5[0:C, 0:4 * 2 * C].rearrange("p (h c) -> p h c", h=4)
            NTp = psB6[0:C, 0:4 * C].rearrange("p (h c) -> p h c", h=4)
            IBp = psB6[0:C, 256:256 + 4 * C].rearrange("p (h c) -> p h c", h=4)
            mask2 = MK[0:C, 0:2, :].rearrange("p a c -> p (a c)")
            for gi in range(NCH // 2):
                sl = gi % 2
                gk = "g%d" % sl
                TMg = TM3[sl][0:C, :, :, :]
                tmk = "TM3_%d" % sl
                for nl in range(2):
                    n = gi * 2 + nl
                    for qi, (SRC, sk) in enumerate(((VB, "VB"), (KT, "KT"), (BT, "BT"))):
                        k.tr(psTb[0:C, nl * 3 + qi, :], SRC[:, n * C:(n + 1) * C], IDB[:, :], [sk, "IDB"], ["psTb"])
                k.cp(TMg, psTb[0:C, 0:6, :].rearrange("p (n a) c -> p n a c", n=2), ["psTb"], [tmk], eng="act")
                if stop_phase <= 3.42:
                    continue
                for nl in range(2):
                    n = gi * 2 + nl
                    for hh in range(2):
                        hc = nl * 2 + hh
                        hs = slice(hh * 64, hh * 64 + 64)
                        k.mm(ABR[:, hc, :], BTP[:, hh, n * C:(n + 1) * C], AR[:, n * 2 * C:(n + 1) * 2 * C], True, True,
                             ["BTP", "AR"], ["psB4"])
                        k.mm(AKR[:, hc, :], KTP[:, hh, n * C:(n + 1) * C], AR[:, n * 2 * C:(n + 1) * 2 * C], True, True,
                             ["KTP", "AR"], ["psB5"])
                if stop_phase <= 3.45:
                    continue
                abrm = ABRM[sl][0:C, :, 0:2 * C]
                akrm = AKRM[sl][0:C, :, 0:2 * C]
                k.tt(abrm, ABR, bc(mask2.unsqueeze(1), [C, 4, 2 * C]), ALU.mult, ["psB4", "MK"], [gk + "abr"])
                k.tt(akrm, AKR, bc(mask2.unsqueeze(1), [C, 4, 2 * C]), ALU.mult, ["psB5", "MK"], [gk + "akr"])
                if stop_phase <= 3.5:
                    continue
                cur = 0
                ptc = PTC[sl][cur][0:C, :, :, 0:C]
                pt = PT[sl][cur][0:C, :, 0:C]
                for hc in range(4):
                    k.tr(psTb[0:C, hc, 0:C], ABRM[sl][0:C, hc, 0:C], IDB[0:C, 0:C], [gk + "abr", "IDB"], ["psTb"])
                k.cp(pt, psTb[0:C, 0:4, 0:C], ["psTb"], [gk + "pt0"])
                k.cp(ptc[:, :, 0, :], ABRM[sl][0:C, :, 0:C], [gk + "abr"], [gk + "ptc0"], eng="act")
                k.cp(ptc[:, :, 1, :], bc(IDB[0:C, 0:C].unsqueeze(1), [C, 4, C]), ["IDB"], [gk + "ptc0"])
                IA = psB4[0:C, 0:4 * 2 * C].rearrange("p (h c) -> p h c", h=4)
                for s in range(nst):
                    last = s == nst - 1
                    ptc = PTC[sl][cur][0:C, :, :, 0:C]
                    pt = PT[sl][cur][0:C, :, 0:C]
                    nx = 1 - cur
                    ptcn = PTC[sl][nx][0:C, :, :, 0:C]
                    ptn = PT[sl][nx][0:C, :, 0:C]
                    ck_ = gk + "ptc%d" % cur
                    pk_ = gk + "pt%d" % cur
                    for hc in range(4):
                        if last:
                            k.mm(IA[:, hc, 0:C], pt[:, hc, :], ptc[:, hc, 1, :], True, True, [ck_, pk_], ["psB4"])
                        else:
                            k.mm(IA[:, hc, :], pt[:, hc, :], ptc[:, hc, :, :], True, True, [ck_, pk_], ["psB4"])
                            k.mm(IBp[:, hc, :], ptc[:, hc, 0, :], pt[:, hc, :], True, True, [ck_, pk_], ["psB6"])
                    if last:
                        k.tt(ptcn[:, :, 1, :], ptc[:, :, 1, :], IA[:, :, 0:C], ALU.add, ["psB4", ck_],
                             [gk + "ptc%d" % nx])
                    else:
                        k.cp(ptcn[:, :, 0, :], IA[:, :, 0:C], ["psB4"], [gk + "ptc%d" % nx], eng="act")
                        k.tt(ptcn[:, :, 1, :], ptc[:, :, 1, :], IA[:, :, C:2 * C], ALU.add, ["psB4", ck_],
                             [gk + "ptc%d" % nx])
                        k.cp(ptn, IBp, ["psB6"], [gk + "pt%d" % nx], eng="act")
                    cur = nx
                if stop_phase <= 3.6:
                    continue
                tk = gk + "ptc%d" % cur
                Tfin = PTC[sl][cur][0:C, :, 1, 0:C]
                for nl in range(2):
                    n = gi * 2 + nl
                    sf, sbf, skey = get_state(n)
                    Zp = psB7[0:C, 0:128].rearrange("p (h i) -> p h i", h=2)
                    Up = psB7[0:C, 128:256].rearrange("p (h i) -> p h i", h=2)
                    for hh in range(2):
                        hc = nl * 2 + hh
                        hs = slice(hh * 64, hh * 64 + 64)
                        k.mm(Zp[:, hh, :], ARv[:, n, 0, :], sbf[:, hh, :], True, False, ["AR", skey + "b"], ["psB7"])
                        k.mm(Zp[:, hh, :], AKRM[sl][0:C, hc, 0:C], TMg[:, nl, 0, hh * 64:(hh + 1) * 64], False, True,
                             [gk + "akr", tmk], ["psB7"])
                    k.cp(ZB[0:C, :, :], Zp, ["psB7"], ["ZB"], eng="act")
                    for hh in range(2):
                        hc = nl * 2 + hh
                        k.mm(Up[:, hh, :], Tfin[:, hc, :], ZB[0:C, hh, :], True, True, [tk, "ZB"], ["psB7"])
                    k.cp(UB[0:C, :, :], Up, ["psB7"], ["UB"])
                    if full:
                        for hh in range(2):
                            hc = nl * 2 + hh
                            hs = slice(hh * 64, hh * 64 + 64)
                            o = psB7[hs, 256:256 + C]
                            k.mm(o, sbf[:, hh, :], ARv[:, n, 1, :], True, False, [skey + "b", "AR"], ["psB7"])
                            k.mm(o, UB[0:C, hh, :], ABRM[sl][0:C, hc, C:2 * C], False, False,
                                 ["UB", gk + "abr"], ["psB7"])
                            k.mm(o, TMg[:, nl, 0, hh * 64:(hh + 1) * 64], AKRM[sl][0:C, hc, C:2 * C], False, True,
                                 [tmk, gk + "akr"], ["psB7"])
                        k.cp(Yt[:, n * C:(n + 1) * C], psB7[:, 256:256 + C], ["psB7"], ["Yt"], eng="act")
                    for hh in range(2):
                        hs = slice(hh * 64, hh * 64 + 64)
                        o = psB7[hs, 384:448]
                        k.mm(o, TMg[:, nl, 2, hh * 64:(hh + 1) * 64], UB[0:C, hh, :], True, False,
                             [tmk, "UB"], ["psB7"])
                        k.mm(o, TMg[:, nl, 1, hh * 64:(hh + 1) * 64], TMg[:, nl, 0, hh * 64:(hh + 1) * 64],
                             False, True, [tmk], ["psB7"])
                    k.tt(sf, sf, psB7[:, 384:448], ALU.add, ["psB7", skey], [skey])
                    k.ts(sf, sf, WC[:, n:n + 1], None, ALU.mult, None, [skey, "WC"], [skey])
                    k.cp(sbf[0:64, 0, :], sf[0:64, :], [skey], [skey + "b"], eng="act")
                    k.cp(sbf[64:128, 1, :], sf[64:128, :], [skey], [skey + "b"], eng="act")
                    put_state(n, sf, skey)

        octr = [0]

        def state_out(sf, skey, dst):
            k.cp(SOB[0:64, 0:64], sf[0:64, :], [skey], ["SOB"])
            k.cp(SOB[64:128, 64:128], sf[64:128, :], [skey], ["SOB"])
            k.tr(psTf[:, 128:256], SOB[:, :], IDF[:, :], ["SOB", "IDF"], ["psTf"])
            so = SOUT[octr[0] % 2]
            sok = "SOUT%d" % (octr[0] % 2)
            octr[0] += 1
            k.cp(so[0:64, :], psTf[0:64, 128:192], ["psTf"], [sok])
            k.cp(so[64:128, :], psTf[64:128, 192:256], ["psTf"], [sok], eng="act")
            k.dma("sp", dst, so[:, :], [sok], ["o_wkv"], is_out=True)

        for p in range((16 if stop_phase >= 4 else 1) if stop_phase > 3 else 0):
            for (c0, blk, dst, dk) in ((p * 128, p, Rt, "Rt"), (2048 + p * 128, 16 + p, Kt, "Kt"),
                                       (4096 + p * 128, 32 + p, Vt, "Vt")):
                if kind == "pre" and dk == "Rt":
                    if not last_pre:
                        continue
                v, key = load_w_in([(c0, 128, 0)])
                ps, pk = cm_matmul(v, key, 128, T)
                shift_block(ps, pk, blk, 128, dst[:, 0:T], [dk])
            if stop_phase <= 3.1:
                continue
            lsl = p % 2
            lk = "LWT%d" % lsl
            k.dma("pool", LWT[0:96, lsl, 0, :], w_lora[:, p * 128:(p + 1) * 128], [], [lk])
            k.dma("pool", LWT[0:96, lsl, 1, :], a_lora[:, p * 128:(p + 1) * 128], [], [lk])
            if full:
                k.dma("pool", LWT[:, lsl, 2:4, :],
                      g_lora[:, p * 128:(p + 1) * 128].rearrange("(j q) c -> q j c", q=128), [], [lk])
            b = mmctr[0] % 2
            mmctr[0] += 1
            pk = "psA%d" % b
            k.mm(psA[b][:, 0:T], LWT[0:96, lsl, 0, :], TW[0:96, 0:T], True, True, [lk, "TW"], [pk])
            k.act(LW[:, 0:T], psA[b][:, 0:T], AF.Sigmoid, [pk, "CP"], ["LW"], bias=W0C[:, p:p + 1])
            k.ts(LW[:, 0:T], LW[:, 0:T], -math.exp(-0.5), None, ALU.mult, None, ["LW"], ["LW"])
            b = mmctr[0] % 2
            mmctr[0] += 1
            pk = "psA%d" % b
            k.mm(psA[b][:, 0:T], LWT[0:96, lsl, 1, :], DAb[0:96, 0:T], True, True, [lk, "DAb"], [pk])
            k.act(ASIG[:, 0:T], psA[b][:, 0:T], AF.Sigmoid, [pk, "CP"], ["ASIG"], bias=A0C[:, p:p + 1])
            if full:
                b = mmctr[0] % 2
                mmctr[0] += 1
                pk = "psA%d" % b
                for j in range(2):
                    k.mm(psA[b][:, 0:T], LWT[:, lsl, 2 + j, :], SG[:, j, 0:T], j == 0, j == 1,
                         [lk, "SG"], [pk])
                k.cp(Gb[:, 0:T], psA[b][:, 0:T], [pk], ["Gb"], eng="act")
            if stop_phase <= 3.2:
                continue
            k.ts(KK[:, 0:T], Kt[:, 0:T], KKC[:, p:p + 1], None, ALU.mult, None, ["Kt", "CP"], ["KK"])
            k.act(SQ[:, 0:T], KK[:, 0:T], AF.Square, ["KK"], ["SQ"])
            b = mmctr[0] % 2
            mmctr[0] += 1
            pk = "psA%d" % b
            k.mm(psA[b][:, 0:T], BONES[:, :], SQ[:, 0:T], True, True, ["BONES", "SQ"], [pk])
            k.act(TMP2[:, 0:T], psA[b][:, 0:T], AF.Sqrt, [pk], ["TMP2"])
            k.ts(TMP2[:, 0:T], TMP2[:, 0:T], 1e-12, None, ALU.max, None, ["TMP2"], ["TMP2"])
            k.recip(TMP2[:, 0:T], TMP2[:, 0:T], ["TMP2"], ["TMP2"])
            k.tt(KK[:, 0:T], KK[:, 0:T], TMP2[:, 0:T], ALU.mult, ["KK", "TMP2"], ["KK"])
            k.ts(TMP1[:, 0:T], ASIG[:, 0:T], KAC[:, p:p + 1], KAC[:, p:p + 1], ALU.mult, ALU.subtract, ["ASIG", "CP"], ["TMP1"])
            k.stt(Kt[:, 0:T], TMP1[:, 0:T], 1.0, Kt[:, 0:T], ALU.add, ALU.mult, ["TMP1", "Kt"], ["Kt"])
            k.tt(TMP1[:, 0:T], KK[:, 0:T], ASIG[:, 0:T], ALU.mult, ["KK", "ASIG"], ["TMP1"])
            if stop_phase <= 3.3:
                continue
            S.op("dve", lambda e: e.tensor_tensor_scan(out=CUM[:, 0:T], data0=ONES[:, 0:T], data1=LW[:, 0:T],
                                                       initial=0.0, op0=ALU.mult, op1=ALU.add),
                 ["ONES", "LW"], ["CUM"])
            if NCH > 1:
                k.cp(BASE[:, 1:NCH], ch3(CUM[:, 0:T])[:, 0:NCH - 1, C - 1], ["CUM"], ["BASE"])
            k.tt(ch3(CUM[:, 0:T]), ch3(CUM[:, 0:T]), bc(BASE[:, 0:NCH].unsqueeze(2), [128, NCH, C]), ALU.subtract,
                 ["CUM", "BASE"], ["CUM"])
            ARv = AR[:, 0:2 * T].rearrange("p (n a c) -> p n a c", n=NCH, a=2)
            k.act(E1[:, 0:T], CUM[:, 0:T], AF.Exp, ["CUM"], ["E1"])
            k.cp(WC[:, 0:NCH], ch3(E1[:, 0:T])[:, :, C - 1], ["E1"], ["WC"])
            if full:
                k.tt(ARv[:, :, 1, :], ch3(Rt[:, 0:T]), ch3(E1[:, 0:T]), ALU.mult, ["Rt", "E1"], ["AR"])
            k.tt(TMP2[:, 0:T], CUM[:, 0:T], LW[:, 0:T], ALU.subtract, ["CUM", "LW"], ["TMP2"])
            k.act(E1[:, 0:T], TMP2[:, 0:T], AF.Exp, ["TMP2"], ["E1"])
            k.stt(ARv[:, :, 0, :], ch3(KK[:, 0:T]), -1.0, ch3(E1[:, 0:T]), ALU.mult, ALU.mult, ["KK", "E1"], ["AR"])
            k.act(E1[:, 0:T], CUM[:, 0:T], AF.Exp, ["CUM"], ["E1"], scale=-1.0)
            k.tt(KT[:, 0:T], Kt[:, 0:T], E1[:, 0:T], ALU.mult, ["Kt", "E1"], ["KT"])
            k.tt(BT[:, 0:T], TMP1[:, 0:T], E1[:, 0:T], ALU.mult, ["TMP1", "E1"], ["BT"])
            for hh_ in range(2):
                hs_ = slice(hh_ * 64, hh_ * 64 + 64)
                k.cp(KTP[hs_, hh_, 0:T], KT[hs_, 0:T], ["KT"], ["KTP"], eng="act")
                k.cp(BTP[hs_, hh_, 0:T], BT[hs_, 0:T], ["BT"], ["BTP"])
            k.cp(VB[:, 0:T], Vt[:, 0:T], ["Vt"], ["VB"], eng="act")
            if stop_phase <= 3.4:
                continue
            if smp:
                def get_state(n, p=p):
                    i2 = n % 2
                    k.dma("sp", SWI[0:64, 0:64], swkv[n, p, 0:64, :], [], ["SWI"])
                    k.dma("sp", SWI[64:128, 64:128], swkv[n, p, 64:128, :], [], ["SWI"])
                    k.tr(psTf[:, 0:128], SWI[:, :], IDF[:, :], ["SWI", "IDF"], ["psTf"])
                    k.cp(SS[i2][0:64, :], psTf[0:64, 0:64], ["psTf"], ["SS%d" % i2])
                    k.cp(SS[i2][64:128, :], psTf[64:128, 64:128], ["psTf"], ["SS%d" % i2])
                    k.cp(SSB[i2][0:64, 0, :], SS[i2][0:64, :], ["SS%d" % i2], ["SS%db" % i2], eng="act")
                    k.cp(SSB[i2][64:128, 1, :], SS[i2][64:128, :], ["SS%d" % i2], ["SS%db" % i2], eng="act")
                    return SS[i2][:, :], SSB[i2][:, :, :], "SS%d" % i2

                def put_state(n, sf, skey, p=p):
                    state_out(sf, skey, wkv_s[n, p])
                wkv_pair(p, get_state, put_state)
            else:
                def get_state(n, p=p):
                    return SST[:, p, :], SBF[:, p, :, :], "SST%d" % p

                def put_state(n, sf, skey):
                    pass
                wkv_pair(p, get_state, put_state)
                if last_main and not (_DBG & 1):
                    state_out(SST[:, p, :], "SST%d" % p, wkv_p[p])
            if not full:
                continue
            k.cp(SQ[:, 0:T], Yt[:, 0:T], ["Yt"], ["SQ"], eng="act")
            b = mmctr[0] % 2
            mmctr[0] += 1
            pk = "psA%d" % b
            k.mm(psA[b][:, 0:T], BONES[:, :], SQ[:, 0:T], True, True, ["BONES", "SQ"], [pk])
            k.stt(Yt[:, 0:T], psA[b][:, 0:T], -1.0 / 64.0, Yt[:, 0:T], ALU.mult, ALU.add, [pk, "Yt"], ["Yt"])
            k.act(SQ[:, 0:T], Yt[:, 0:T], AF.Square, ["Yt"], ["SQ"])
            b = mmctr[0] % 2
            mmctr[0] += 1
            pk = "psA%d" % b
            k.mm(psA[b][:, 0:T], BONES[:, :], SQ[:, 0:T], True, True, ["BONES", "SQ"], [pk])
            k.act(TMP2[:, 0:T], psA[b][:, 0:T], AF.Sqrt, [pk, "EPSC"], ["TMP2"], bias=EPSC[:, 1:2], scale=1.0 / 64.0)
            k.recip(TMP2[:, 0:T], TMP2[:, 0:T], ["TMP2"], ["TMP2"])
            k.tt(Yt[:, 0:T], Yt[:, 0:T], TMP2[:, 0:T], ALU.mult, ["Yt", "TMP2"], ["Yt"])
            k.ts(Yt[:, 0:T], Yt[:, 0:T], LNW[:, p:p + 1], LNB[:, p:p + 1], ALU.mult, ALU.add, ["Yt", "CP"], ["Yt"])
            k.tt(TMP1[:, 0:T], Rt[:, 0:T], Kt[:, 0:T], ALU.mult, ["Rt", "Kt"], ["TMP1"])
            k.ts(SQ[:, 0:T], TMP1[:, 0:T], RKC[:, p:p + 1], None, ALU.mult, None, ["TMP1", "CP"], ["SQ"])
            b = mmctr[0] % 2
            mmctr[0] += 1
            pk = "psA%d" % b
            k.mm(psA[b][:, 0:T], BONES[:, :], SQ[:, 0:T], True, True, ["BONES", "SQ"], [pk])
            k.tt(TMP1[:, 0:T], psA[b][:, 0:T], Vt[:, 0:T], ALU.mult, [pk, "Vt"], ["TMP1"])
            k.tt(Yt[:, 0:T], Yt[:, 0:T], TMP1[:, 0:T], ALU.add, ["Yt", "TMP1"], ["Yt"])
            k.tt(Yt[:, 0:T], Yt[:, 0:T], Gb[:, 0:T], ALU.mult, ["Yt", "Gb"], ["Yt"])
            v, key = load_w_in([(GA0 + p * 128, 128, 0)])
            ps, pk = cm_matmul(v, key, 128, T)
            k.act(SGA[:, 0:T], ps[:, 0:T], AF.Sigmoid, [pk], ["SGA"])
            v, key = load_w_in([(GB0 + p * 128, 128, 0)])
            ps, pk = cm_matmul(v, key, 128, T)
            if smp:
                k.act(SGBS[:, p, :], ps[:, 0:T], AF.Sigmoid, [pk], ["SGBS"])
                k.tt(MIX[:, p, 0:T], Yt[:, 0:T], SGA[:, 0:T], ALU.mult, ["Yt", "SGA"], ["MIX"])
                v, key = load_w_in([(Q0 + p * 128, 128, 0)])
                b = mmctr[0] % 2
                mmctr[0] += 1
                pk = "psA%d" % b
                for kc in range(16):
                    k.mm(psA[b][0:64, 0:128], XN[:, kc, 0:64], v[:, kc, :], kc == 0, kc == 15, [key, "XN"], [pk])
                k.cp(QT[:, p * 128:(p + 1) * 128], psA[b][0:64, 0:128], [pk], ["QT"], eng="act")
                continue
            k.act(SGB[:, 0:T], ps[:, 0:T], AF.Sigmoid, [pk], ["SGB"])
            k.tt(MIXA[:, 0:T], Yt[:, 0:T], SGA[:, 0:T], ALU.mult, ["Yt", "SGA"], ["PEXT"])
            v, key = load_w_in([(Q0 + p * 128, 128, 0)])
            ps, pk = cm_matmul(v, key, 128, T)
            k.cp(QP[0:64, 0, 0:T], ps[0:64, 0:T], [pk], ["QP"], eng="act")
            k.cp(QP[64:128, 1, 0:T], ps[64:128, 0:T], [pk], ["QP"], eng="act")
            g = p // 2
            for hh in range(2):
                h = 2 * p + hh
                k.ts(BH[:, hh, :], DM[:, :], SLOPES[h], None, ALU.mult, None, ["DM"], ["BH"])
                if first_main:
                    k.ts(BH0[:, hh, :], DM0[:, :], SLOPES[h], None, ALU.mult, None, ["DM0"], ["BH0"])
            for n in range(T // 128):
                for hh in range(2):
                    h = 2 * p + hh
                    hs = slice(hh * 64, hh * 64 + 64)
                    bh = BH0 if (first_main and n == 0) else BH
                    bhk = "BH0" if (first_main and n == 0) else "BH"
                    k.mm(psB5[:, 0:256], QP[:, hh, n * 128:(n + 1) * 128], KC[:, g, n * 128:n * 128 + 256], True, True,
                         ["QP", "KC"], ["psB5"])
                    k.stt(TS[:, :], psB5[:, 0:256], 0.125, bh[:, hh, :], ALU.mult, ALU.add, ["psB5", bhk], ["TS"])
                    k.red(ST[:, 4:5], TS[:, :], ALU.max, ["TS"], ["ST4"])
                    k.tt(ST[:, 5:6], ST[:, 4:5], SINKB[:, h:h + 1], ALU.max, ["ST4", "SINKB"], ["ST5"])
                    k.ts(ST[:, 5:6], ST[:, 5:6], -1.0, None, ALU.mult, None, ["ST5"], ["ST5"])
                    k.act(PB[:, :], TS[:, :], AF.Exp, ["TS", "ST5"], ["PB", "ST6"], bias=ST[:, 5:6], accum=ST[:, 6:7])
                    k.act(ST[:, 7:8], ST[:, 5:6], AF.Exp, ["ST5", "SINKB"], ["ST7"], bias=SINKB[:, h:h + 1])
                    k.tt(ST[:, 8:9], ST[:, 6:7], ST[:, 7:8], ALU.add, ["ST6", "ST7"], ["ST8"])
                    k.recip(ST[:, 8:9], ST[:, 8:9], ["ST8"], ["ST8"])
                    k.ts(PN[:, :], PB[:, :], ST[:, 8:9], None, ALU.mult, None, ["PB", "ST8"], ["PN"])
                    k.tr(psTb[:, 0, :], PN[:, 0:128], IDB[:, :], ["PN", "IDB"], ["psTb"])
                    k.tr(psTb[:, 1, :], PN[:, 128:256], IDB[:, :], ["PN", "IDB"], ["psTb"])
                    k.cp(PNT[:, :].rearrange("p (a c) -> p a c", a=2), psTb[:, 0:2, :], ["psTb"], ["PNT"], eng="act")
                    o = psB7[hs, 256:384]
                    k.mm(o, VT[:, n, g * 64:(g + 1) * 64], PNT[:, 0:128], True, False, ["VT", "PNT"], ["psB7"])
                    k.mm(o, VT[:, n + 1, g * 64:(g + 1) * 64], PNT[:, 128:256], False, True, ["VT", "PNT"], ["psB7"])
                k.tt(TMP1[:, 0:128], psB7[:, 256:384], SGB[:, n * 128:(n + 1) * 128], ALU.mult, ["psB7", "SGB"],
                     ["TMP1"])
                k.tt(MIX[:, p, n * 128:(n + 1) * 128], TMP1[:, 0:128], MIXA[:, n * 128:(n + 1) * 128], ALU.add,
                     ["TMP1", "PEXT"], ["MIX"])

        if not full:
            return

        if smp:
            for h in range(32):
                k.tr(psTb[0:64, h % 8, 0:64], QT[:, h * 64:(h + 1) * 64], IDB[0:64, 0:64], ["QT", "IDB"], ["psTb"])
                if h % 8 == 7:
                    k.cp(QC[:, :, h - 7:h + 1, :], psTb[0:64, :, 0:64].rearrange("p h (s t) -> p s h t", s=16),
                         ["psTb"], ["QC"], eng=("act" if h % 16 == 7 else "dve"))
            k.memset(PNF[:, :, :], 0.0, ["PNF"])
            SCa = psB4[0:16, :].rearrange("p (g t) -> p g t", g=4)
            SCb = psB5[0:16, :].rearrange("p (g t) -> p g t", g=4)
            SCn = psB6[0:16, 0:32].rearrange("p (g t) -> p g t", g=8)
            for s in range(16):
                k.dma("sp", CKf[:, :], ckd[s], [], ["CKf"])
                k.dma("sp", CVf[:, :], cvd[s], [], ["CVf"])
                k.cp(CKb[:, :, :], CKf[:, :].rearrange("p (g d) -> p g d", g=8), ["CKf"], ["CKb"])
                k.cp(CVb[:, :, :], CVf[:, :].rearrange("p (g d) -> p g d", g=8), ["CVf"], ["CVb"], eng="act")
                k.dma("sp", kwin_s[s, 0:124, :], ckd[s, 4:128, :], [], ["o_kws"], is_out=True)
                k.dma("sp", vwin_s[s, 0:124, :], cvd[s, 4:128, :], [], ["o_vws"], is_out=True)
                for g in range(8):
                    k.tr(psTb[0:64, g, :], CKb[:, g, :], IDB[:, :], ["CKb", "IDB"], ["psTb"])
                k.cp(CKC[:, :, :], psTb[0:64, :, :], ["psTb"], ["CKC"], eng="act")
                for g in range(8):
                    sc = (SCa if g < 4 else SCb)
                    sk_ = "psB4" if g < 4 else "psB5"
                    lq = QC[:, s, 4 * g:4 * g + 4, :].rearrange("p h t -> p (h t)")
                    k.mm(sc[:, g % 4, :], lq, CKC[:, g, :], True, True, ["QC", "CKC"], [sk_])
                    k.mm(SCn[:, g, :], lq, KNC[:, g, 4 * s:4 * s + 4], True, True, ["QC", "KNC"], ["psB6"])
                k.stt(TSs[:, 0:4, 0:128], SCa, 0.125, BIASC[:, 0:4, :], ALU.mult, ALU.add, ["psB4", "BIASC"], ["TSs"])
                k.stt(TSs[:, 4:8, 0:128], SCb, 0.125, BIASC[:, 4:8, :], ALU.mult, ALU.add, ["psB5", "BIASC"], ["TSs"])
                k.stt(TSs[:, :, 128:132], SCn, 0.125, BIASN[:, :, :], ALU.mult, ALU.add, ["psB6", "BIASN"], ["TSs"])
                k.red(ST[0:16, 0:8], TSs[:, :, :], ALU.max, ["TSs"], ["STs0"])
                k.tt(ST[0:16, 0:8], ST[0:16, 0:8], SINKT[:, :], ALU.max, ["STs0", "SINKT"], ["STs0"])
                k.tt(TSs[:, :, :], TSs[:, :, :], bc(ST[0:16, 0:8].unsqueeze(2), [16, 8, 132]), ALU.subtract,
                     ["TSs", "STs0"], ["TSs"])
                k.act(TSs[:, :, :], TSs[:, :, :], AF.Exp, ["TSs"], ["TSs"])
                k.red(ST[0:16, 8:16], TSs[:, :, :], ALU.add, ["TSs"], ["STs1"])
                k.tt(ST[0:16, 0:8], SINKT[:, :], ST[0:16, 0:8], ALU.subtract,
                     ["STs0", "SINKT"], ["STs0"])
                k.act(ST[0:16, 0:8], ST[0:16, 0:8], AF.Exp, ["STs0"], ["STs0"])
                k.tt(ST[0:16, 8:16], ST[0:16, 8:16], ST[0:16, 0:8], ALU.add, ["STs0", "STs1"], ["STs1"])
                k.recip(ST[0:16, 8:16], ST[0:16, 8:16], ["STs1"], ["STs1"])
                k.tt(PNs[:, :, :], TSs[:, :, :], bc(ST[0:16, 8:16].unsqueeze(2), [16, 8, 132]), ALU.mult,
                     ["TSs", "STs1"], ["PNs"])
                k.cp(PNF[:, :, 4 * s:4 * s + 4], PNs[:, :, 128:132], ["PNs"], ["PNF"])
                for g in range(8):
                    k.tr(psTb[:, 0, g * 16:(g + 1) * 16], PNs[:, g, 0:128], IDB[0:16, 0:16], ["PNs", "IDB"], ["psTb"])
                    k.tr(psTb[0:64, 1, g * 16:(g + 1) * 16], PNF[:, g, :], IDB[0:16, 0:16], ["PNF", "IDB"], ["psTb"])
                k.cp(PNTc[:, :, :], psTb[:, 0, :].rearrange("p (g t) -> p g t", g=8), ["psTb"], ["PNTc"])
                k.cp(PNTn[:, :, :], psTb[0:64, 1, :].rearrange("p (g t) -> p g t", g=8), ["psTb"], ["PNTn"], eng="act")
                k.memset(PNF[:, :, 4 * s:4 * s + 4], 0.0, ["PNF"])
                Op = psB7[:, 256:320].rearrange("p (a t) -> p a t", a=16)
                for pp in range(16):
                    g = pp // 2
                    for hh in range(2):
                        hl = (pp % 2) * 2 + hh
                        hs = slice(hh * 64, hh * 64 + 64)
                        k.mm(Op[hs, pp, :], CVb[:, g, :], PNTc[:, g, hl * 4:(hl + 1) * 4], True, False,
                             ["CVb", "PNTc"], ["psB7"])
                        k.mm(Op[hs, pp, :], VNTb[:, g, :], PNTn[:, g, hl * 4:(hl + 1) * 4], False, True,
                             ["VNTb", "PNTn"], ["psB7"])
                k.cp(YBS[:, :, 4 * s:4 * s + 4], Op, ["psB7"], ["YBS"])
                k.dma("sp", kwin_s[s, 124:128, :], KNT[4 * s:4 * s + 4, :], ["KNT"], ["o_kws"], is_out=True)
                k.dma("sp", vwin_s[s, 124:128, :], VNT[4 * s:4 * s + 4, :], ["VNT"], ["o_vws"], is_out=True)
            k.tt(YBS[:, :, :], YBS[:, :, :], SGBS[:, :, :], ALU.mult, ["YBS", "SGBS"], ["YBS"])
            k.tt(MIX[:, :, 0:64], MIX[:, :, 0:64], YBS[:, :, :], ALU.add, ["MIX", "YBS"], ["MIX"])
            for c8 in range(6):
                for j in range(8):
                    k.tr(psTf[0:16, (j % 4) * 128:(j % 4 + 1) * 128], SHS[:, c8 * 8 + j, :], IDF[:, :],
                         ["SHS", "IDF"], ["psTf"])
                    if j % 4 == 3:
                        k.cp(SSR[:, (j - 3) * 128:(j + 1) * 128], psTf[0:16, 0:512], ["psTf"], ["SSR"])
                k.dma("sp", shift_s[:, c8 * 1024:(c8 + 1) * 1024], SSR[:, :], ["SSR"], ["o_shs"], is_out=True)
            for j, (o, n, blk) in enumerate(((0, 96, 48), (96, 96, 49), (192, 128, 50), (320, 128, 51))):
                k.tr(psTf[0:16, j * 128:j * 128 + n], SHS[0:n, blk, :], IDF[0:n, 0:n], ["SHS", "IDF"], ["psTf"])
                k.cp(SSR[:, o:o + n], psTf[0:16, j * 128:j * 128 + n], ["psTf"], ["SSR"])
            k.dma("sp", shift_s[:, 6144:6592], SSR[:, 0:448], ["SSR"], ["o_shs"], is_out=True)

        if last_main and not (_DBG & 2):
            for blk in range(52):
                if blk < 48:
                    c0, nr = blk * 128, 128
                else:
                    c0, nr = ((6144, 96), (6240, 96), (6336, 128), (6464, 128))[blk - 48]
                k.dma("sp", shift_p[c0:c0 + nr, :], SHIFT[0:nr, blk:blk + 1], ["SHIFT"], ["o_shp"], is_out=True)

        wctr = [0]

        def load_w3(src3, nslots_key):
            s = wctr[0] % 3
            wctr[0] += 1
            keys = ["wa%d" % (2 * s), "wa%d" % (2 * s + 1)]
            return s, keys

        for cc in range(8):
            s, keys = load_w3(None, None)
            v = WA[:, s * 4096:(s + 1) * 4096].rearrange("p (a b) -> p a b", a=16)
            k.dma("pool", v, w_out[:, cc * 256:(cc + 1) * 256].rearrange("(kc p) c -> p kc c", p=128), [], keys)
            for ti in range(ntile):
                b = mmctr[0] % 2
                mmctr[0] += 1
                pk = "psA%d" % b
                for kc in range(16):
                    k.mm(psA[b][0:rows, 0:256], MIX[:, kc, ti * 128:ti * 128 + rows], v[:, kc, :], kc == 0, kc == 15,
                         keys + ["MIX"], [pk])
                hk = "H%d" % ti
                k.tt(H[0:rows, ti, cc * 256:(cc + 1) * 256], H[0:rows, ti, cc * 256:(cc + 1) * 256],
                     psA[b][0:rows, 0:256], ALU.add, [pk, hk], [hk])
        k.dma("sp", NW[:], nmlp.partition_broadcast(128), [], ["NW"])
        for ti in range(ntile):
            hk = "H%d" % ti
            k.act(XNT[0:rows, :], H[0:rows, ti, :], AF.Square, [hk], ["XNT", "ST0"], accum=ST[0:rows, 0:1])
            k.act(ST[0:rows, 1:2], ST[0:rows, 0:1], AF.Sqrt, ["ST0", "EPSC"], ["ST1"], bias=EPSC[0:rows, 0:1],
                  scale=1.0 / D)
            k.recip(ST[0:rows, 1:2], ST[0:rows, 1:2], ["ST1"], ["ST1"])
            k.stt(XNT[0:rows, :], H[0:rows, ti, :], ST[0:rows, 1:2], NW[0:rows, :], ALU.mult, ALU.mult,
                  [hk, "ST1", "NW"], ["XNT"])
            for half in range(2):
                for j in range(8):
                    kc = half * 8 + j
                    k.tr(psTb[:, j, 0:rows], XNT[0:rows, kc * 128:(kc + 1) * 128], IDB[0:rows, 0:rows],
                         ["XNT", "IDB"], ["psTb"])
                k.cp(XN[:, half * 8:(half + 1) * 8, ti * 128:ti * 128 + rows], psTb[:, :, 0:rows],
                     ["psTb"], ["XN"], eng=("act" if half else "dve"))
        UT = [Rt, Kt]
        UTb = [(KT, "KT"), (BT, "BT"), (VB, "VB"), (SQ, "SQ")]
        for sb_ in range(32):
            s, ukeys = load_w3(None, None)
            wu = WA[:, s * 4096:(s + 1) * 4096].rearrange("p (a b) -> p a b", a=16)
            k.dma("pool", wu, w_up[:, sb_ * 256:(sb_ + 1) * 256].rearrange("(kc p) c -> p kc c", p=128), [], ukeys)
            s2, dkeys = load_w3(None, None)
            wd = WA[:, s2 * 4096:(s2 + 1) * 4096].rearrange("p (a b) -> p a b", a=2)
            k.dma("pool", wd, w_down[sb_ * 256:(sb_ + 1) * 256, :].rearrange("(fb p) c -> p fb c", p=128), [], dkeys)
            uts = []
            for fb in range(2):
                b = mmctr[0] % 2
                mmctr[0] += 1
                pk = "psA%d" % b
                for kc in range(16):
                    k.mm(psA[b][:, 0:T], wu[:, kc, fb * 128:(fb + 1) * 128], XN[:, kc, 0:T], kc == 0, kc == 15,
                         ukeys + ["XN"], [pk])
                ut, utk = UTb[(sb_ % 2) * 2 + fb]
                k.act(TMP1[:, 0:T], psA[b][:, 0:T], AF.Relu, [pk], ["TMP1"])
                k.tt(ut[:, 0:T], TMP1[:, 0:T], TMP1[:, 0:T], ALU.mult, ["TMP1"], [utk])
                uts.append((ut, utk))
            for ti in range(ntile):
                hk = "H%d" % ti
                for cc in range(4):
                    b = mmctr[0] % 2
                    mmctr[0] += 1
                    pk = "psA%d" % b
                    for fb in range(2):
                        ut, utk = uts[fb]
                        k.mm(psA[b][0:rows, 0:512], ut[:, ti * 128:ti * 128 + rows], wd[:, fb, cc * 512:(cc + 1) * 512],
                             fb == 0, fb == 1, dkeys + [utk], [pk])
                    k.tt(H[0:rows, ti, cc * 512:(cc + 1) * 512], H[0:rows, ti, cc * 512:(cc + 1) * 512],
                         psA[b][0:rows, 0:512], ALU.add, [pk, hk], [hk])
        k.dma("sp", NW[:], nfin.partition_broadcast(128), [], ["NW"])
        ydst = y_smp if smp else y_main
        for ti in range(ntile):
            hk = "H%d" % ti
            k.act(XNT[0:rows, :], H[0:rows, ti, :], AF.Square, [hk], ["XNT", "ST0"], accum=ST[0:rows, 0:1])
            k.act(ST[0:rows, 1:2], ST[0:rows, 0:1], AF.Sqrt, ["ST0", "EPSC"], ["ST1"], bias=EPSC[0:rows, 0:1],
                  scale=1.0 / D)
            k.recip(ST[0:rows, 1:2], ST[0:rows, 1:2], ["ST1"], ["ST1"])
            k.stt(H[0:rows, ti, :], H[0:rows, ti, :], ST[0:rows, 1:2], NW[0:rows, :], ALU.mult, ALU.mult,
                  [hk, "ST1", "NW"], [hk])
            r0 = (0 if smp else pidx * 512) + ti * 128
            k.dma("sp", ydst[r0:r0 + rows, :], H[0:rows, ti, :], [hk], ["o_y"], is_out=True)

    def fz(e):
        return e.memset(DUM[:, 0:1], 0.0)

    if passes is not None:
        for i_, nm in enumerate(passes.split(",")):
            if i_ > 0:
                S.fence(fz)
            if nm == "p0":
                run_pass("pre", xpre[0:512, :], 512, 1, 512, 64, 0)
            elif nm == "p1":
                run_pass("pre", xpre[512:1024, :], 512, 1, 512, 64, 1, last_pre=True)
            elif nm == "m0":
                run_pass("main", xmain[0:512, :], 512, 1, 512, 64, 0, first_main=True)
            elif nm == "m1":
                run_pass("main", xmain[512:1024, :], 512, 1, 512, 64, 1, last_main=True)
            elif nm == "m1x":
                run_pass("main", xmain[512:1024, :], 512, 1, 512, 64, 1)
            elif nm == "s":
                run_pass("smp", xsmp, 64, 16, 4, 4, 0)
        S.emit()
        k.st.close()
        return nc
    run_pass("pre", xpre[0:512, :], 512, 1, 512, 64, 0)
    if npass > 1:
        S.fence(fz)
        run_pass("pre", xpre[512:1024, :], 512, 1, 512, 64, 1, last_pre=True)
    if npass > 2:
        S.fence(fz)
        run_pass("main", xmain[0:512, :], 512, 1, 512, 64, 0, first_main=True)
    if npass > 3:
        S.fence(fz)
        run_pass("main", xmain[512:1024, :], 512, 1, 512, 64, 1, last_main=True)
    if npass > 4:
        S.fence(fz)
        run_pass("smp", xsmp, 64, 16, 4, 4, 0)
    S.emit()
    k.st.close()
    return nc


def _consts(half):
    c = {}
    c["ident"] = np.eye(128, dtype=np.float32)
    m64 = np.zeros((64, 3, 64), np.float32)
    m64[:, 0, :] = np.triu(np.ones((64, 64), np.float32), 1)
    m64[:, 1, :] = np.triu(np.ones((64, 64), np.float32), 0)
    m64[:, 2, :] = np.tril(np.ones((64, 64), np.float32), -1)
    c["mask64"] = m64
    c["mask4"] = np.ascontiguousarray(m64[0:4, :, 0:4])
    bo = np.zeros((128, 128), np.float32)
    bo[0:64, 0:64] = 1.0
    bo[64:128, 64:128] = 1.0
    c["bones"] = bo
    i = np.arange(128)[:, None]
    kj = np.arange(256)[None, :]
    dist = i - kj + 128
    dm = np.where((dist >= 0) & (dist <= 128), -dist.astype(np.float32), np.float32(NEG)).astype(np.float32)
    c["dm"] = dm
    dm0 = dm.copy()
    if half == 0:
        dm0[:, 0:128] = NEG
    c["dm0"] = dm0
    bc_ = np.zeros((16, 8, 128), np.float32)
    bn_ = np.zeros((16, 8, 4), np.float32)
    for hl in range(4):
        for t in range(4):
            r = hl * 4 + t
            for g in range(8):
                sl = SLOPES[4 * g + hl]
                cc = np.arange(128)
                bc_[r, g, :] = np.where(cc >= t, -sl * (t + 128 - cc), NEG)
                tp = np.arange(4)
                bn_[r, g, :] = np.where(tp <= t, -sl * (t - tp), NEG)
    c["biasc"] = bc_
    c["biasn"] = bn_
    return c


_NC_CACHE = {}


def make_in_maps(inp):
    f = lambda a: np.ascontiguousarray(np.asarray(a, dtype=np.float32))
    x_prompt = f(inp["x_prompt"])
    x_sample = f(inp["x_sample"])
    state_shift = f(inp["state_shift"])[0]
    state_wkv = f(inp["state_wkv"])[0]
    cache_k = f(inp["cache_k_win"])[0]
    cache_v = f(inp["cache_v_win"])[0]
    mu = f(inp["tshift_mu"])[0]
    cp = np.zeros((128, NCP), np.float32)
    cp[:, 0:48] = mu[0:6144].reshape(48, 128).T
    cp[0:96, 48] = mu[6144:6240]
    cp[0:96, 49] = mu[6240:6336]
    cp[:, 50] = mu[6336:6464]
    cp[:, 51] = mu[6464:6592]
    for j, name in enumerate(("w0", "a0", "k_k", "k_a", "r_k", "ln_x_w", "ln_x_b")):
        cp[:, 52 + 16 * j:68 + 16 * j] = f(inp[name])[0].reshape(2048).reshape(16, 128).T
    sinks = f(inp["attn_sinks"])[0]
    sinkb = np.ascontiguousarray(np.tile(sinks[None, :], (128, 1)))
    sinkt = np.zeros((16, 8), np.float32)
    for hl in range(4):
        for t in range(4):
            sinkt[hl * 4 + t, :] = sinks[hl::4]
    shared = dict(
        w_in=f(inp["w_in"])[0], w_out=f(inp["w_out"])[0], w_up=f(inp["w_up"])[0], w_down=f(inp["w_down"])[0],
        w_lora=f(inp["w_lora"])[0], a_lora=f(inp["a_lora"])[0], g_lora=f(inp["g_lora"])[0],
        nmix=f(inp["norm_mix_w"])[0], nmlp=f(inp["norm_mlp_w"])[0], nfin=f(inp["norm_final_w"]),
        cp=cp, sinkb=sinkb, sinkt=sinkt)
    in_maps = []
    for c in range(NCORES):
        b, half = c // 2, c % 2
        m = dict(shared)
        m.update(_consts(half))
        m["xpre"] = x_prompt[b, 0:1024] if half == 1 else np.zeros((1024, D), np.float32)
        m["xmain"] = np.ascontiguousarray(x_prompt[b, half * 1024:(half + 1) * 1024])
        m["xsmp"] = np.ascontiguousarray(x_sample[16 * c:16 * c + 16].reshape(64, D))
        m["sshift"] = np.ascontiguousarray(state_shift[16 * c:16 * c + 16])
        m["swkv"] = np.ascontiguousarray(state_wkv[16 * c:16 * c + 16].reshape(16, 16, 128, 64))
        m["ck"] = np.ascontiguousarray(cache_k[16 * c:16 * c + 16].reshape(16, 128, 512))
        m["cv"] = np.ascontiguousarray(cache_v[16 * c:16 * c + 16].reshape(16, 128, 512))
        in_maps.append(m)
    return in_maps


def kernel(**inp):
    in_maps = make_in_maps(inp)
    if "nc" not in _NC_CACHE:
        _NC_CACHE["nc"] = build_program()
    nc = _NC_CACHE["nc"]
    res = run_bass_kernel_spmd(nc, in_maps, core_ids=list(range(NCORES))).results
    y_prompt = np.zeros((4, 2048, D), np.float32)
    y_sample = np.zeros((128, 4, D), np.float32)
    shift_p = np.zeros((1, 4, RC), np.float32)
    wkv_p = np.zeros((1, 4, 32, 64, 64), np.float32)
    kw_p = np.zeros((1, 4, 128, 8, 64), np.float32)
    vw_p = np.zeros((1, 4, 128, 8, 64), np.float32)
    shift_s = np.zeros((1, 128, RC), np.float32)
    wkv_s = np.zeros((1, 128, 32, 64, 64), np.float32)
    kw_s = np.zeros((1, 128, 128, 8, 64), np.float32)
    vw_s = np.zeros((1, 128, 128, 8, 64), np.float32)
    for c in range(NCORES):
        b, half = c // 2, c % 2
        r = res[c]
        y_prompt[b, half * 1024:(half + 1) * 1024] = r["y_main"]
        y_sample[16 * c:16 * c + 16] = r["y_smp"].reshape(16, 4, D)
        shift_s[0, 16 * c:16 * c + 16] = r["shift_s"]
        wkv_s[0, 16 * c:16 * c + 16] = r["wkv_s"].reshape(16, 32, 64, 64)
        kw_s[0, 16 * c:16 * c + 16] = r["kwin_s"].reshape(16, 128, 8, 64)
        vw_s[0, 16 * c:16 * c + 16] = r["vwin_s"].reshape(16, 128, 8, 64)
        if half == 1:
            shift_p[0, b] = r["shift_p"].reshape(RC)
            wkv_p[0, b] = r["wkv_p"].reshape(32, 64, 64)
            kw_p[0, b] = r["kwin_p"].reshape(128, 8, 64)
            vw_p[0, b] = r["vwin_p"].reshape(128, 8, 64)
    return (y_prompt, y_sample, shift_p, wkv_p, kw_p, vw_p, shift_s, wkv_s, kw_s, vw_s)
```

```python
import contextlib
import math
import os
_DBG = int(os.environ.get('K_DBG', '0'))
import numpy as np
import concourse.bass as bass
import concourse.mybir as mybir
from concourse.bass_utils import run_bass_kernel_spmd

F32 = mybir.dt.float32
BF16 = mybir.dt.bfloat16
AF = mybir.ActivationFunctionType
ALU = mybir.AluOpType
AX = mybir.AxisListType

D = 2048
RC = 6592
CIN = 13760
DFF = 8192
NCORES = 8
Q0 = RC
KA0 = RC + 2048
VA0 = KA0 + 512
GA0 = VA0 + 512
GB0 = GA0 + 2048
RMS_EPS = 1e-5
GN_EPS = 64e-5
NEG = -1.0e9
SLOPES = [2.0 ** (-8.0 * (h + 1.0) / 32.0) for h in range(32)]
NCP = 164


class _Op:
    __slots__ = ("eng", "fn", "dma", "deps", "sig", "semi", "val")

    def __init__(self, eng, fn, dma):
        self.eng = eng
        self.fn = fn
        self.dma = dma
        self.deps = []
        self.sig = False
        self.semi = None
        self.val = 0


class Sched:
    ENGS = ("pe", "act", "dve", "pool", "sp")
    NDMA = 8

    def __init__(self, nc):
        self.nc = nc
        self.ops = {e: [] for e in self.ENGS}
        self.last_w = {}
        self.rd_c = {}
        self.rd_d = {}
        self.dma_hist = {e: [] for e in self.ENGS}
        self.out_dmas = []
        self.nops = 0

    def op(self, eng, fn, reads=(), writes=(), dma=False, is_out=False):
        o = _Op(eng, fn, dma)
        self.nops += 1
        deps = []
        for k in reads:
            w = self.last_w.get(k)
            if w is not None:
                deps.append(w)
        for k in writes:
            w = self.last_w.get(k)
            if w is not None:
                deps.append(w)
            c = self.rd_c.get(k)
            if c:
                deps.extend(c.values())
            dd = self.rd_d.get(k)
            if dd:
                deps.extend(dd)
        if dma:
            h = self.dma_hist[eng]
            if len(h) >= self.NDMA:
                deps.append(h[-self.NDMA])
            h.append(o)
        seen = set()
        for d in deps:
            if id(d) in seen or d is o:
                continue
            seen.add(id(d))
            if d.eng != eng or d.dma or eng != "pe":
                o.deps.append(d)
                d.sig = True
        for k in reads:
            if dma:
                self.rd_d.setdefault(k, []).append(o)
            else:
                self.rd_c.setdefault(k, {})[eng] = o
        for k in writes:
            self.last_w[k] = o
            self.rd_c[k] = {}
            self.rd_d[k] = []
        self.ops[eng].append(o)
        if is_out:
            self.out_dmas.append(o)
        return o

    def fence(self, fn):
        keys = set(self.last_w.keys()) | set(self.rd_c.keys()) | set(self.rd_d.keys())
        keys = sorted(keys)
        self.op("dve", fn, reads=keys, writes=keys)

    def emit(self):
        nc = self.nc
        fin = _Op("sp", None, False)
        for d in self.out_dmas:
            fin.deps.append(d)
            d.sig = True
        self.ops["sp"].append(fin)
        sem_names = ["c_" + e for e in self.ENGS]
        for e in self.ENGS:
            for i in range(self.NDMA):
                sem_names.append("d_%s_%d" % (e, i))
        with contextlib.ExitStack() as st:
            sems = {n: st.enter_context(nc.semaphore(n)) for n in sem_names}
            for e in self.ENGS:
                cnt = 0
                dcnt = [0] * self.NDMA
                di = 0
                for o in self.ops[e]:
                    if o.dma:
                        s = di % self.NDMA
                        di += 1
                        dcnt[s] += 16
                        o.semi = "d_%s_%d" % (e, s)
                        o.val = dcnt[s]
                        o.sig = True
                    elif o.sig:
                        cnt += 1
                        o.semi = "c_" + e
                        o.val = cnt
            block = st.enter_context(nc.Block())
            engmap = {"pe": block.tensor, "act": block.scalar, "dve": block.vector,
                      "pool": block.gpsimd, "sp": block.sync}

            def make(e):
                def body(eng):
                    waited = {}
                    for o in self.ops[e]:
                        for d in o.deps:
                            if waited.get(d.semi, 0) >= d.val:
                                continue
                            waited[d.semi] = d.val
                            eng.wait_ge(sems[d.semi], d.val)
                        if o.fn is None:
                            continue
                        ins = o.fn(eng)
                        if o.sig:
                            ins.then_inc(sems[o.semi], 16 if o.dma else 1)
                return body

            for e in self.ENGS:
                if self.ops[e]:
                    engmap[e](make(e))


class KB:
    def __init__(self):
        self.nc = bass.Bass("TRN2", target_bir_lowering=False)
        self.S = Sched(self.nc)
        self.st = contextlib.ExitStack()

    def din(self, name, shape):
        return self.nc.dram_tensor(name, list(shape), F32, kind="ExternalInput").ap()

    def dout(self, name, shape):
        return self.nc.dram_tensor(name, list(shape), F32, kind="ExternalOutput").ap()

    def sb(self, name, shape, dt=F32):
        return self.st.enter_context(self.nc.sbuf_tensor(name, list(shape), dt))

    def ps(self, name, shape, dt=F32):
        return self.st.enter_context(self.nc.psum_tensor(name, list(shape), dt))

    def mm(self, out, lhsT, rhs, start, stop, r, w):
        self.S.op("pe", lambda e: e.matmul(out, lhsT=lhsT, rhs=rhs, start=start, stop=stop), r, w)

    def tr(self, out, in_, ident, r, w):
        self.S.op("pe", lambda e: e.transpose(out, in_, ident), r, w)

    def act(self, out, in_, func, r, w, bias=0.0, scale=1.0, accum=None):
        if accum is None:
            self.S.op("act", lambda e: e.activation(out=out, in_=in_, func=func, bias=bias, scale=scale), r, w)
        else:
            self.S.op("act", lambda e: e.activation(out=out, in_=in_, func=func, bias=bias, scale=scale,
                                                    accum_out=accum), r, w)

    def tt(self, out, in0, in1, op, r, w, eng="dve"):
        self.S.op(eng, lambda e: e.tensor_tensor(out=out, in0=in0, in1=in1, op=op), r, w)

    def ts(self, out, in0, s1, s2, op0, op1, r, w, eng="dve"):
        if s2 is None:
            self.S.op(eng, lambda e: e.tensor_single_scalar(out=out, in_=in0, scalar=s1, op=op0), r, w)
        else:
            self.S.op(eng, lambda e: e.tensor_scalar(out=out, in0=in0, scalar1=s1, scalar2=s2, op0=op0, op1=op1), r, w)

    def stt(self, out, in0, scalar, in1, op0, op1, r, w):
        self.S.op("dve", lambda e: e.scalar_tensor_tensor(out=out, in0=in0, scalar=scalar, in1=in1,
                                                          op0=op0, op1=op1), r, w)

    def cp(self, out, in_, r, w, eng="dve"):
        if eng == "act" and not (_DBG & 8):
            eng = "dve"
        if eng == "act":
            self.S.op("act", lambda e: e.activation(out=out, in_=in_, func=AF.Copy), r, w)
        else:
            self.S.op(eng, lambda e: e.tensor_copy(out=out, in_=in_), r, w)

    def red(self, out, in_, op, r, w):
        self.S.op("dve", lambda e: e.tensor_reduce(out=out, in_=in_, axis=AX.X, op=op), r, w)

    def recip(self, out, in_, r, w):
        self.S.op("dve", lambda e: e.reciprocal(out=out, in_=in_), r, w)

    def memset(self, ap, val, w, eng="dve"):
        self.S.op(eng, lambda e: e.memset(ap, val), (), w)

    def dma(self, q, out, in_, r, w, is_out=False):
        self.S.op(q, lambda e: e.dma_start(out=out, in_=in_), r, w, dma=True, is_out=is_out)


def bc(ap, shape):
    return ap.broadcast_to(list(shape))


def build_program(npass=5, stop_phase=99, passes=None):
    k = KB()
    nc = k.nc
    xpre = k.din("xpre", [1024, D])
    xmain = k.din("xmain", [1024, D])
    xsmp = k.din("xsmp", [64, D])
    sshift = k.din("sshift", [16, RC])
    swkv = k.din("swkv", [16, 16, 128, 64])
    ckd = k.din("ck", [16, 128, 512])
    cvd = k.din("cv", [16, 128, 512])
    w_in = k.din("w_in", [D, CIN])
    w_out = k.din("w_out", [D, D])
    w_up = k.din("w_up", [D, DFF])
    w_down = k.din("w_down", [DFF, D])
    w_lora = k.din("w_lora", [96, D])
    a_lora = k.din("a_lora", [96, D])
    g_lora = k.din("g_lora", [256, D])
    nmix = k.din("nmix", [D])
    nmlp = k.din("nmlp", [D])
    nfin = k.din("nfin", [D])
    cpd = k.din("cp", [128, NCP])
    sinkbd = k.din("sinkb", [128, 32])
    sinktd = k.din("sinkt", [16, 8])
    identd = k.din("ident", [128, 128])
    mask64d = k.din("mask64", [64, 3, 64])
    mask4d = k.din("mask4", [4, 3, 4])
    bonesd = k.din("bones", [128, 128])
    dmd = k.din("dm", [128, 256])
    dm0d = k.din("dm0", [128, 256])
    biascd = k.din("biasc", [16, 8, 128])
    biasnd = k.din("biasn", [16, 8, 4])

    y_main = k.dout("y_main", [1024, D])
    y_smp = k.dout("y_smp", [64, D])
    shift_p = k.dout("shift_p", [RC, 1])
    wkv_p = k.dout("wkv_p", [16, 128, 64])
    kwin_p = k.dout("kwin_p", [128, 512])
    vwin_p = k.dout("vwin_p", [128, 512])
    shift_s = k.dout("shift_s", [16, RC])
    wkv_s = k.dout("wkv_s", [16, 16, 128, 64])
    kwin_s = k.dout("kwin_s", [16, 128, 512])
    vwin_s = k.dout("vwin_s", [16, 128, 512])

    XN = k.sb("XN", [128, 16, 512], BF16)
    MIX = k.sb("MIX", [128, 16, 512], BF16)
    H = k.sb("H", [128, 4, D], F32)
    WA = k.sb("WA", [128, 12288], BF16)
    NW = k.sb("NW", [128, D], F32)
    XNT = k.sb("XNT", [128, D], BF16)
    PEXT = k.sb("PEXT", [128, 520], F32)
    Rt = k.sb("Rt", [128, 512], F32)
    Kt = k.sb("Kt", [128, 512], F32)
    Vt = k.sb("Vt", [128, 512], F32)
    LW = k.sb("LW", [128, 512], F32)
    ASIG = k.sb("ASIG", [128, 512], F32)
    KK = k.sb("KK", [128, 512], F32)
    TMP1 = k.sb("TMP1", [128, 512], F32)
    TMP2 = k.sb("TMP2", [128, 512], F32)
    CUM = k.sb("CUM", [128, 512], F32)
    E1 = k.sb("E1", [128, 512], F32)
    Yt = k.sb("Yt", [128, 512], F32)
    ONES = k.sb("ONES", [128, 512], F32)
    MIXA = PEXT[:, 0:512]
    Gb = k.sb("Gb", [128, 512], BF16)
    SGA = k.sb("SGA", [128, 512], BF16)
    SGB = k.sb("SGB", [128, 512], BF16)
    TW = k.sb("TW", [128, 512], BF16)
    DAb = k.sb("DAb", [128, 512], BF16)
    SG = k.sb("SG", [128, 2, 512], BF16)
    AR = k.sb("AR", [128, 1024], BF16)
    KT = k.sb("KT", [128, 512], BF16)
    BT = k.sb("BT", [128, 512], BF16)
    VB = k.sb("VB", [128, 512], BF16)
    SQ = k.sb("SQ", [128, 512], BF16)
    WC = k.sb("WC", [128, 16], F32)
    BASE = k.sb("BASE", [128, 16], F32)
    TM3 = [k.sb("TM3_%d" % i, [64, 2, 3, 128], BF16) for i in range(2)]
    ABRM = [k.sb("ABRM%d" % i, [64, 4, 128], BF16) for i in range(2)]
    AKRM = [k.sb("AKRM%d" % i, [64, 4, 128], BF16) for i in range(2)]
    PTC = [[k.sb("PTC%d_%d" % (i, j), [64, 4, 2, 64], BF16) for j in range(2)] for i in range(2)]
    PT = [[k.sb("PT%d_%d" % (i, j), [64, 4, 64], BF16) for j in range(2)] for i in range(2)]
    ZB = k.sb("ZB", [64, 2, 64], BF16)
    UB = k.sb("UB", [64, 2, 64], BF16)
    SST = k.sb("SST", [128, 16, 64], F32)
    SBF = k.sb("SBF", [128, 16, 2, 64], BF16)
    KTP = k.sb("KTP", [128, 2, 512], BF16)
    BTP = k.sb("BTP", [128, 2, 512], BF16)
    SHIFT = k.sb("SHIFT", [128, 52], F32)
    CP = k.sb("CP", [128, NCP], F32)
    OMU = k.sb("OMU", [128, 52], F32)
    SINKB = k.sb("SINKB", [128, 32], F32)
    SINKT = k.sb("SINKT", [16, 8], F32)
    IDF = k.sb("IDF", [128, 128], F32)
    IDB = k.sb("IDB", [128, 128], BF16)
    BONES = k.sb("BONES", [128, 128], BF16)
    BONESF = k.sb("BONESF", [128, 128], F32)
    M64 = k.sb("M64", [64, 3, 64], F32)
    M4 = k.sb("M4", [4, 3, 4], F32)
    DM = k.sb("DM", [128, 256], F32)
    DM0 = k.sb("DM0", [128, 256], F32)
    EPSC = k.sb("EPSC", [128, 2], F32)
    LWT = k.sb("LWT", [128, 2, 4, 128], BF16)
    ST = k.sb("ST", [128, 16], F32)
    DUM = k.sb("DUM", [128, 4], F32)
    PA16 = k.sb("PA16", [128, 11264], BF16)
    TS = k.sb("TS", [128, 256], F32)
    BH = k.sb("BH", [128, 2, 256], F32)
    BH0 = k.sb("BH0", [128, 2, 256], F32)
    PB = k.sb("PB", [128, 256], BF16)
    PN = k.sb("PN", [128, 256], BF16)
    PNT = k.sb("PNT", [128, 256], BF16)
    KWO = CUM
    VWO = E1
    SWI = k.sb("SWI", [128, 128], F32)
    SOB = k.sb("SOB", [128, 128], F32)
    SOUT = [k.sb("SOUT%d" % i, [128, 64], F32) for i in range(2)]
    SS = [k.sb("SS%d" % i, [128, 64], F32) for i in range(2)]
    SSB = [k.sb("SSB%d" % i, [128, 2, 64], BF16) for i in range(2)]
    BIASC = k.sb("BIASC", [16, 8, 128], F32)
    BIASN = k.sb("BIASN", [16, 8, 4], F32)

    KC = PA16[:, 0:5120].rearrange("p (g t) -> p g t", g=8)
    VT = PA16[:, 5120:7680].rearrange("p (a c) -> p a c", a=5)
    QP = PA16[:, 7680:8704].rearrange("p (h t) -> p h t", h=2)
    QT = PA16[0:64, 0:2048]
    QC = PA16[0:64, 2048:4096].rearrange("p (s h t) -> p s h t", s=16, h=32)
    VNTb = PA16[0:64, 4096:4608].rearrange("p (g d) -> p g d", g=8)
    KNC = PA16[0:64, 4608:5120].rearrange("p (g t) -> p g t", g=8)
    CKb = PA16[:, 5120:5632].rearrange("p (g d) -> p g d", g=8)
    CVb = PA16[:, 5632:6144].rearrange("p (g d) -> p g d", g=8)
    CKC = PA16[0:64, 6144:7168].rearrange("p (g t) -> p g t", g=8)
    PNs = PA16[0:16, 7168:8224].rearrange("p (g t) -> p g t", g=8)
    PNF = PA16[0:16, 8224:8736].rearrange("p (g t) -> p g t", g=8)
    PNTc = PA16[:, 8736:8864].rearrange("p (g t) -> p g t", g=8)
    PNTn = PA16[0:64, 8864:8992].rearrange("p (g t) -> p g t", g=8)
    SGBS = PA16[:, 8992:10016].rearrange("p (a t) -> p a t", a=16)
    HA = H[:, 1:4, :].rearrange("p a b -> p (a b)")
    KNT = HA[0:64, 0:512]
    VNT = HA[0:64, 512:1024]
    CKf = HA[:, 1024:1536]
    CVf = HA[:, 1536:2048]
    TSs = HA[0:16, 2048:3104].rearrange("p (g t) -> p g t", g=8)
    YBS = HA[:, 3104:4128].rearrange("p (a t) -> p a t", a=16)
    SHS = HA[:, 4128:4960].rearrange("p (b s) -> p b s", b=52)
    SSR = HA[0:16, 4960:5984]

    psA = [k.ps("psA%d" % i, [128, 512]) for i in range(2)]
    psTb = k.ps("psTb", [128, 8, 128], BF16)
    psTf = k.ps("psTf", [128, 512])
    psB4 = k.ps("psB4", [128, 512])
    psB5 = k.ps("psB5", [128, 512])
    psB6 = k.ps("psB6", [128, 512])
    psB7 = k.ps("psB7", [128, 512])

    S = k.S
    mmctr = [0]

    k.dma("sp", CP[:], cpd, [], ["CP"])
    k.dma("sp", SINKB[:], sinkbd, [], ["SINKB"])
    k.dma("sp", SINKT[:], sinktd, [], ["SINKT"])
    k.dma("sp", IDF[:], identd, [], ["IDF"])
    k.dma("sp", BONESF[:], bonesd, [], ["BONESF"])
    k.dma("sp", M64[:], mask64d, [], ["M64"])
    k.dma("sp", M4[:], mask4d, [], ["M4"])
    k.dma("sp", DM[:], dmd, [], ["DM"])
    k.dma("sp", DM0[:], dm0d, [], ["DM0"])
    k.dma("sp", BIASC[:], biascd, [], ["BIASC"])
    k.dma("sp", BIASN[:], biasnd, [], ["BIASN"])
    k.cp(IDB[:], IDF[:], ["IDF"], ["IDB"])
    k.cp(BONES[:], BONESF[:], ["BONESF"], ["BONES"])
    k.ts(OMU[:], CP[:, 0:52], -1.0, 1.0, ALU.mult, ALU.add, ["CP"], ["OMU"])
    k.memset(ONES[:], 1.0, ["ONES"])
    k.memset(EPSC[:, 0:1], RMS_EPS, ["EPSC"])
    k.memset(EPSC[:, 1:2], GN_EPS, ["EPSC"])
    k.memset(SST[:], 0.0, ["SST"])
    k.memset(SBF[:], 0.0, ["SBF"])
    k.memset(SHIFT[:], 0.0, ["SHIFT"])
    k.memset(BASE[:], 0.0, ["BASE"])
    k.memset(PA16[:], 0.0, ["KC", "VT", "QP"])
    k.memset(SWI[:], 0.0, ["SWI"])
    k.memset(SOB[:], 0.0, ["SOB"])
    k.memset(DUM[:], 0.0, ["DUM"])
    k.memset(AR[:], 0.0, ["AR"])
    k.memset(KTP[:], 0.0, ["KTP"])
    k.memset(BTP[:], 0.0, ["BTP"])
    for i_ in range(2):
        k.memset(SSB[i_][:], 0.0, ["SS%db" % i_])

    MU = CP[:, 0:52]
    W0C = CP[:, 52:68]
    A0C = CP[:, 68:84]
    KKC = CP[:, 84:100]
    KAC = CP[:, 100:116]
    RKC = CP[:, 116:132]
    LNW = CP[:, 132:148]
    LNB = CP[:, 148:164]

    wslot = [0]

    def load_w_in(segs):
        s = wslot[0] % 6
        wslot[0] += 1
        key = "wa%d" % s
        v = WA[:, s * 2048:(s + 1) * 2048].rearrange("p (a b) -> p a b", a=16)
        for (c0, n, off) in segs:
            k.dma("pool", v[:, :, off:off + n], w_in[:, c0:c0 + n].rearrange("(kc p) c -> p kc c", p=128),
                  [], [key])
        return v, key

    def cm_matmul(v, key, ncols, T, t0=0):
        b = mmctr[0] % 2
        mmctr[0] += 1
        pk = "psA%d" % b
        for kc in range(16):
            k.mm(psA[b][0:ncols, 0:T], v[:, kc, 0:ncols], XN[:, kc, t0:t0 + T], kc == 0, kc == 15,
                 [key, "XN"], [pk])
        return psA[b], pk

    def run_pass(kind, xsrc, T, nseg, L, C, pidx, last_pre=False, first_main=False, last_main=False):
        full = kind != "pre"
        smp = kind == "smp"
        NCH = T // C
        ntile = max(1, T // 128)
        rows = min(128, T)
        MK = M64 if C == 64 else M4
        nst = int(round(math.log2(C)))

        def seg3(ap2):
            return ap2.rearrange("p (s l) -> p s l", s=nseg)

        def ch3(ap2):
            return ap2.rearrange("p (n c) -> p n c", n=NCH)

        k.dma("sp", NW[:], nmix.partition_broadcast(128), [], ["NW"])
        for ti in range(ntile):
            hk = "H%d" % ti
            k.dma("sp", H[0:rows, ti, :], xsrc[ti * 128:ti * 128 + rows, :], [], [hk])
            k.act(XNT[0:rows, :], H[0:rows, ti, :], AF.Square, [hk], ["XNT", "ST0"], accum=ST[0:rows, 0:1])
            k.act(ST[0:rows, 1:2], ST[0:rows, 0:1], AF.Sqrt, ["ST0", "EPSC"], ["ST1"], bias=EPSC[0:rows, 0:1],
                  scale=1.0 / D)
            k.recip(ST[0:rows, 1:2], ST[0:rows, 1:2], ["ST1"], ["ST1"])
            k.stt(XNT[0:rows, :], H[0:rows, ti, :], ST[0:rows, 1:2], NW[0:rows, :], ALU.mult, ALU.mult,
                  [hk, "ST1", "NW"], ["XNT"])
            for half in range(2):
                for j in range(8):
                    kc = half * 8 + j
                    k.tr(psTb[:, j, 0:rows], XNT[0:rows, kc * 128:(kc + 1) * 128], IDB[0:rows, 0:rows],
                         ["XNT", "IDB"], ["psTb"])
                k.cp(XN[:, half * 8:(half + 1) * 8, ti * 128:ti * 128 + rows], psTb[:, :, 0:rows],
                     ["psTb"], ["XN"], eng=("act" if half else "dve"))

        if stop_phase <= 1:
            return
        if smp:
            for c8 in range(6):
                k.dma("sp", SSR[:, :], sshift[:, c8 * 1024:(c8 + 1) * 1024], [], ["SSR"])
                for j in range(8):
                    k.tr(psTf[:, j * 16:(j + 1) * 16], SSR[:, j * 128:(j + 1) * 128], IDF[0:16, 0:16],
                         ["SSR", "IDF"], ["psTf"])
                k.cp(SHS[:, c8 * 8:(c8 + 1) * 8, :], psTf[:, 0:128].rearrange("p (b s) -> p b s", b=8),
                     ["psTf"], ["SHS"])
            k.dma("sp", SSR[:, 0:448], sshift[:, 6144:6592], [], ["SSR"])
            for j, (o, n) in enumerate(((0, 96), (96, 96), (192, 128), (320, 128))):
                k.tr(psTf[0:n, j * 16:(j + 1) * 16], SSR[:, o:o + n], IDF[0:16, 0:16], ["SSR", "IDF"], ["psTf"])
            k.memset(SHS[:, 48:50, :], 0.0, ["SHS"])
            k.cp(SHS[0:96, 48:50, :], psTf[0:96, 0:32].rearrange("p (b s) -> p b s", b=2), ["psTf"], ["SHS"])
            k.cp(SHS[:, 50:52, :], psTf[:, 32:64].rearrange("p (b s) -> p b s", b=2), ["psTf"], ["SHS"])

        def shift_block(ps, pk, blk, nr, out2, okeys):
            PX = PEXT[0:nr, 0:nseg * (L + 1)].rearrange("p (s l) -> p s l", s=nseg)
            if smp:
                prev = SHS[0:nr, blk, :]
                pkey = "SHS"
            else:
                prev = SHIFT[0:nr, blk:blk + 1]
                pkey = "SHIFT"
            k.cp(PX[:, :, 0], prev, [pkey], ["PEXT"])
            k.act(PX[:, :, 1:L + 1], seg3(ps[0:nr, 0:T]), AF.Copy, [pk], ["PEXT"])
            k.cp(prev, PX[:, :, L], ["PEXT"], [pkey])
            k.ts(seg3(TMP2[0:nr, 0:T]), PX[:, :, 0:L], MU[0:nr, blk:blk + 1], None, ALU.mult, None,
                 ["PEXT", "CP"], ["TMP2"])
            k.stt(seg3(out2), PX[:, :, 1:L + 1], OMU[0:nr, blk:blk + 1], seg3(TMP2[0:nr, 0:T]),
                  ALU.mult, ALU.add, ["PEXT", "OMU", "TMP2"], okeys)

        v, key = load_w_in([(6144, 96, 0)])
        ps, pk = cm_matmul(v, key, 96, T)
        shift_block(ps, pk, 48, 96, TMP1[0:96, 0:T], ["TMP1"])
        k.act(TW[0:96, 0:T], TMP1[0:96, 0:T], AF.Tanh, ["TMP1"], ["TW"])
        v, key = load_w_in([(6240, 96, 0)])
        ps, pk = cm_matmul(v, key, 96, T)
        shift_block(ps, pk, 49, 96, DAb[0:96, 0:T], ["DAb"])
        if full or last_pre:
            for j in range(2):
                v, key = load_w_in([(6336 + j * 128, 128, 0)])
                ps, pk = cm_matmul(v, key, 128, T)
                shift_block(ps, pk, 50 + j, 128, TMP1[:, 0:T], ["TMP1"])
                k.act(SG[:, j, 0:T], TMP1[:, 0:T], AF.Sigmoid, ["TMP1"], ["SG"])

        if stop_phase <= 2:
            return
        if kind == "main":
            k.cp(KC[:, :, 0:128], KC[:, :, 512:640], ["KC"], ["KC"])
            k.cp(VT[:, 0, :], VT[:, 4, :], ["VT"], ["VT"])
        if kind == "main" or last_pre:
            tl = T - 128
            for g in range(8):
                v, key = load_w_in([(KA0 + g * 64, 64, 0), (KA0 + g * 64, 64, 64)])
                if kind == "main":
                    ps, pk = cm_matmul(v, key, 128, T)
                    k.cp(KC[:, g, 128:640], ps[:, 0:T], [pk], ["KC"], eng="act")
                else:
                    ps, pk = cm_matmul(v, key, 128, 128, t0=tl)
                    k.cp(KC[:, g, 512:640], ps[:, 0:128], [pk], ["KC"], eng="act")
            for j in range(4):
                v, key = load_w_in([(VA0 + j * 128, 128, 0)])
                for ti in (range(ntile) if kind == "main" else [ntile - 1]):
                    b = mmctr[0] % 2
                    mmctr[0] += 1
                    pk = "psA%d" % b
                    for kc in range(16):
                        k.mm(psA[b][:, 0:128], XN[:, kc, ti * 128:(ti + 1) * 128], v[:, kc, :], kc == 0, kc == 15,
                             [key, "XN"], [pk])
                    k.cp(VT[:, 1 + ti, j * 128:(j + 1) * 128], psA[b][:, 0:128], [pk], ["VT"], eng="act")
                    if last_main and ti == ntile - 1:
                        k.cp(VWO[:, j * 128:(j + 1) * 128], psA[b][:, 0:128], [pk], ["E1"])
            if last_main and not (_DBG & 4):
                for j in range(4):
                    v, key = load_w_in([(KA0 + j * 128, 128, 0)])
                    b = mmctr[0] % 2
                    mmctr[0] += 1
                    pk = "psA%d" % b
                    for kc in range(16):
                        k.mm(psA[b][:, 0:128], XN[:, kc, tl:tl + 128], v[:, kc, :], kc == 0, kc == 15,
                             [key, "XN"], [pk])
                    k.cp(KWO[:, j * 128:(j + 1) * 128], psA[b][:, 0:128], [pk], ["CUM"])
                k.dma("sp", kwin_p, KWO[:], ["CUM"], ["o_kwin"], is_out=True)
                k.dma("sp", vwin_p, VWO[:], ["E1"], ["o_vwin"], is_out=True)
        if smp:
            for j in range(8):
                v, key = load_w_in([((KA0 if j < 4 else VA0) + (j % 4) * 128, 128, 0)])
                b = mmctr[0] % 2
                mmctr[0] += 1
                pk = "psA%d" % b
                for kc in range(16):
                    k.mm(psA[b][0:64, 0:128], XN[:, kc, 0:64], v[:, kc, :], kc == 0, kc == 15, [key, "XN"], [pk])
                if j < 4:
                    k.cp(KNT[:, j * 128:(j + 1) * 128], psA[b][0:64, 0:128], [pk], ["KNT"])
                else:
                    jj = j - 4
                    k.cp(VNT[:, jj * 128:(jj + 1) * 128], psA[b][0:64, 0:128], [pk], ["VNT"])
                    k.cp(VNTb[:, 2 * jj:2 * jj + 2, :], psA[b][0:64, 0:128].rearrange("p (g d) -> p g d", g=2),
                         [pk], ["VNTb"], eng="act")
            k.cp(QT[:, 0:512], KNT[:, :], ["KNT"], ["QT"])
            for g in range(8):
                k.tr(psTb[0:64, g, 0:64], QT[:, g * 64:(g + 1) * 64], IDB[0:64, 0:64], ["QT", "IDB"], ["psTb"])
            k.cp(KNC[:, :, :], psTb[0:64, :, 0:64], ["psTb"], ["KNC"])

        def wkv_pair(p, get_state, put_state):
            ARv = AR[:, 0:2 * T].rearrange("p (n a c) -> p n a c", n=NCH, a=2)
            ABR = psB4[0:C, 0:4 * 2 * C].rearrange("p (h c) -> p h c", h=4)
            AKR = psB5[0:C, 0:4 * 2 * C].rearrange("p (h c) -> p h c", h=4)
            IBp = psB6[0:C, 256:256 + 4 * C].rearrange("p (h c) -> p h c", h=4)
            IA = psB4[0:C, 0:4 * 2 * C].rearrange("p (h c) -> p h c", h=4)
            mask2 = MK[0:C, 0:2, :].rearrange("p a c -> p (a c)")
            fin = {}

            def prep(gi):
                sl = gi % 2
                gk = "g%d" % sl
                TMg = TM3[sl][0:C, :, :, :]
                tmk = "TM3_%d" % sl
                for nl in range(2):
                    n = gi * 2 + nl
                    for qi, (SRC, sk) in enumerate(((VB, "VB"), (KT, "KT"), (BT, "BT"))):
                        k.tr(psTb[0:C, nl * 3 + qi, :], SRC[:, n * C:(n + 1) * C], IDB[:, :], [sk, "IDB"], ["psTb"])
                k.cp(TMg, psTb[0:C, 0:6, :].rearrange("p (n a) c -> p n a c", n=2), ["psTb"], [tmk], eng="act")
                yield
                for nl in range(2):
                    n = gi * 2 + nl
                    for hh in range(2):
                        hc = nl * 2 + hh
                        k.mm(ABR[:, hc, :], BTP[:, hh, n * C:(n + 1) * C], AR[:, n * 2 * C:(n + 1) * 2 * C], True, True,
                             ["BTP", "AR"], ["psB4"])
                        k.mm(AKR[:, hc, :], KTP[:, hh, n * C:(n + 1) * C], AR[:, n * 2 * C:(n + 1) * 2 * C], True, True,
                             ["KTP", "AR"], ["psB5"])
                abrm = ABRM[sl][0:C, :, 0:2 * C]
                akrm = AKRM[sl][0:C, :, 0:2 * C]
                k.tt(abrm, ABR, bc(mask2.unsqueeze(1), [C, 4, 2 * C]), ALU.mult, ["psB4", "MK"], [gk + "abr"])
                k.tt(akrm, AKR, bc(mask2.unsqueeze(1), [C, 4, 2 * C]), ALU.mult, ["psB5", "MK"], [gk + "akr"])
                yield
                cur = 0
                ptc = PTC[sl][cur][0:C, :, :, 0:C]
                pt = PT[sl][cur][0:C, :, 0:C]
                for hc in range(4):
                    k.tr(psTb[0:C, hc, 0:C], ABRM[sl][0:C, hc, 0:C], IDB[0:C, 0:C], [gk + "abr", "IDB"], ["psTb"])
                k.cp(pt, psTb[0:C, 0:4, 0:C], ["psTb"], [gk + "pt0"])
                k.cp(ptc[:, :, 0, :], ABRM[sl][0:C, :, 0:C], [gk + "abr"], [gk + "ptc0"], eng="act")
                k.cp(ptc[:, :, 1, :], bc(IDB[0:C, 0:C].unsqueeze(1), [C, 4, C]), ["IDB"], [gk + "ptc0"])
                yield
                for s_ in range(nst):
                    last = s_ == nst - 1
                    ptc = PTC[sl][cur][0:C, :, :, 0:C]
                    pt = PT[sl][cur][0:C, :, 0:C]
                    nx = 1 - cur
                    ptcn = PTC[sl][nx][0:C, :, :, 0:C]
                    ptn = PT[sl][nx][0:C, :, 0:C]
                    ck_ = gk + "ptc%d" % cur
                    pk_ = gk + "pt%d" % cur
                    for hc in range(4):
                        if last:
                            k.mm(IA[:, hc, 0:C], pt[:, hc, :], ptc[:, hc, 1, :], True, True, [ck_, pk_], ["psB4"])
                        else:
                            k.mm(IA[:, hc, :], pt[:, hc, :], ptc[:, hc, :, :], True, True, [ck_, pk_], ["psB4"])
                            k.mm(IBp[:, hc, :], ptc[:, hc, 0, :], pt[:, hc, :], True, True, [ck_, pk_], ["psB6"])
                    if last:
                        k.tt(ptcn[:, :, 1, :], ptc[:, :, 1, :], IA[:, :, 0:C], ALU.add, ["psB4", ck_],
                             [gk + "ptc%d" % nx])
                    else:
                        k.cp(ptcn[:, :, 0, :], IA[:, :, 0:C], ["psB4"], [gk + "ptc%d" % nx], eng="act")
                        k.tt(ptcn[:, :, 1, :], ptc[:, :, 1, :], IA[:, :, C:2 * C], ALU.add, ["psB4", ck_],
                             [gk + "ptc%d" % nx])
                        k.cp(ptn, IBp, ["psB6"], [gk + "pt%d" % nx], eng="act")
                    cur = nx
                    yield
                fin[gi] = cur

            def chain(gi):
                sl = gi % 2
                gk = "g%d" % sl
                TMg = TM3[sl][0:C, :, :, :]
                tmk = "TM3_%d" % sl
                cur = fin[gi]
                tk = gk + "ptc%d" % cur
                Tfin = PTC[sl][cur][0:C, :, 1, 0:C]
                for nl in range(2):
                    n = gi * 2 + nl
                    sf, sbf, skey = get_state(n)
                    Zp = psB7[0:C, 0:128].rearrange("p (h i) -> p h i", h=2)
                    Up = psB7[0:C, 128:256].rearrange("p (h i) -> p h i", h=2)
                    for hh in range(2):
                        hc = nl * 2 + hh
                        k.mm(Zp[:, hh, :], ARv[:, n, 0, :], sbf[:, hh, :], True, False, ["AR", skey + "b"], ["psB7"])
                        k.mm(Zp[:, hh, :], AKRM[sl][0:C, hc, 0:C], TMg[:, nl, 0, hh * 64:(hh + 1) * 64], False, True,
                             [gk + "akr", tmk], ["psB7"])
                    k.cp(ZB[0:C, :, :], Zp, ["psB7"], ["ZB"], eng="act")
                    yield
                    for hh in range(2):
                        hc = nl * 2 + hh
                        k.mm(Up[:, hh, :], Tfin[:, hc, :], ZB[0:C, hh, :], True, True, [tk, "ZB"], ["psB7"])
                    k.cp(UB[0:C, :, :], Up, ["psB7"], ["UB"])
                    yield
                    if full:
                        for hh in range(2):
                            hc = nl * 2 + hh
                            hs = slice(hh * 64, hh * 64 + 64)
                            o = psB7[hs, 256:256 + C]
                            k.mm(o, sbf[:, hh, :], ARv[:, n, 1, :], True, False, [skey + "b", "AR"], ["psB7"])
                            k.mm(o, UB[0:C, hh, :], ABRM[sl][0:C, hc, C:2 * C], False, False,
                                 ["UB", gk + "abr"], ["psB7"])
                            k.mm(o, TMg[:, nl, 0, hh * 64:(hh + 1) * 64], AKRM[sl][0:C, hc, C:2 * C], False, True,
                                 [tmk, gk + "akr"], ["psB7"])
                        k.cp(Yt[:, n * C:(n + 1) * C], psB7[:, 256:256 + C], ["psB7"], ["Yt"], eng="act")
                        yield
                    for hh in range(2):
                        hs = slice(hh * 64, hh * 64 + 64)
                        o = psB7[hs, 384:448]
                        k.mm(o, TMg[:, nl, 2, hh * 64:(hh + 1) * 64], UB[0:C, hh, :], True, False,
                             [tmk, "UB"], ["psB7"])
                        k.mm(o, TMg[:, nl, 1, hh * 64:(hh + 1) * 64], TMg[:, nl, 0, hh * 64:(hh + 1) * 64],
                             False, True, [tmk], ["psB7"])
                    k.tt(sf, sf, psB7[:, 384:448], ALU.add, ["psB7", skey], [skey])
                    k.ts(sf, sf, WC[:, n:n + 1], None, ALU.mult, None, [skey, "WC"], [skey])
                    k.cp(sbf[0:64, 0, :], sf[0:64, :], [skey], [skey + "b"], eng="act")
                    k.cp(sbf[64:128, 1, :], sf[64:128, :], [skey], [skey + "b"], eng="act")
                    put_state(n, sf, skey)
                    yield

            NG = NCH // 2
            for _ in prep(0):
                pass
            for gi in range(NG):
                ga = chain(gi)
                gb = prep(gi + 1) if gi + 1 < NG else iter(())
                da = db = False
                while not (da and db):
                    if not da:
                        try:
                            next(ga)
                        except StopIteration:
                            da = True
                    if not db:
                        try:
                            next(gb)
                        except StopIteration:
                            db = True

        octr = [0]

        def state_out(sf, skey, dst):
            k.cp(SOB[0:64, 0:64], sf[0:64, :], [skey], ["SOB"])
            k.cp(SOB[64:128, 64:128], sf[64:128, :], [skey], ["SOB"])
            k.tr(psTf[:, 128:256], SOB[:, :], IDF[:, :], ["SOB", "IDF"], ["psTf"])
            so = SOUT[octr[0] % 2]
            sok = "SOUT%d" % (octr[0] % 2)
            octr[0] += 1
            k.cp(so[0:64, :], psTf[0:64, 128:192], ["psTf"], [sok])
            k.cp(so[64:128, :], psTf[64:128, 192:256], ["psTf"], [sok], eng="act")
            k.dma("sp", dst, so[:, :], [sok], ["o_wkv"], is_out=True)

        for p in range((16 if stop_phase >= 4 else 1) if stop_phase > 3 else 0):
            for (c0, blk, dst, dk) in ((p * 128, p, Rt, "Rt"), (2048 + p * 128, 16 + p, Kt, "Kt"),
                                       (4096 + p * 128, 32 + p, Vt, "Vt")):
                if kind == "pre" and dk == "Rt":
                    if not last_pre:
                        continue
                v, key = load_w_in([(c0, 128, 0)])
                ps, pk = cm_matmul(v, key, 128, T)
                shift_block(ps, pk, blk, 128, dst[:, 0:T], [dk])
            if stop_phase <= 3.1:
                continue
            lsl = p % 2
            lk = "LWT%d" % lsl
            k.dma("pool", LWT[0:96, lsl, 0, :], w_lora[:, p * 128:(p + 1) * 128], [], [lk])
            k.dma("pool", LWT[0:96, lsl, 1, :], a_lora[:, p * 128:(p + 1) * 128], [], [lk])
            if full:
                k.dma("pool", LWT[:, lsl, 2:4, :],
                      g_lora[:, p * 128:(p + 1) * 128].rearrange("(j q) c -> q j c", q=128), [], [lk])
            b = mmctr[0] % 2
            mmctr[0] += 1
            pk = "psA%d" % b
            k.mm(psA[b][:, 0:T], LWT[0:96, lsl, 0, :], TW[0:96, 0:T], True, True, [lk, "TW"], [pk])
            k.act(LW[:, 0:T], psA[b][:, 0:T], AF.Sigmoid, [pk, "CP"], ["LW"], bias=W0C[:, p:p + 1])
            k.ts(LW[:, 0:T], LW[:, 0:T], -math.exp(-0.5), None, ALU.mult, None, ["LW"], ["LW"])
            b = mmctr[0] % 2
            mmctr[0] += 1
            pk = "psA%d" % b
            k.mm(psA[b][:, 0:T], LWT[0:96, lsl, 1, :], DAb[0:96, 0:T], True, True, [lk, "DAb"], [pk])
            k.act(ASIG[:, 0:T], psA[b][:, 0:T], AF.Sigmoid, [pk, "CP"], ["ASIG"], bias=A0C[:, p:p + 1])
            if full:
                b = mmctr[0] % 2
                mmctr[0] += 1
                pk = "psA%d" % b
                for j in range(2):
                    k.mm(psA[b][:, 0:T], LWT[:, lsl, 2 + j, :], SG[:, j, 0:T], j == 0, j == 1,
                         [lk, "SG"], [pk])
                k.cp(Gb[:, 0:T], psA[b][:, 0:T], [pk], ["Gb"], eng="act")
            if stop_phase <= 3.2:
                continue
            k.ts(KK[:, 0:T], Kt[:, 0:T], KKC[:, p:p + 1], None, ALU.mult, None, ["Kt", "CP"], ["KK"])
            k.act(SQ[:, 0:T], KK[:, 0:T], AF.Square, ["KK"], ["SQ"])
            b = mmctr[0] % 2
            mmctr[0] += 1
            pk = "psA%d" % b
            k.mm(psA[b][:, 0:T], BONES[:, :], SQ[:, 0:T], True, True, ["BONES", "SQ"], [pk])
            k.act(TMP2[:, 0:T], psA[b][:, 0:T], AF.Sqrt, [pk], ["TMP2"])
            k.ts(TMP2[:, 0:T], TMP2[:, 0:T], 1e-12, None, ALU.max, None, ["TMP2"], ["TMP2"])
            k.recip(TMP2[:, 0:T], TMP2[:, 0:T], ["TMP2"], ["TMP2"])
            k.tt(KK[:, 0:T], KK[:, 0:T], TMP2[:, 0:T], ALU.mult, ["KK", "TMP2"], ["KK"])
            k.ts(TMP1[:, 0:T], ASIG[:, 0:T], KAC[:, p:p + 1], KAC[:, p:p + 1], ALU.mult, ALU.subtract, ["ASIG", "CP"], ["TMP1"])
            k.stt(Kt[:, 0:T], TMP1[:, 0:T], 1.0, Kt[:, 0:T], ALU.add, ALU.mult, ["TMP1", "Kt"], ["Kt"])
            k.tt(TMP1[:, 0:T], KK[:, 0:T], ASIG[:, 0:T], ALU.mult, ["KK", "ASIG"], ["TMP1"])
            if stop_phase <= 3.3:
                continue
            S.op("dve", lambda e: e.tensor_tensor_scan(out=CUM[:, 0:T], data0=ONES[:, 0:T], data1=LW[:, 0:T],
                                                       initial=0.0, op0=ALU.mult, op1=ALU.add),
                 ["ONES", "LW"], ["CUM"])
            if NCH > 1:
                k.cp(BASE[:, 1:NCH], ch3(CUM[:, 0:T])[:, 0:NCH - 1, C - 1], ["CUM"], ["BASE"])
            k.tt(ch3(CUM[:, 0:T]), ch3(CUM[:, 0:T]), bc(BASE[:, 0:NCH].unsqueeze(2), [128, NCH, C]), ALU.subtract,
                 ["CUM", "BASE"], ["CUM"])
            ARv = AR[:, 0:2 * T].rearrange("p (n a c) -> p n a c", n=NCH, a=2)
            k.act(E1[:, 0:T], CUM[:, 0:T], AF.Exp, ["CUM"], ["E1"])
            k.cp(WC[:, 0:NCH], ch3(E1[:, 0:T])[:, :, C - 1], ["E1"], ["WC"])
            if full:
                k.tt(ARv[:, :, 1, :], ch3(Rt[:, 0:T]), ch3(E1[:, 0:T]), ALU.mult, ["Rt", "E1"], ["AR"])
            k.tt(TMP2[:, 0:T], CUM[:, 0:T], LW[:, 0:T], ALU.subtract, ["CUM", "LW"], ["TMP2"])
            k.act(E1[:, 0:T], TMP2[:, 0:T], AF.Exp, ["TMP2"], ["E1"])
            k.stt(ARv[:, :, 0, :], ch3(KK[:, 0:T]), -1.0, ch3(E1[:, 0:T]), ALU.mult, ALU.mult, ["KK", "E1"], ["AR"])
            k.act(E1[:, 0:T], CUM[:, 0:T], AF.Exp, ["CUM"], ["E1"], scale=-1.0)
            k.tt(KT[:, 0:T], Kt[:, 0:T], E1[:, 0:T], ALU.mult, ["Kt", "E1"], ["KT"])
            k.tt(BT[:, 0:T], TMP1[:, 0:T], E1[:, 0:T], ALU.mult, ["TMP1", "E1"], ["BT"])
            for hh_ in range(2):
                hs_ = slice(hh_ * 64, hh_ * 64 + 64)
                k.cp(KTP[hs_, hh_, 0:T], KT[hs_, 0:T], ["KT"], ["KTP"], eng="act")
                k.cp(BTP[hs_, hh_, 0:T], BT[hs_, 0:T], ["BT"], ["BTP"])
            k.cp(VB[:, 0:T], Vt[:, 0:T], ["Vt"], ["VB"], eng="act")
            if stop_phase <= 3.4:
                continue
            if smp:
                def get_state(n, p=p):
                    i2 = n % 2
                    k.dma("sp", SWI[0:64, 0:64], swkv[n, p, 0:64, :], [], ["SWI"])
                    k.dma("sp", SWI[64:128, 64:128], swkv[n, p, 64:128, :], [], ["SWI"])
                    k.tr(psTf[:, 0:128], SWI[:, :], IDF[:, :], ["SWI", "IDF"], ["psTf"])
                    k.cp(SS[i2][0:64, :], psTf[0:64, 0:64], ["psTf"], ["SS%d" % i2])
                    k.cp(SS[i2][64:128, :], psTf[64:128, 64:128], ["psTf"], ["SS%d" % i2])
                    k.cp(SSB[i2][0:64, 0, :], SS[i2][0:64, :], ["SS%d" % i2], ["SS%db" % i2], eng="act")
                    k.cp(SSB[i2][64:128, 1, :], SS[i2][64:128, :], ["SS%d" % i2], ["SS%db" % i2], eng="act")
                    return SS[i2][:, :], SSB[i2][:, :, :], "SS%d" % i2

                def put_state(n, sf, skey, p=p):
                    state_out(sf, skey, wkv_s[n, p])
                wkv_pair(p, get_state, put_state)
            else:
                def get_state(n, p=p):
                    return SST[:, p, :], SBF[:, p, :, :], "SST%d" % p

                def put_state(n, sf, skey):
                    pass
                wkv_pair(p, get_state, put_state)
                if last_main and not (_DBG & 1):
                    state_out(SST[:, p, :], "SST%d" % p, wkv_p[p])
            if not full:
                continue
            k.cp(SQ[:, 0:T], Yt[:, 0:T], ["Yt"], ["SQ"], eng="act")
            b = mmctr[0] % 2
            mmctr[0] += 1
            pk = "psA%d" % b
            k.mm(psA[b][:, 0:T], BONES[:, :], SQ[:, 0:T], True, True, ["BONES", "SQ"], [pk])
            k.stt(Yt[:, 0:T], psA[b][:, 0:T], -1.0 / 64.0, Yt[:, 0:T], ALU.mult, ALU.add, [pk, "Yt"], ["Yt"])
            k.act(SQ[:, 0:T], Yt[:, 0:T], AF.Square, ["Yt"], ["SQ"])
            b = mmctr[0] % 2
            mmctr[0] += 1
            pk = "psA%d" % b
            k.mm(psA[b][:, 0:T], BONES[:, :], SQ[:, 0:T], True, True, ["BONES", "SQ"], [pk])
            k.act(TMP2[:, 0:T], psA[b][:, 0:T], AF.Sqrt, [pk, "EPSC"], ["TMP2"], bias=EPSC[:, 1:2], scale=1.0 / 64.0)
            k.recip(TMP2[:, 0:T], TMP2[:, 0:T], ["TMP2"], ["TMP2"])
            k.tt(Yt[:, 0:T], Yt[:, 0:T], TMP2[:, 0:T], ALU.mult, ["Yt", "TMP2"], ["Yt"])
            k.ts(Yt[:, 0:T], Yt[:, 0:T], LNW[:, p:p + 1], LNB[:, p:p + 1], ALU.mult, ALU.add, ["Yt", "CP"], ["Yt"])
            k.tt(TMP1[:, 0:T], Rt[:, 0:T], Kt[:, 0:T], ALU.mult, ["Rt", "Kt"], ["TMP1"])
            k.ts(SQ[:, 0:T], TMP1[:, 0:T], RKC[:, p:p + 1], None, ALU.mult, None, ["TMP1", "CP"], ["SQ"])
            b = mmctr[0] % 2
            mmctr[0] += 1
            pk = "psA%d" % b
            k.mm(psA[b][:, 0:T], BONES[:, :], SQ[:, 0:T], True, True, ["BONES", "SQ"], [pk])
            k.tt(TMP1[:, 0:T], psA[b][:, 0:T], Vt[:, 0:T], ALU.mult, [pk, "Vt"], ["TMP1"])
            k.tt(Yt[:, 0:T], Yt[:, 0:T], TMP1[:, 0:T], ALU.add, ["Yt", "TMP1"], ["Yt"])
            k.tt(Yt[:, 0:T], Yt[:, 0:T], Gb[:, 0:T], ALU.mult, ["Yt", "Gb"], ["Yt"])
            v, key = load_w_in([(GA0 + p * 128, 128, 0)])
            ps, pk = cm_matmul(v, key, 128, T)
            k.act(SGA[:, 0:T], ps[:, 0:T], AF.Sigmoid, [pk], ["SGA"])
            v, key = load_w_in([(GB0 + p * 128, 128, 0)])
            ps, pk = cm_matmul(v, key, 128, T)
            if smp:
                k.act(SGBS[:, p, :], ps[:, 0:T], AF.Sigmoid, [pk], ["SGBS"])
                k.tt(MIX[:, p, 0:T], Yt[:, 0:T], SGA[:, 0:T], ALU.mult, ["Yt", "SGA"], ["MIX"])
                v, key = load_w_in([(Q0 + p * 128, 128, 0)])
                b = mmctr[0] % 2
                mmctr[0] += 1
                pk = "psA%d" % b
                for kc in range(16):
                    k.mm(psA[b][0:64, 0:128], XN[:, kc, 0:64], v[:, kc, :], kc == 0, kc == 15, [key, "XN"], [pk])
                k.cp(QT[:, p * 128:(p + 1) * 128], psA[b][0:64, 0:128], [pk], ["QT"], eng="act")
                continue
            k.act(SGB[:, 0:T], ps[:, 0:T], AF.Sigmoid, [pk], ["SGB"])
            k.tt(MIXA[:, 0:T], Yt[:, 0:T], SGA[:, 0:T], ALU.mult, ["Yt", "SGA"], ["PEXT"])
            v, key = load_w_in([(Q0 + p * 128, 128, 0)])
            ps, pk = cm_matmul(v, key, 128, T)
            k.cp(QP[0:64, 0, 0:T], ps[0:64, 0:T], [pk], ["QP"], eng="act")
            k.cp(QP[64:128, 1, 0:T], ps[64:128, 0:T], [pk], ["QP"], eng="act")
            g = p // 2
            for hh in range(2):
                h = 2 * p + hh
                k.ts(BH[:, hh, :], DM[:, :], SLOPES[h], None, ALU.mult, None, ["DM"], ["BH"])
                if first_main:
                    k.ts(BH0[:, hh, :], DM0[:, :], SLOPES[h], None, ALU.mult, None, ["DM0"], ["BH0"])
            for n in range(T // 128):
                for hh in range(2):
                    h = 2 * p + hh
                    hs = slice(hh * 64, hh * 64 + 64)
                    bh = BH0 if (first_main and n == 0) else BH
                    bhk = "BH0" if (first_main and n == 0) else "BH"
                    k.mm(psB5[:, 0:256], QP[:, hh, n * 128:(n + 1) * 128], KC[:, g, n * 128:n * 128 + 256], True, True,
                         ["QP", "KC"], ["psB5"])
                    k.stt(TS[:, :], psB5[:, 0:256], 0.125, bh[:, hh, :], ALU.mult, ALU.add, ["psB5", bhk], ["TS"])
                    k.red(ST[:, 4:5], TS[:, :], ALU.max, ["TS"], ["ST4"])
                    k.tt(ST[:, 5:6], ST[:, 4:5], SINKB[:, h:h + 1], ALU.max, ["ST4", "SINKB"], ["ST5"])
                    k.ts(ST[:, 5:6], ST[:, 5:6], -1.0, None, ALU.mult, None, ["ST5"], ["ST5"])
                    k.act(PB[:, :], TS[:, :], AF.Exp, ["TS", "ST5"], ["PB", "ST6"], bias=ST[:, 5:6], accum=ST[:, 6:7])
                    k.act(ST[:, 7:8], ST[:, 5:6], AF.Exp, ["ST5", "SINKB"], ["ST7"], bias=SINKB[:, h:h + 1])
                    k.tt(ST[:, 8:9], ST[:, 6:7], ST[:, 7:8], ALU.add, ["ST6", "ST7"], ["ST8"])
                    k.recip(ST[:, 8:9], ST[:, 8:9], ["ST8"], ["ST8"])
                    k.ts(PN[:, :], PB[:, :], ST[:, 8:9], None, ALU.mult, None, ["PB", "ST8"], ["PN"])
                    k.tr(psTb[:, 0, :], PN[:, 0:128], IDB[:, :], ["PN", "IDB"], ["psTb"])
                    k.tr(psTb[:, 1, :], PN[:, 128:256], IDB[:, :], ["PN", "IDB"], ["psTb"])
                    k.cp(PNT[:, :].rearrange("p (a c) -> p a c", a=2), psTb[:, 0:2, :], ["psTb"], ["PNT"], eng="act")
                    o = psB7[hs, 256:384]
                    k.mm(o, VT[:, n, g * 64:(g + 1) * 64], PNT[:, 0:128], True, False, ["VT", "PNT"], ["psB7"])
                    k.mm(o, VT[:, n + 1, g * 64:(g + 1) * 64], PNT[:, 128:256], False, True, ["VT", "PNT"], ["psB7"])
                k.tt(TMP1[:, 0:128], psB7[:, 256:384], SGB[:, n * 128:(n + 1) * 128], ALU.mult, ["psB7", "SGB"],
                     ["TMP1"])
                k.tt(MIX[:, p, n * 128:(n + 1) * 128], TMP1[:, 0:128], MIXA[:, n * 128:(n + 1) * 128], ALU.add,
                     ["TMP1", "PEXT"], ["MIX"])

        if not full:
            return

        if smp:
            for h in range(32):
                k.tr(psTb[0:64, h % 8, 0:64], QT[:, h * 64:(h + 1) * 64], IDB[0:64, 0:64], ["QT", "IDB"], ["psTb"])
                if h % 8 == 7:
                    k.cp(QC[:, :, h - 7:h + 1, :], psTb[0:64, :, 0:64].rearrange("p h (s t) -> p s h t", s=16),
                         ["psTb"], ["QC"], eng=("act" if h % 16 == 7 else "dve"))
            k.memset(PNF[:, :, :], 0.0, ["PNF"])
            SCa = psB4[0:16, :].rearrange("p (g t) -> p g t", g=4)
            SCb = psB5[0:16, :].rearrange("p (g t) -> p g t", g=4)
            SCn = psB6[0:16, 0:32].rearrange("p (g t) -> p g t", g=8)
            for s in range(16):
                k.dma("sp", CKf[:, :], ckd[s], [], ["CKf"])
                k.dma("sp", CVf[:, :], cvd[s], [], ["CVf"])
                k.cp(CKb[:, :, :], CKf[:, :].rearrange("p (g d) -> p g d", g=8), ["CKf"], ["CKb"])
                k.cp(CVb[:, :, :], CVf[:, :].rearrange("p (g d) -> p g d", g=8), ["CVf"], ["CVb"], eng="act")
                k.dma("sp", kwin_s[s, 0:124, :], ckd[s, 4:128, :], [], ["o_kws"], is_out=True)
                k.dma("sp", vwin_s[s, 0:124, :], cvd[s, 4:128, :], [], ["o_vws"], is_out=True)
                for g in range(8):
                    k.tr(psTb[0:64, g, :], CKb[:, g, :], IDB[:, :], ["CKb", "IDB"], ["psTb"])
                k.cp(CKC[:, :, :], psTb[0:64, :, :], ["psTb"], ["CKC"], eng="act")
                for g in range(8):
                    sc = (SCa if g < 4 else SCb)
                    sk_ = "psB4" if g < 4 else "psB5"
                    lq = QC[:, s, 4 * g:4 * g + 4, :].rearrange("p h t -> p (h t)")
                    k.mm(sc[:, g % 4, :], lq, CKC[:, g, :], True, True, ["QC", "CKC"], [sk_])
                    k.mm(SCn[:, g, :], lq, KNC[:, g, 4 * s:4 * s + 4], True, True, ["QC", "KNC"], ["psB6"])
                k.stt(TSs[:, 0:4, 0:128], SCa, 0.125, BIASC[:, 0:4, :], ALU.mult, ALU.add, ["psB4", "BIASC"], ["TSs"])
                k.stt(TSs[:, 4:8, 0:128], SCb, 0.125, BIASC[:, 4:8, :], ALU.mult, ALU.add, ["psB5", "BIASC"], ["TSs"])
                k.stt(TSs[:, :, 128:132], SCn, 0.125, BIASN[:, :, :], ALU.mult, ALU.add, ["psB6", "BIASN"], ["TSs"])
                k.red(ST[0:16, 0:8], TSs[:, :, :], ALU.max, ["TSs"], ["STs0"])
                k.tt(ST[0:16, 0:8], ST[0:16, 0:8], SINKT[:, :], ALU.max, ["STs0", "SINKT"], ["STs0"])
                k.tt(TSs[:, :, :], TSs[:, :, :], bc(ST[0:16, 0:8].unsqueeze(2), [16, 8, 132]), ALU.subtract,
                     ["TSs", "STs0"], ["TSs"])
                k.act(TSs[:, :, :], TSs[:, :, :], AF.Exp, ["TSs"], ["TSs"])
                k.red(ST[0:16, 8:16], TSs[:, :, :], ALU.add, ["TSs"], ["STs1"])
                k.tt(ST[0:16, 0:8], SINKT[:, :], ST[0:16, 0:8], ALU.subtract,
                     ["STs0", "SINKT"], ["STs0"])
                k.act(ST[0:16, 0:8], ST[0:16, 0:8], AF.Exp, ["STs0"], ["STs0"])
                k.tt(ST[0:16, 8:16], ST[0:16, 8:16], ST[0:16, 0:8], ALU.add, ["STs0", "STs1"], ["STs1"])
                k.recip(ST[0:16, 8:16], ST[0:16, 8:16], ["STs1"], ["STs1"])
                k.tt(PNs[:, :, :], TSs[:, :, :], bc(ST[0:16, 8:16].unsqueeze(2), [16, 8, 132]), ALU.mult,
                     ["TSs", "STs1"], ["PNs"])
                k.cp(PNF[:, :, 4 * s:4 * s + 4], PNs[:, :, 128:132], ["PNs"], ["PNF"])
                for g in range(8):
                    k.tr(psTb[:, 0, g * 16:(g + 1) * 16], PNs[:, g, 0:128], IDB[0:16, 0:16], ["PNs", "IDB"], ["psTb"])
                    k.tr(psTb[0:64, 1, g * 16:(g + 1) * 16], PNF[:, g, :], IDB[0:16, 0:16], ["PNF", "IDB"], ["psTb"])
                k.cp(PNTc[:, :, :], psTb[:, 0, :].rearrange("p (g t) -> p g t", g=8), ["psTb"], ["PNTc"])
                k.cp(PNTn[:, :, :], psTb[0:64, 1, :].rearrange("p (g t) -> p g t", g=8), ["psTb"], ["PNTn"], eng="act")
                k.memset(PNF[:, :, 4 * s:4 * s + 4], 0.0, ["PNF"])
                Op = psB7[:, 256:320].rearrange("p (a t) -> p a t", a=16)
                for pp in range(16):
                    g = pp // 2
                    for hh in range(2):
                        hl = (pp % 2) * 2 + hh
                        hs = slice(hh * 64, hh * 64 + 64)
                        k.mm(Op[hs, pp, :], CVb[:, g, :], PNTc[:, g, hl * 4:(hl + 1) * 4], True, False,
                             ["CVb", "PNTc"], ["psB7"])
                        k.mm(Op[hs, pp, :], VNTb[:, g, :], PNTn[:, g, hl * 4:(hl + 1) * 4], False, True,
                             ["VNTb", "PNTn"], ["psB7"])
                k.cp(YBS[:, :, 4 * s:4 * s + 4], Op, ["psB7"], ["YBS"])
                k.dma("sp", kwin_s[s, 124:128, :], KNT[4 * s:4 * s + 4, :], ["KNT"], ["o_kws"], is_out=True)
                k.dma("sp", vwin_s[s, 124:128, :], VNT[4 * s:4 * s + 4, :], ["VNT"], ["o_vws"], is_out=True)
            k.tt(YBS[:, :, :], YBS[:, :, :], SGBS[:, :, :], ALU.mult, ["YBS", "SGBS"], ["YBS"])
            k.tt(MIX[:, :, 0:64], MIX[:, :, 0:64], YBS[:, :, :], ALU.add, ["MIX", "YBS"], ["MIX"])
            for c8 in range(6):
                for j in range(8):
                    k.tr(psTf[0:16, (j % 4) * 128:(j % 4 + 1) * 128], SHS[:, c8 * 8 + j, :], IDF[:, :],
                         ["SHS", "IDF"], ["psTf"])
                    if j % 4 == 3:
                        k.cp(SSR[:, (j - 3) * 128:(j + 1) * 128], psTf[0:16, 0:512], ["psTf"], ["SSR"])
                k.dma("sp", shift_s[:, c8 * 1024:(c8 + 1) * 1024], SSR[:, :], ["SSR"], ["o_shs"], is_out=True)
            for j, (o, n, blk) in enumerate(((0, 96, 48), (96, 96, 49), (192, 128, 50), (320, 128, 51))):
                k.tr(psTf[0:16, j * 128:j * 128 + n], SHS[0:n, blk, :], IDF[0:n, 0:n], ["SHS", "IDF"], ["psTf"])
                k.cp(SSR[:, o:o + n], psTf[0:16, j * 128:j * 128 + n], ["psTf"], ["SSR"])
            k.dma("sp", shift_s[:, 6144:6592], SSR[:, 0:448], ["SSR"], ["o_shs"], is_out=True)

        if last_main and not (_DBG & 2):
            for blk in range(52):
                if blk < 48:
                    c0, nr = blk * 128, 128
                else:
                    c0, nr = ((6144, 96), (6240, 96), (6336, 128), (6464, 128))[blk - 48]
                k.dma("sp", shift_p[c0:c0 + nr, :], SHIFT[0:nr, blk:blk + 1], ["SHIFT"], ["o_shp"], is_out=True)

        wctr = [0]

        def load_w3(src3, nslots_key):
            s = wctr[0] % 3
            wctr[0] += 1
            keys = ["wa%d" % (2 * s), "wa%d" % (2 * s + 1)]
            return s, keys

        for cc in range(8):
            s, keys = load_w3(None, None)
            v = WA[:, s * 4096:(s + 1) * 4096].rearrange("p (a b) -> p a b", a=16)
            k.dma("pool", v, w_out[:, cc * 256:(cc + 1) * 256].rearrange("(kc p) c -> p kc c", p=128), [], keys)
            for ti in range(ntile):
                b = mmctr[0] % 2
                mmctr[0] += 1
                pk = "psA%d" % b
                for kc in range(16):
                    k.mm(psA[b][0:rows, 0:256], MIX[:, kc, ti * 128:ti * 128 + rows], v[:, kc, :], kc == 0, kc == 15,
                         keys + ["MIX"], [pk])
                hk = "H%d" % ti
                k.tt(H[0:rows, ti, cc * 256:(cc + 1) * 256], H[0:rows, ti, cc * 256:(cc + 1) * 256],
                     psA[b][0:rows, 0:256], ALU.add, [pk, hk], [hk])
        k.dma("sp", NW[:], nmlp.partition_broadcast(128), [], ["NW"])
        for ti in range(ntile):
            hk = "H%d" % ti
            k.act(XNT[0:rows, :], H[0:rows, ti, :], AF.Square, [hk], ["XNT", "ST0"], accum=ST[0:rows, 0:1])
            k.act(ST[0:rows, 1:2], ST[0:rows, 0:1], AF.Sqrt, ["ST0", "EPSC"], ["ST1"], bias=EPSC[0:rows, 0:1],
                  scale=1.0 / D)
            k.recip(ST[0:rows, 1:2], ST[0:rows, 1:2], ["ST1"], ["ST1"])
            k.stt(XNT[0:rows, :], H[0:rows, ti, :], ST[0:rows, 1:2], NW[0:rows, :], ALU.mult, ALU.mult,
                  [hk, "ST1", "NW"], ["XNT"])
            for half in range(2):
                for j in range(8):
                    kc = half * 8 + j
                    k.tr(psTb[:, j, 0:rows], XNT[0:rows, kc * 128:(kc + 1) * 128], IDB[0:rows, 0:rows],
                         ["XNT", "IDB"], ["psTb"])
                k.cp(XN[:, half * 8:(half + 1) * 8, ti * 128:ti * 128 + rows], psTb[:, :, 0:rows],
                     ["psTb"], ["XN"], eng=("act" if half else "dve"))
        UT = [Rt, Kt]
        UTb = [(KT, "KT"), (BT, "BT"), (VB, "VB"), (SQ, "SQ")]
        for sb_ in range(32):
            s, ukeys = load_w3(None, None)
            wu = WA[:, s * 4096:(s + 1) * 4096].rearrange("p (a b) -> p a b", a=16)
            k.dma("pool", wu, w_up[:, sb_ * 256:(sb_ + 1) * 256].rearrange("(kc p) c -> p kc c", p=128), [], ukeys)
            s2, dkeys = load_w3(None, None)
            wd = WA[:, s2 * 4096:(s2 + 1) * 4096].rearrange("p (a b) -> p a b", a=2)
            k.dma("pool", wd, w_down[sb_ * 256:(sb_ + 1) * 256, :].rearrange("(fb p) c -> p fb c", p=128), [], dkeys)
            uts = []
            for fb in range(2):
                b = mmctr[0] % 2
                mmctr[0] += 1
                pk = "psA%d" % b
                for kc in range(16):
                    k.mm(psA[b][:, 0:T], wu[:, kc, fb * 128:(fb + 1) * 128], XN[:, kc, 0:T], kc == 0, kc == 15,
                         ukeys + ["XN"], [pk])
                ut, utk = UTb[(sb_ % 2) * 2 + fb]
                k.act(TMP1[:, 0:T], psA[b][:, 0:T], AF.Relu, [pk], ["TMP1"])
                k.tt(ut[:, 0:T], TMP1[:, 0:T], TMP1[:, 0:T], ALU.mult, ["TMP1"], [utk])
                uts.append((ut, utk))
            for ti in range(ntile):
                hk = "H%d" % ti
                for cc in range(4):
                    b = mmctr[0] % 2
                    mmctr[0] += 1
                    pk = "psA%d" % b
                    for fb in range(2):
                        ut, utk = uts[fb]
                        k.mm(psA[b][0:rows, 0:512], ut[:, ti * 128:ti * 128 + rows], wd[:, fb, cc * 512:(cc + 1) * 512],
                             fb == 0, fb == 1, dkeys + [utk], [pk])
                    k.tt(H[0:rows, ti, cc * 512:(cc + 1) * 512], H[0:rows, ti, cc * 512:(cc + 1) * 512],
                         psA[b][0:rows, 0:512], ALU.add, [pk, hk], [hk])
        k.dma("sp", NW[:], nfin.partition_broadcast(128), [], ["NW"])
        ydst = y_smp if smp else y_main
        for ti in range(ntile):
            hk = "H%d" % ti
            k.act(XNT[0:rows, :], H[0:rows, ti, :], AF.Square, [hk], ["XNT", "ST0"], accum=ST[0:rows, 0:1])
            k.act(ST[0:rows, 1:2], ST[0:rows, 0:1], AF.Sqrt, ["ST0", "EPSC"], ["ST1"], bias=EPSC[0:rows, 0:1],
                  scale=1.0 / D)
            k.recip(ST[0:rows, 1:2], ST[0:rows, 1:2], ["ST1"], ["ST1"])
            k.stt(H[0:rows, ti, :], H[0:rows, ti, :], ST[0:rows, 1:2], NW[0:rows, :], ALU.mult, ALU.mult,
                  [hk, "ST1", "NW"], [hk])
            r0 = (0 if smp else pidx * 512) + ti * 128
            k.dma("sp", ydst[r0:r0 + rows, :], H[0:rows, ti, :], [hk], ["o_y"], is_out=True)

    def fz(e):
        return e.memset(DUM[:, 0:1], 0.0)

    if passes is not None:
        for i_, nm in enumerate(passes.split(",")):
            if i_ > 0:
                S.fence(fz)
            if nm == "p0":
                run_pass("pre", xpre[0:512, :], 512, 1, 512, 64, 0)
            elif nm == "p1":
                run_pass("pre", xpre[512:1024, :], 512, 1, 512, 64, 1, last_pre=True)
            elif nm == "m0":
                run_pass("main", xmain[0:512, :], 512, 1, 512, 64, 0, first_main=True)
            elif nm == "m1":
                run_pass("main", xmain[512:1024, :], 512, 1, 512, 64, 1, last_main=True)
            elif nm == "m1x":
                run_pass("main", xmain[512:1024, :], 512, 1, 512, 64, 1)
            elif nm == "s":
                run_pass("smp", xsmp, 64, 16, 4, 4, 0)
        S.emit()
        k.st.close()
        return nc
    run_pass("pre", xpre[0:512, :], 512, 1, 512, 64, 0)
    if npass > 1:
        S.fence(fz)
        run_pass("pre", xpre[512:1024, :], 512, 1, 512, 64, 1, last_pre=True)
    if npass > 2:
        S.fence(fz)
        run_pass("main", xmain[0:512, :], 512, 1, 512, 64, 0, first_main=True)
    if npass > 3:
        S.fence(fz)
        run_pass("main", xmain[512:1024, :], 512, 1, 512, 64, 1, last_main=True)
    if npass > 4:
        S.fence(fz)
        run_pass("smp", xsmp, 64, 16, 4, 4, 0)
    S.emit()
    k.st.close()
    return nc


def _consts(half):
    c = {}
    c["ident"] = np.eye(128, dtype=np.float32)
    m64 = np.zeros((64, 3, 64), np.float32)
    m64[:, 0, :] = np.triu(np.ones((64, 64), np.float32), 1)
    m64[:, 1, :] = np.triu(np.ones((64, 64), np.float32), 0)
    m64[:, 2, :] = np.tril(np.ones((64, 64), np.float32), -1)
    c["mask64"] = m64
    c["mask4"] = np.ascontiguousarray(m64[0:4, :, 0:4])
    bo = np.zeros((128, 128), np.float32)
    bo[0:64, 0:64] = 1.0
    bo[64:128, 64:128] = 1.0
    c["bones"] = bo
    i = np.arange(128)[:, None]
    kj = np.arange(256)[None, :]
    dist = i - kj + 128
    dm = np.where((dist >= 0) & (dist <= 128), -dist.astype(np.float32), np.float32(NEG)).astype(np.float32)
    c["dm"] = dm
    dm0 = dm.copy()
    if half == 0:
        dm0[:, 0:128] = NEG
    c["dm0"] = dm0
    bc_ = np.zeros((16, 8, 128), np.float32)
    bn_ = np.zeros((16, 8, 4), np.float32)
    for hl in range(4):
        for t in range(4):
            r = hl * 4 + t
            for g in range(8):
                sl = SLOPES[4 * g + hl]
                cc = np.arange(128)
                bc_[r, g, :] = np.where(cc >= t, -sl * (t + 128 - cc), NEG)
                tp = np.arange(4)
                bn_[r, g, :] = np.where(tp <= t, -sl * (t - tp), NEG)
    c["biasc"] = bc_
    c["biasn"] = bn_
    return c


_NC_CACHE = {}


def make_in_maps(inp):
    f = lambda a: np.ascontiguousarray(np.asarray(a, dtype=np.float32))
    x_prompt = f(inp["x_prompt"])
    x_sample = f(inp["x_sample"])
    state_shift = f(inp["state_shift"])[0]
    state_wkv = f(inp["state_wkv"])[0]
    cache_k = f(inp["cache_k_win"])[0]
    cache_v = f(inp["cache_v_win"])[0]
    mu = f(inp["tshift_mu"])[0]
    cp = np.zeros((128, NCP), np.float32)
    cp[:, 0:48] = mu[0:6144].reshape(48, 128).T
    cp[0:96, 48] = mu[6144:6240]
    cp[0:96, 49] = mu[6240:6336]
    cp[:, 50] = mu[6336:6464]
    cp[:, 51] = mu[6464:6592]
    for j, name in enumerate(("w0", "a0", "k_k", "k_a", "r_k", "ln_x_w", "ln_x_b")):
        cp[:, 52 + 16 * j:68 + 16 * j] = f(inp[name])[0].reshape(2048).reshape(16, 128).T
    sinks = f(inp["attn_sinks"])[0]
    sinkb = np.ascontiguousarray(np.tile(sinks[None, :], (128, 1)))
    sinkt = np.zeros((16, 8), np.float32)
    for hl in range(4):
        for t in range(4):
            sinkt[hl * 4 + t, :] = sinks[hl::4]
    shared = dict(
        w_in=f(inp["w_in"])[0], w_out=f(inp["w_out"])[0], w_up=f(inp["w_up"])[0], w_down=f(inp["w_down"])[0],
        w_lora=f(inp["w_lora"])[0], a_lora=f(inp["a_lora"])[0], g_lora=f(inp["g_lora"])[0],
        nmix=f(inp["norm_mix_w"])[0], nmlp=f(inp["norm_mlp_w"])[0], nfin=f(inp["norm_final_w"]),
        cp=cp, sinkb=sinkb, sinkt=sinkt)
    in_maps = []
    for c in range(NCORES):
        b, half = c // 2, c % 2
        m = dict(shared)
        m.update(_consts(half))
        m["xpre"] = x_prompt[b, 0:1024] if half == 1 else np.zeros((1024, D), np.float32)
        m["xmain"] = np.ascontiguousarray(x_prompt[b, half * 1024:(half + 1) * 1024])
        m["xsmp"] = np.ascontiguousarray(x_sample[16 * c:16 * c + 16].reshape(64, D))
        m["sshift"] = np.ascontiguousarray(state_shift[16 * c:16 * c + 16])
        m["swkv"] = np.ascontiguousarray(state_wkv[16 * c:16 * c + 16].reshape(16, 16, 128, 64))
        m["ck"] = np.ascontiguousarray(cache_k[16 * c:16 * c + 16].reshape(16, 128, 512))
        m["cv"] = np.ascontiguousarray(cache_v[16 * c:16 * c + 16].reshape(16, 128, 512))
        in_maps.append(m)
    return in_maps


def kernel(**inp):
    in_maps = make_in_maps(inp)
    if "nc" not in _NC_CACHE:
        _NC_CACHE["nc"] = build_program()
    nc = _NC_CACHE["nc"]
    res = run_bass_kernel_spmd(nc, in_maps, core_ids=list(range(NCORES))).results
    y_prompt = np.zeros((4, 2048, D), np.float32)
    y_sample = np.zeros((128, 4, D), np.float32)
    shift_p = np.zeros((1, 4, RC), np.float32)
    wkv_p = np.zeros((1, 4, 32, 64, 64), np.float32)
    kw_p = np.zeros((1, 4, 128, 8, 64), np.float32)
    vw_p = np.zeros((1, 4, 128, 8, 64), np.float32)
    shift_s = np.zeros((1, 128, RC), np.float32)
    wkv_s = np.zeros((1, 128, 32, 64, 64), np.float32)
    kw_s = np.zeros((1, 128, 128, 8, 64), np.float32)
    vw_s = np.zeros((1, 128, 128, 8, 64), np.float32)
    for c in range(NCORES):
        b, half = c // 2, c % 2
        r = res[c]
        y_prompt[b, half * 1024:(half + 1) * 1024] = r["y_main"]
        y_sample[16 * c:16 * c + 16] = r["y_smp"].reshape(16, 4, D)
        shift_s[0, 16 * c:16 * c + 16] = r["shift_s"]
        wkv_s[0, 16 * c:16 * c + 16] = r["wkv_s"].reshape(16, 32, 64, 64)
        kw_s[0, 16 * c:16 * c + 16] = r["kwin_s"].reshape(16, 128, 8, 64)
        vw_s[0, 16 * c:16 * c + 16] = r["vwin_s"].reshape(16, 128, 8, 64)
        if half == 1:
            shift_p[0, b] = r["shift_p"].reshape(RC)
            wkv_p[0, b] = r["wkv_p"].reshape(32, 64, 64)
            kw_p[0, b] = r["kwin_p"].reshape(128, 8, 64)
            vw_p[0, b] = r["vwin_p"].reshape(128, 8, 64)
    return (y_prompt, y_sample, shift_p, wkv_p, kw_p, vw_p, shift_s, wkv_s, kw_s, vw_s)
```

```python
import contextlib
import math
import os
_DBG = int(os.environ.get('K_DBG', '0'))
import numpy as np
import concourse.bass as bass
import concourse.mybir as mybir
from concourse.bass_utils import run_bass_kernel_spmd

F32 = mybir.dt.float32
BF16 = mybir.dt.bfloat16
AF = mybir.ActivationFunctionType
ALU = mybir.AluOpType
AX = mybir.AxisListType

D = 2048
RC = 6592
CIN = 13760
DFF = 8192
NCORES = 8
Q0 = RC
KA0 = RC + 2048
VA0 = KA0 + 512
GA0 = VA0 + 512
GB0 = GA0 + 2048
RMS_EPS = 1e-5
GN_EPS = 64e-5
NEG = -1.0e9
SLOPES = [2.0 ** (-8.0 * (h + 1.0) / 32.0) for h in range(32)]
NCP = 164


class _Op:
    __slots__ = ("eng", "fn", "dma", "deps", "sig", "semi", "val")

    def __init__(self, eng, fn, dma):
        self.eng = eng
        self.fn = fn
        self.dma = dma
        self.deps = []
        self.sig = False
        self.semi = None
        self.val = 0


class Sched:
    ENGS = ("pe", "act", "dve", "pool", "sp")
    NDMA = 8

    def __init__(self, nc):
        self.nc = nc
        self.ops = {e: [] for e in self.ENGS}
        self.last_w = {}
        self.rd_c = {}
        self.rd_d = {}
        self.dma_hist = {e: [] for e in self.ENGS}
        self.out_dmas = []
        self.nops = 0

    def op(self, eng, fn, reads=(), writes=(), dma=False, is_out=False):
        o = _Op(eng, fn, dma)
        self.nops += 1
        deps = []
        for k in reads:
            w = self.last_w.get(k)
            if w is not None:
                deps.append(w)
        for k in writes:
            w = self.last_w.get(k)
            if w is not None:
                deps.append(w)
            c = self.rd_c.get(k)
            if c:
                deps.extend(c.values())
            dd = self.rd_d.get(k)
            if dd:
                deps.extend(dd)
        if dma:
            h = self.dma_hist[eng]
            if len(h) >= self.NDMA:
                deps.append(h[-self.NDMA])
            h.append(o)
        seen = set()
        for d in deps:
            if id(d) in seen or d is o:
                continue
            seen.add(id(d))
            if d.eng != eng or d.dma or eng != "pe":
                o.deps.append(d)
                d.sig = True
        for k in reads:
            if dma:
                self.rd_d.setdefault(k, []).append(o)
            else:
                self.rd_c.setdefault(k, {})[eng] = o
        for k in writes:
            self.last_w[k] = o
            self.rd_c[k] = {}
            self.rd_d[k] = []
        self.ops[eng].append(o)
        if is_out:
            self.out_dmas.append(o)
        return o

    def fence(self, fn):
        keys = set(self.last_w.keys()) | set(self.rd_c.keys()) | set(self.rd_d.keys())
        keys = sorted(keys)
        self.op("dve", fn, reads=keys, writes=keys)

    def emit(self):
        nc = self.nc
        fin = _Op("sp", None, False)
        for d in self.out_dmas:
            fin.deps.append(d)
            d.sig = True
        self.ops["sp"].append(fin)
        sem_names = ["c_" + e for e in self.ENGS]
        for e in self.ENGS:
            for i in range(self.NDMA):
                sem_names.append("d_%s_%d" % (e, i))
        with contextlib.ExitStack() as st:
            sems = {n: st.enter_context(nc.semaphore(n)) for n in sem_names}
            for e in self.ENGS:
                cnt = 0
                dcnt = [0] * self.NDMA
                di = 0
                for o in self.ops[e]:
                    if o.dma:
                        s = di % self.NDMA
                        di += 1
                        dcnt[s] += 16
                        o.semi = "d_%s_%d" % (e, s)
                        o.val = dcnt[s]
                        o.sig = True
                    elif o.sig:
                        cnt += 1
                        o.semi = "c_" + e
                        o.val = cnt
            block = st.enter_context(nc.Block())
            engmap = {"pe": block.tensor, "act": block.scalar, "dve": block.vector,
                      "pool": block.gpsimd, "sp": block.sync}

            def make(e):
                def body(eng):
                    waited = {}
                    for o in self.ops[e]:
                        for d in o.deps:
                            if waited.get(d.semi, 0) >= d.val:
                                continue
                            waited[d.semi] = d.val
                            eng.wait_ge(sems[d.semi], d.val)
                        if o.fn is None:
                            continue
                        ins = o.fn(eng)
                        if o.sig:
                            ins.then_inc(sems[o.semi], 16 if o.dma else 1)
                return body

            for e in self.ENGS:
                if self.ops[e]:
                    engmap[e](make(e))


class KB:
    def __init__(self):
        self.nc = bass.Bass("TRN2", target_bir_lowering=False)
        self.S = Sched(self.nc)
        self.st = contextlib.ExitStack()

    def din(self, name, shape):
        return self.nc.dram_tensor(name, list(shape), F32, kind="ExternalInput").ap()

    def dout(self, name, shape):
        return self.nc.dram_tensor(name, list(shape), F32, kind="ExternalOutput").ap()

    def sb(self, name, shape, dt=F32):
        return self.st.enter_context(self.nc.sbuf_tensor(name, list(shape), dt))

    def ps(self, name, shape, dt=F32):
        return self.st.enter_context(self.nc.psum_tensor(name, list(shape), dt))

    def mm(self, out, lhsT, rhs, start, stop, r, w):
        self.S.op("pe", lambda e: e.matmul(out, lhsT=lhsT, rhs=rhs, start=start, stop=stop), r, w)

    def tr(self, out, in_, ident, r, w):
        self.S.op("pe", lambda e: e.transpose(out, in_, ident), r, w)

    def act(self, out, in_, func, r, w, bias=0.0, scale=1.0, accum=None):
        if accum is None:
            self.S.op("act", lambda e: e.activation(out=out, in_=in_, func=func, bias=bias, scale=scale), r, w)
        else:
            self.S.op("act", lambda e: e.activation(out=out, in_=in_, func=func, bias=bias, scale=scale,
                                                    accum_out=accum), r, w)

    def tt(self, out, in0, in1, op, r, w, eng="dve"):
        self.S.op(eng, lambda e: e.tensor_tensor(out=out, in0=in0, in1=in1, op=op), r, w)

    def ts(self, out, in0, s1, s2, op0, op1, r, w, eng="dve"):
        if s2 is None:
            self.S.op(eng, lambda e: e.tensor_single_scalar(out=out, in_=in0, scalar=s1, op=op0), r, w)
        else:
            self.S.op(eng, lambda e: e.tensor_scalar(out=out, in0=in0, scalar1=s1, scalar2=s2, op0=op0, op1=op1), r, w)

    def stt(self, out, in0, scalar, in1, op0, op1, r, w):
        self.S.op("dve", lambda e: e.scalar_tensor_tensor(out=out, in0=in0, scalar=scalar, in1=in1,
                                                          op0=op0, op1=op1), r, w)

    def cp(self, out, in_, r, w, eng="dve"):
        if eng == "act" and not (_DBG & 8):
            eng = "dve"
        if eng == "act":
            self.S.op("act", lambda e: e.activation(out=out, in_=in_, func=AF.Copy), r, w)
        else:
            self.S.op(eng, lambda e: e.tensor_copy(out=out, in_=in_), r, w)

    def red(self, out, in_, op, r, w):
        self.S.op("dve", lambda e: e.tensor_reduce(out=out, in_=in_, axis=AX.X, op=op), r, w)

    def recip(self, out, in_, r, w):
        self.S.op("dve", lambda e: e.reciprocal(out=out, in_=in_), r, w)

    def memset(self, ap, val, w, eng="dve"):
        self.S.op(eng, lambda e: e.memset(ap, val), (), w)

    def dma(self, q, out, in_, r, w, is_out=False):
        self.S.op(q, lambda e: e.dma_start(out=out, in_=in_), r, w, dma=True, is_out=is_out)


def bc(ap, shape):
    return ap.broadcast_to(list(shape))


def build_program(npass=5, stop_phase=99, passes=None):
    k = KB()
    nc = k.nc
    xpre = k.din("xpre", [1024, D])
    xmain = k.din("xmain", [1024, D])
    xsmp = k.din("xsmp", [64, D])
    sshift = k.din("sshift", [16, RC])
    swkv = k.din("swkv", [16, 16, 128, 64])
    ckd = k.din("ck", [16, 128, 512])
    cvd = k.din("cv", [16, 128, 512])
    w_in = k.din("w_in", [D, CIN])
    w_out = k.din("w_out", [D, D])
    w_up = k.din("w_up", [D, DFF])
    w_down = k.din("w_down", [DFF, D])
    w_lora = k.din("w_lora", [96, D])
    a_lora = k.din("a_lora", [96, D])
    g_lora = k.din("g_lora", [256, D])
    nmix = k.din("nmix", [D])
    nmlp = k.din("nmlp", [D])
    nfin = k.din("nfin", [D])
    cpd = k.din("cp", [128, NCP])
    sinkbd = k.din("sinkb", [128, 32])
    sinktd = k.din("sinkt", [16, 8])
    identd = k.din("ident", [128, 128])
    mask64d = k.din("mask64", [64, 3, 64])
    mask4d = k.din("mask4", [4, 3, 4])
    bonesd = k.din("bones", [128, 128])
    dmd = k.din("dm", [128, 256])
    dm0d = k.din("dm0", [128, 256])
    biascd = k.din("biasc", [16, 8, 128])
    biasnd = k.din("biasn", [16, 8, 4])

    y_main = k.dout("y_main", [1024, D])
    y_smp = k.dout("y_smp", [64, D])
    shift_p = k.dout("shift_p", [RC, 1])
    wkv_p = k.dout("wkv_p", [16, 128, 64])
    kwin_p = k.dout("kwin_p", [128, 512])
    vwin_p = k.dout("vwin_p", [128, 512])
    shift_s = k.dout("shift_s", [16, RC])
    wkv_s = k.dout("wkv_s", [16, 16, 128, 64])
    kwin_s = k.dout("kwin_s", [16, 128, 512])
    vwin_s = k.dout("vwin_s", [16, 128, 512])

    XN = k.sb("XN", [128, 16, 512], BF16)
    MIX = k.sb("MIX", [128, 16, 512], BF16)
    H = k.sb("H", [128, 4, D], F32)
    WA = k.sb("WA", [128, 12288], BF16)
    NW = k.sb("NW", [128, D], F32)
    XNT = k.sb("XNT", [128, D], BF16)
    PEXT = k.sb("PEXT", [128, 520], F32)
    Rt = k.sb("Rt", [128, 512], F32)
    Kt = k.sb("Kt", [128, 512], F32)
    Vt = k.sb("Vt", [128, 512], F32)
    LW = k.sb("LW", [128, 512], F32)
    ASIG = k.sb("ASIG", [128, 512], F32)
    KK = k.sb("KK", [128, 512], F32)
    TMP1 = k.sb("TMP1", [128, 512], F32)
    TMP2 = k.sb("TMP2", [128, 512], F32)
    CUM = k.sb("CUM", [128, 512], F32)
    E1 = k.sb("E1", [128, 512], F32)
    Yt = k.sb("Yt", [128, 512], F32)
    ONES = k.sb("ONES", [128, 512], F32)
    MIXA = PEXT[:, 0:512]
    Gb = k.sb("Gb", [128, 512], BF16)
    SGA = k.sb("SGA", [128, 512], BF16)
    SGB = k.sb("SGB", [128, 512], BF16)
    TW = k.sb("TW", [128, 512], BF16)
    DAb = k.sb("DAb", [128, 512], BF16)
    SG = k.sb("SG", [128, 2, 512], BF16)
    AR = k.sb("AR", [128, 1024], BF16)
    KT = k.sb("KT", [128, 512], BF16)
    BT = k.sb("BT", [128, 512], BF16)
    VB = k.sb("VB", [128, 512], BF16)
    SQ = k.sb("SQ", [128, 512], BF16)
    WC = k.sb("WC", [128, 16], F32)
    BASE = k.sb("BASE", [128, 16], F32)
    TM3 = [k.sb("TM3_%d" % i, [64, 2, 3, 128], BF16) for i in range(2)]
    ABRM = [k.sb("ABRM%d" % i, [64, 4, 128], BF16) for i in range(2)]
    AKRM = [k.sb("AKRM%d" % i, [64, 4, 128], BF16) for i in range(2)]
    PTC = [[k.sb("PTC%d_%d" % (i, j), [64, 4, 2, 64], BF16) for j in range(2)] for i in range(2)]
    PT = [[k.sb("PT%d_%d" % (i, j), [64, 4, 64], BF16) for j in range(2)] for i in range(2)]
    ZB = k.sb("ZB", [64, 2, 64], BF16)
    UB = k.sb("UB", [64, 2, 64], BF16)
    SST = k.sb("SST", [128, 16, 64], F32)
    SBF = k.sb("SBF", [128, 16, 2, 64], BF16)
    KTP = k.sb("KTP", [128, 2, 512], BF16)
    BTP = k.sb("BTP", [128, 2, 512], BF16)
    SHIFT = k.sb("SHIFT", [128, 52], F32)
    CP = k.sb("CP", [128, NCP], F32)
    OMU = k.sb("OMU", [128, 52], F32)
    SINKB = k.sb("SINKB", [128, 32], F32)
    SINKT = k.sb("SINKT", [16, 8], F32)
    IDF = k.sb("IDF", [128, 128], F32)
    IDB = k.sb("IDB", [128, 128], BF16)
    BONES = k.sb("BONES", [128, 128], BF16)
    BONESF = k.sb("BONESF", [128, 128], F32)
    M64 = k.sb("M64", [64, 3, 64], F32)
    M4 = k.sb("M4", [4, 3, 4], F32)
    DM = k.sb("DM", [128, 256], F32)
    DM0 = k.sb("DM0", [128, 256], F32)
    EPSC = k.sb("EPSC", [128, 2], F32)
    LWT = k.sb("LWT", [128, 2, 4, 128], BF16)
    ST = k.sb("ST", [128, 16], F32)
    DUM = k.sb("DUM", [128, 4], F32)
    PA16 = k.sb("PA16", [128, 11264], BF16)
    TS = k.sb("TS", [128, 256], F32)
    BH = k.sb("BH", [128, 2, 256], F32)
    BH0 = k.sb("BH0", [128, 2, 256], F32)
    PB = k.sb("PB", [128, 256], BF16)
    PN = k.sb("PN", [128, 256], BF16)
    PNT = k.sb("PNT", [128, 256], BF16)
    KWO = CUM
    VWO = E1
    SWI = k.sb("SWI", [128, 128], F32)
    SOB = k.sb("SOB", [128, 128], F32)
    SOUT = [k.sb("SOUT%d" % i, [128, 64], F32) for i in range(2)]
    SS = [k.sb("SS%d" % i, [128, 64], F32) for i in range(2)]
    SSB = [k.sb("SSB%d" % i, [128, 2, 64], BF16) for i in range(2)]
    BIASC = k.sb("BIASC", [16, 8, 128], F32)
    BIASN = k.sb("BIASN", [16, 8, 4], F32)

    KC = PA16[:, 0:5120].rearrange("p (g t) -> p g t", g=8)
    VT = PA16[:, 5120:7680].rearrange("p (a c) -> p a c", a=5)
    QP = PA16[:, 7680:8704].rearrange("p (h t) -> p h t", h=2)
    QT = PA16[0:64, 0:2048]
    QC = PA16[0:64, 2048:4096].rearrange("p (s h t) -> p s h t", s=16, h=32)
    VNTb = PA16[0:64, 4096:4608].rearrange("p (g d) -> p g d", g=8)
    KNC = PA16[0:64, 4608:5120].rearrange("p (g t) -> p g t", g=8)
    CKb = PA16[:, 5120:5632].rearrange("p (g d) -> p g d", g=8)
    CVb = PA16[:, 5632:6144].rearrange("p (g d) -> p g d", g=8)
    CKC = PA16[0:64, 6144:7168].rearrange("p (g t) -> p g t", g=8)
    PNs = PA16[0:16, 7168:8224].rearrange("p (g t) -> p g t", g=8)
    PNF = PA16[0:16, 8224:8736].rearrange("p (g t) -> p g t", g=8)
    PNTc = PA16[:, 8736:8864].rearrange("p (g t) -> p g t", g=8)
    PNTn = PA16[0:64, 8864:8992].rearrange("p (g t) -> p g t", g=8)
    SGBS = PA16[:, 8992:10016].rearrange("p (a t) -> p a t", a=16)
    HA = H[:, 1:4, :].rearrange("p a b -> p (a b)")
    KNT = HA[0:64, 0:512]
    VNT = HA[0:64, 512:1024]
    CKf = HA[:, 1024:1536]
    CVf = HA[:, 1536:2048]
    TSs = HA[0:16, 2048:3104].rearrange("p (g t) -> p g t", g=8)
    YBS = HA[:, 3104:4128].rearrange("p (a t) -> p a t", a=16)
    SHS = HA[:, 4128:4960].rearrange("p (b s) -> p b s", b=52)
    SSR = HA[0:16, 4960:5984]

    psA = [k.ps("psA%d" % i, [128, 512]) for i in range(2)]
    psTb = k.ps("psTb", [128, 8, 128], BF16)
    psTf = k.ps("psTf", [128, 512])
    psB4 = k.ps("psB4", [128, 512])
    psB5 = k.ps("psB5", [128, 512])
    psB6 = k.ps("psB6", [128, 512])
    psB7 = k.ps("psB7", [128, 512])

    S = k.S
    mmctr = [0]

    k.dma("sp", CP[:], cpd, [], ["CP"])
    k.dma("sp", SINKB[:], sinkbd, [], ["SINKB"])
    k.dma("sp", SINKT[:], sinktd, [], ["SINKT"])
    k.dma("sp", IDF[:], identd, [], ["IDF"])
    k.dma("sp", BONESF[:], bonesd, [], ["BONESF"])
    k.dma("sp", M64[:], mask64d, [], ["M64"])
    k.dma("sp", M4[:], mask4d, [], ["M4"])
    k.dma("sp", DM[:], dmd, [], ["DM"])
    k.dma("sp", DM0[:], dm0d, [], ["DM0"])
    k.dma("sp", BIASC[:], biascd, [], ["BIASC"])
    k.dma("sp", BIASN[:], biasnd, [], ["BIASN"])
    k.cp(IDB[:], IDF[:], ["IDF"], ["IDB"])
    k.cp(BONES[:], BONESF[:], ["BONESF"], ["BONES"])
    k.ts(OMU[:], CP[:, 0:52], -1.0, 1.0, ALU.mult, ALU.add, ["CP"], ["OMU"])
    k.memset(ONES[:], 1.0, ["ONES"])
    k.memset(EPSC[:, 0:1], RMS_EPS, ["EPSC"])
    k.memset(EPSC[:, 1:2], GN_EPS, ["EPSC"])
    k.memset(SST[:], 0.0, ["SST"])
    k.memset(SBF[:], 0.0, ["SBF"])
    k.memset(SHIFT[:], 0.0, ["SHIFT"])
    k.memset(BASE[:], 0.0, ["BASE"])
    k.memset(PA16[:], 0.0, ["KC", "VT", "QP"])
    k.memset(SWI[:], 0.0, ["SWI"])
    k.memset(SOB[:], 0.0, ["SOB"])
    k.memset(DUM[:], 0.0, ["DUM"])
    k.memset(AR[:], 0.0, ["AR"])
    k.memset(KTP[:], 0.0, ["KTP"])
    k.memset(BTP[:], 0.0, ["BTP"])
    for i_ in range(2):
        k.memset(SSB[i_][:], 0.0, ["SS%db" % i_])

    MU = CP[:, 0:52]
    W0C = CP[:, 52:68]
    A0C = CP[:, 68:84]
    KKC = CP[:, 84:100]
    KAC = CP[:, 100:116]
    RKC = CP[:, 116:132]
    LNW = CP[:, 132:148]
    LNB = CP[:, 148:164]

    wslot = [0]

    def load_w_in(segs):
        s = wslot[0] % 6
        wslot[0] += 1
        key = "wa%d" % s
        v = WA[:, s * 2048:(s + 1) * 2048].rearrange("p (a b) -> p a b", a=16)
        for (c0, n, off) in segs:
            k.dma("pool", v[:, :, off:off + n], w_in[:, c0:c0 + n].rearrange("(kc p) c -> p kc c", p=128),
                  [], [key])
        return v, key

    def cm_matmul(v, key, ncols, T, t0=0):
        b = mmctr[0] % 2
        mmctr[0] += 1
        pk = "psA%d" % b
        for kc in range(16):
            k.mm(psA[b][0:ncols, 0:T], v[:, kc, 0:ncols], XN[:, kc, t0:t0 + T], kc == 0, kc == 15,
                 [key, "XN"], [pk])
        return psA[b], pk

    def run_pass(kind, xsrc, T, nseg, L, C, pidx, last_pre=False, first_main=False, last_main=False):
        full = kind != "pre"
        smp = kind == "smp"
        NCH = T // C
        ntile = max(1, T // 128)
        rows = min(128, T)
        MK = M64 if C == 64 else M4
        nst = int(round(math.log2(C)))

        def seg3(ap2):
            return ap2.rearrange("p (s l) -> p s l", s=nseg)

        def ch3(ap2):
            return ap2.rearrange("p (n c) -> p n c", n=NCH)

        k.dma("sp", NW[:], nmix.partition_broadcast(128), [], ["NW"])
        for ti in range(ntile):
            hk = "H%d" % ti
            k.dma("sp", H[0:rows, ti, :], xsrc[ti * 128:ti * 128 + rows, :], [], [hk])
            k.act(XNT[0:rows, :], H[0:rows, ti, :], AF.Square, [hk], ["XNT", "ST0"], accum=ST[0:rows, 0:1])
            k.act(ST[0:rows, 1:2], ST[0:rows, 0:1], AF.Sqrt, ["ST0", "EPSC"], ["ST1"], bias=EPSC[0:rows, 0:1],
                  scale=1.0 / D)
            k.recip(ST[0:rows, 1:2], ST[0:rows, 1:2], ["ST1"], ["ST1"])
            k.stt(XNT[0:rows, :], H[0:rows, ti, :], ST[0:rows, 1:2], NW[0:rows, :], ALU.mult, ALU.mult,
                  [hk, "ST1", "NW"], ["XNT"])
            for half in range(2):
                for j in range(8):
                    kc = half * 8 + j
                    k.tr(psTb[:, j, 0:rows], XNT[0:rows, kc * 128:(kc + 1) * 128], IDB[0:rows, 0:rows],
                         ["XNT", "IDB"], ["psTb"])
                k.cp(XN[:, half * 8:(half + 1) * 8, ti * 128:ti * 128 + rows], psTb[:, :, 0:rows],
                     ["psTb"], ["XN"], eng=("act" if half else "dve"))

        if stop_phase <= 1:
            return
        if smp:
            for c8 in range(6):
                k.dma("sp", SSR[:, :], sshift[:, c8 * 1024:(c8 + 1) * 1024], [], ["SSR"])
                for j in range(8):
                    k.tr(psTf[:, j * 16:(j + 1) * 16], SSR[:, j * 128:(j + 1) * 128], IDF[0:16, 0:16],
                         ["SSR", "IDF"], ["psTf"])
                k.cp(SHS[:, c8 * 8:(c8 + 1) * 8, :], psTf[:, 0:128].rearrange("p (b s) -> p b s", b=8),
                     ["psTf"], ["SHS"])
            k.dma("sp", SSR[:, 0:448], sshift[:, 6144:6592], [], ["SSR"])
            for j, (o, n) in enumerate(((0, 96), (96, 96), (192, 128), (320, 128))):
                k.tr(psTf[0:n, j * 16:(j + 1) * 16], SSR[:, o:o + n], IDF[0:16, 0:16], ["SSR", "IDF"], ["psTf"])
            k.memset(SHS[:, 48:50, :], 0.0, ["SHS"])
            k.cp(SHS[0:96, 48:50, :], psTf[0:96, 0:32].rearrange("p (b s) -> p b s", b=2), ["psTf"], ["SHS"])
            k.cp(SHS[:, 50:52, :], psTf[:, 32:64].rearrange("p (b s) -> p b s", b=2), ["psTf"], ["SHS"])

        def shift_block(ps, pk, blk, nr, out2, okeys):
            PX = PEXT[0:nr, 0:nseg * (L + 1)].rearrange("p (s l) -> p s l", s=nseg)
            if smp:
                prev = SHS[0:nr, blk, :]
                pkey = "SHS"
            else:
                prev = SHIFT[0:nr, blk:blk + 1]
                pkey = "SHIFT"
            k.cp(PX[:, :, 0], prev, [pkey], ["PEXT"])
            k.act(PX[:, :, 1:L + 1], seg3(ps[0:nr, 0:T]), AF.Copy, [pk], ["PEXT"])
            k.cp(prev, PX[:, :, L], ["PEXT"], [pkey])
            k.ts(seg3(TMP2[0:nr, 0:T]), PX[:, :, 0:L], MU[0:nr, blk:blk + 1], None, ALU.mult, None,
                 ["PEXT", "CP"], ["TMP2"])
            k.stt(seg3(out2), PX[:, :, 1:L + 1], OMU[0:nr, blk:blk + 1], seg3(TMP2[0:nr, 0:T]),
                  ALU.mult, ALU.add, ["PEXT", "OMU", "TMP2"], okeys)

        v, key = load_w_in([(6144, 96, 0)])
        ps, pk = cm_matmul(v, key, 96, T)
        shift_block(ps, pk, 48, 96, TMP1[0:96, 0:T], ["TMP1"])
        k.act(TW[0:96, 0:T], TMP1[0:96, 0:T], AF.Tanh, ["TMP1"], ["TW"])
        v, key = load_w_in([(6240, 96, 0)])
        ps, pk = cm_matmul(v, key, 96, T)
        shift_block(ps, pk, 49, 96, DAb[0:96, 0:T], ["DAb"])
        if full or last_pre:
            for j in range(2):
                v, key = load_w_in([(6336 + j * 128, 128, 0)])
                ps, pk = cm_matmul(v, key, 128, T)
                shift_block(ps, pk, 50 + j, 128, TMP1[:, 0:T], ["TMP1"])
                k.act(SG[:, j, 0:T], TMP1[:, 0:T], AF.Sigmoid, ["TMP1"], ["SG"])

        if stop_phase <= 2:
            return
        if kind == "main":
            k.cp(KC[:, :, 0:128], KC[:, :, 512:640], ["KC"], ["KC"])
            k.cp(VT[:, 0, :], VT[:, 4, :], ["VT"], ["VT"])
        if kind == "main" or last_pre:
            tl = T - 128
            for g in range(8):
                v, key = load_w_in([(KA0 + g * 64, 64, 0), (KA0 + g * 64, 64, 64)])
                if kind == "main":
                    ps, pk = cm_matmul(v, key, 128, T)
                    k.cp(KC[:, g, 128:640], ps[:, 0:T], [pk], ["KC"], eng="act")
                else:
                    ps, pk = cm_matmul(v, key, 128, 128, t0=tl)
                    k.cp(KC[:, g, 512:640], ps[:, 0:128], [pk], ["KC"], eng="act")
            for j in range(4):
                v, key = load_w_in([(VA0 + j * 128, 128, 0)])
                for ti in (range(ntile) if kind == "main" else [ntile - 1]):
                    b = mmctr[0] % 2
                    mmctr[0] += 1
                    pk = "psA%d" % b
                    for kc in range(16):
                        k.mm(psA[b][:, 0:128], XN[:, kc, ti * 128:(ti + 1) * 128], v[:, kc, :], kc == 0, kc == 15,
                             [key, "XN"], [pk])
                    k.cp(VT[:, 1 + ti, j * 128:(j + 1) * 128], psA[b][:, 0:128], [pk], ["VT"], eng="act")
                    if last_main and ti == ntile - 1:
                        k.cp(VWO[:, j * 128:(j + 1) * 128], psA[b][:, 0:128], [pk], ["E1"])
            if last_main and not (_DBG & 4):
                for j in range(4):
                    v, key = load_w_in([(KA0 + j * 128, 128, 0)])
                    b = mmctr[0] % 2
                    mmctr[0] += 1
                    pk = "psA%d" % b
                    for kc in range(16):
                        k.mm(psA[b][:, 0:128], XN[:, kc, tl:tl + 128], v[:, kc, :], kc == 0, kc == 15,
                             [key, "XN"], [pk])
                    k.cp(KWO[:, j * 128:(j + 1) * 128], psA[b][:, 0:128], [pk], ["CUM"])
                k.dma("sp", kwin_p, KWO[:], ["CUM"], ["o_kwin"], is_out=True)
                k.dma("sp", vwin_p, VWO[:], ["E1"], ["o_vwin"], is_out=True)
        if smp:
            for j in range(8):
                v, key = load_w_in([((KA0 if j < 4 else VA0) + (j % 4) * 128, 128, 0)])
                b = mmctr[0] % 2
                mmctr[0] += 1
                pk = "psA%d" % b
                for kc in range(16):
                    k.mm(psA[b][0:64, 0:128], XN[:, kc, 0:64], v[:, kc, :], kc == 0, kc == 15, [key, "XN"], [pk])
                if j < 4:
                    k.cp(KNT[:, j * 128:(j + 1) * 128], psA[b][0:64, 0:128], [pk], ["KNT"])
                else:
                    jj = j - 4
                    k.cp(VNT[:, jj * 128:(jj + 1) * 128], psA[b][0:64, 0:128], [pk], ["VNT"])
                    k.cp(VNTb[:, 2 * jj:2 * jj + 2, :], psA[b][0:64, 0:128].rearrange("p (g d) -> p g d", g=2),
                         [pk], ["VNTb"], eng="act")
            k.cp(QT[:, 0:512], KNT[:, :], ["KNT"], ["QT"])
            for g in range(8):
                k.tr(psTb[0:64, g, 0:64], QT[:, g * 64:(g + 1) * 64], IDB[0:64, 0:64], ["QT", "IDB"], ["psTb"])
            k.cp(KNC[:, :, :], psTb[0:64, :, 0:64], ["psTb"], ["KNC"])

        def wkv_pair(p, get_state, put_state):
            ARv = AR[:, 0:2 * T].rearrange("p (n a c) -> p n a c", n=NCH, a=2)
            ABR = psB4[0:C, 0:4 * 2 * C].rearrange("p (h c) -> p h c", h=4)
            AKR = psB5[0:C, 0:4 * 2 * C].rearrange("p (h c) -> p h c", h=4)
            IBp = psB6[0:C, 256:256 + 4 * C].rearrange("p (h c) -> p h c", h=4)
            IA = psB4[0:C, 0:4 * 2 * C].rearrange("p (h c) -> p h c", h=4)
            mask2 = MK[0:C, 0:2, :].rearrange("p a c -> p (a c)")
            fin = {}

            def prep(gi):
                sl = gi % 2
                gk = "g%d" % sl
                TMg = TM3[sl][0:C, :, :, :]
                tmk = "TM3_%d" % sl
                for nl in range(2):
                    n = gi * 2 + nl
                    for qi, (SRC, sk) in enumerate(((VB, "VB"), (KT, "KT"), (BT, "BT"))):
                        k.tr(psTb[0:C, nl * 3 + qi, :], SRC[:, n * C:(n + 1) * C], IDB[:, :], [sk, "IDB"], ["psTb"])
                k.cp(TMg, psTb[0:C, 0:6, :].rearrange("p (n a) c -> p n a c", n=2), ["psTb"], [tmk], eng="act")
                yield
                for nl in range(2):
                    n = gi * 2 + nl
                    for hh in range(2):
                        hc = nl * 2 + hh
                        k.mm(ABR[:, hc, :], BTP[:, hh, n * C:(n + 1) * C], AR[:, n * 2 * C:(n + 1) * 2 * C], True, True,
                             ["BTP", "AR"], ["psB4"])
                        k.mm(AKR[:, hc, :], KTP[:, hh, n * C:(n + 1) * C], AR[:, n * 2 * C:(n + 1) * 2 * C], True, True,
                             ["KTP", "AR"], ["psB5"])
                abrm = ABRM[sl][0:C, :, 0:2 * C]
                akrm = AKRM[sl][0:C, :, 0:2 * C]
                k.tt(abrm, ABR, bc(mask2.unsqueeze(1), [C, 4, 2 * C]), ALU.mult, ["psB4", "MK"], [gk + "abr"])
                k.tt(akrm, AKR, bc(mask2.unsqueeze(1), [C, 4, 2 * C]), ALU.mult, ["psB5", "MK"], [gk + "akr"])
                yield
                cur = 0
                ptc = PTC[sl][cur][0:C, :, :, 0:C]
                pt = PT[sl][cur][0:C, :, 0:C]
                for hc in range(4):
                    k.tr(psTb[0:C, hc, 0:C], ABRM[sl][0:C, hc, 0:C], IDB[0:C, 0:C], [gk + "abr", "IDB"], ["psTb"])
                k.cp(pt, psTb[0:C, 0:4, 0:C], ["psTb"], [gk + "pt0"])
                k.cp(ptc[:, :, 0, :], ABRM[sl][0:C, :, 0:C], [gk + "abr"], [gk + "ptc0"], eng="act")
                k.cp(ptc[:, :, 1, :], bc(IDB[0:C, 0:C].unsqueeze(1), [C, 4, C]), ["IDB"], [gk + "ptc0"])
                yield
                for s_ in range(nst):
                    last = s_ == nst - 1
                    ptc = PTC[sl][cur][0:C, :, :, 0:C]
                    pt = PT[sl][cur][0:C, :, 0:C]
                    nx = 1 - cur
                    ptcn = PTC[sl][nx][0:C, :, :, 0:C]
                    ptn = PT[sl][nx][0:C, :, 0:C]
                    ck_ = gk + "ptc%d" % cur
                    pk_ = gk + "pt%d" % cur
                    for hc in range(4):
                        if last:
                            k.mm(IA[:, hc, 0:C], pt[:, hc, :], ptc[:, hc, 1, :], True, True, [ck_, pk_], ["psB4"])
                        else:
                            k.mm(IA[:, hc, :], pt[:, hc, :], ptc[:, hc, :, :], True, True, [ck_, pk_], ["psB4"])
                            k.mm(IBp[:, hc, :], ptc[:, hc, 0, :], pt[:, hc, :], True, True, [ck_, pk_], ["psB6"])
                    if last:
                        k.tt(ptcn[:, :, 1, :], ptc[:, :, 1, :], IA[:, :, 0:C], ALU.add, ["psB4", ck_],
                             [gk + "ptc%d" % nx])
                    else:
                        k.cp(ptcn[:, :, 0, :], IA[:, :, 0:C], ["psB4"], [gk + "ptc%d" % nx], eng="act")
                        k.tt(ptcn[:, :, 1, :], ptc[:, :, 1, :], IA[:, :, C:2 * C], ALU.add, ["psB4", ck_],
                             [gk + "ptc%d" % nx])
                        k.cp(ptn, IBp, ["psB6"], [gk + "pt%d" % nx], eng="act")
                    cur = nx
                    yield
                fin[gi] = cur

            def chain(gi):
                sl = gi % 2
                gk = "g%d" % sl
                TMg = TM3[sl][0:C, :, :, :]
                tmk = "TM3_%d" % sl
                cur = fin[gi]
                tk = gk + "ptc%d" % cur
                Tfin = PTC[sl][cur][0:C, :, 1, 0:C]
                for nl in range(2):
                    n = gi * 2 + nl
                    sf, sbf, skey = get_state(n)
                    Zp = psB7[0:C, 0:128].rearrange("p (h i) -> p h i", h=2)
                    Up = psB7[0:C, 128:256].rearrange("p (h i) -> p h i", h=2)
                    for hh in range(2):
                        hc = nl * 2 + hh
                        k.mm(Zp[:, hh, :], ARv[:, n, 0, :], sbf[:, hh, :], True, False, ["AR", skey + "b"], ["psB7"])
                        k.mm(Zp[:, hh, :], AKRM[sl][0:C, hc, 0:C], TMg[:, nl, 0, hh * 64:(hh + 1) * 64], False, True,
                             [gk + "akr", tmk], ["psB7"])
                    k.cp(ZB[0:C, :, :], Zp, ["psB7"], ["ZB"], eng="act")
                    yield
                    for hh in range(2):
                        hc = nl * 2 + hh
                        k.mm(Up[:, hh, :], Tfin[:, hc, :], ZB[0:C, hh, :], True, True, [tk, "ZB"], ["psB7"])
                    k.cp(UB[0:C, :, :], Up, ["psB7"], ["UB"])
                    yield
                    if full:
                        for hh in range(2):
                            hc = nl * 2 + hh
                            hs = slice(hh * 64, hh * 64 + 64)
                            o = psB7[hs, 256:256 + C]
                            k.mm(o, sbf[:, hh, :], ARv[:, n, 1, :], True, False, [skey + "b", "AR"], ["psB7"])
                            k.mm(o, UB[0:C, hh, :], ABRM[sl][0:C, hc, C:2 * C], False, False,
                                 ["UB", gk + "abr"], ["psB7"])
                            k.mm(o, TMg[:, nl, 0, hh * 64:(hh + 1) * 64], AKRM[sl][0:C, hc, C:2 * C], False, True,
                                 [tmk, gk + "akr"], ["psB7"])
                        k.cp(Yt[:, n * C:(n + 1) * C], psB7[:, 256:256 + C], ["psB7"], ["Yt"], eng="act")
                        yield
                    for hh in range(2):
                        hs = slice(hh * 64, hh * 64 + 64)
                        o = psB7[hs, 384:448]
                        k.mm(o, TMg[:, nl, 2, hh * 64:(hh + 1) * 64], UB[0:C, hh, :], True, False,
                             [tmk, "UB"], ["psB7"])
                        k.mm(o, TMg[:, nl, 1, hh * 64:(hh + 1) * 64], TMg[:, nl, 0, hh * 64:(hh + 1) * 64],
                             False, True, [tmk], ["psB7"])
                    k.tt(sf, sf, psB7[:, 384:448], ALU.add, ["psB7", skey], [skey])
                    k.ts(sf, sf, WC[:, n:n + 1], None, ALU.mult, None, [skey, "WC"], [skey])
                    k.cp(sbf[0:64, 0, :], sf[0:64, :], [skey], [skey + "b"], eng="act")
                    k.cp(sbf[64:128, 1, :], sf[64:128, :], [skey], [skey + "b"], eng="act")
                    put_state(n, sf, skey)
                    yield

            NG = NCH // 2
            for _ in prep(0):
                pass
            for gi in range(NG):
                ga = chain(gi)
                gb = prep(gi + 1) if gi + 1 < NG else iter(())
                da = db = False
                while not (da and db):
                    if not da:
                        try:
                            next(ga)
                        except StopIteration:
                            da = True
                    if not db:
                        try:
                            next(gb)
                        except StopIteration:
                            db = True

        octr = [0]

        def state_out(sf, skey, dst):
            k.cp(SOB[0:64, 0:64], sf[0:64, :], [skey], ["SOB"])
            k.cp(SOB[64:128, 64:128], sf[64:128, :], [skey], ["SOB"])
            k.tr(psTf[:, 128:256], SOB[:, :], IDF[:, :], ["SOB", "IDF"], ["psTf"])
            so = SOUT[octr[0] % 2]
            sok = "SOUT%d" % (octr[0] % 2)
            octr[0] += 1
            k.cp(so[0:64, :], psTf[0:64, 128:192], ["psTf"], [sok])
            k.cp(so[64:128, :], psTf[64:128, 192:256], ["psTf"], [sok], eng="act")
            k.dma("sp", dst, so[:, :], [sok], ["o_wkv"], is_out=True)

        for p in range((16 if stop_phase >= 4 else 1) if stop_phase > 3 else 0):
            for (c0, blk, dst, dk) in ((p * 128, p, Rt, "Rt"), (2048 + p * 128, 16 + p, Kt, "Kt"),
                                       (4096 + p * 128, 32 + p, Vt, "Vt")):
                if kind == "pre" and dk == "Rt":
                    if not last_pre:
                        continue
                v, key = load_w_in([(c0, 128, 0)])
                ps, pk = cm_matmul(v, key, 128, T)
                shift_block(ps, pk, blk, 128, dst[:, 0:T], [dk])
            if stop_phase <= 3.1:
                continue
            lsl = p % 2
            lk = "LWT%d" % lsl
            k.dma("pool", LWT[0:96, lsl, 0, :], w_lora[:, p * 128:(p + 1) * 128], [], [lk])
            k.dma("pool", LWT[0:96, lsl, 1, :], a_lora[:, p * 128:(p + 1) * 128], [], [lk])
            if full:
                k.dma("pool", LWT[:, lsl, 2:4, :],
                      g_lora[:, p * 128:(p + 1) * 128].rearrange("(j q) c -> q j c", q=128), [], [lk])
            b = mmctr[0] % 2
            mmctr[0] += 1
            pk = "psA%d" % b
            k.mm(psA[b][:, 0:T], LWT[0:96, lsl, 0, :], TW[0:96, 0:T], True, True, [lk, "TW"], [pk])
            k.act(LW[:, 0:T], psA[b][:, 0:T], AF.Sigmoid, [pk, "CP"], ["LW"], bias=W0C[:, p:p + 1])
            k.ts(LW[:, 0:T], LW[:, 0:T], -math.exp(-0.5), None, ALU.mult, None, ["LW"], ["LW"])
            b = mmctr[0] % 2
            mmctr[0] += 1
            pk = "psA%d" % b
            k.mm(psA[b][:, 0:T], LWT[0:96, lsl, 1, :], DAb[0:96, 0:T], True, True, [lk, "DAb"], [pk])
            k.act(ASIG[:, 0:T], psA[b][:, 0:T], AF.Sigmoid, [pk, "CP"], ["ASIG"], bias=A0C[:, p:p + 1])
            if full:
                b = mmctr[0] % 2
                mmctr[0] += 1
                pk = "psA%d" % b
                for j in range(2):
                    k.mm(psA[b][:, 0:T], LWT[:, lsl, 2 + j, :], SG[:, j, 0:T], j == 0, j == 1,
                         [lk, "SG"], [pk])
                k.cp(Gb[:, 0:T], psA[b][:, 0:T], [pk], ["Gb"], eng="act")
            if stop_phase <= 3.2:
                continue
            k.ts(KK[:, 0:T], Kt[:, 0:T], KKC[:, p:p + 1], None, ALU.mult, None, ["Kt", "CP"], ["KK"])
            k.act(SQ[:, 0:T], KK[:, 0:T], AF.Square, ["KK"], ["SQ"])
            b = mmctr[0] % 2
            mmctr[0] += 1
            pk = "psA%d" % b
            k.mm(psA[b][:, 0:T], BONES[:, :], SQ[:, 0:T], True, True, ["BONES", "SQ"], [pk])
            k.ts(TMP1[:, 0:T], ASIG[:, 0:T], KAC[:, p:p + 1], KAC[:, p:p + 1], ALU.mult, ALU.subtract, ["ASIG", "CP"], ["TMP1"])
            k.stt(Kt[:, 0:T], TMP1[:, 0:T], 1.0, Kt[:, 0:T], ALU.add, ALU.mult, ["TMP1", "Kt"], ["Kt"])
            S.op("dve", lambda e: e.tensor_tensor_scan(out=CUM[:, 0:T], data0=ONES[:, 0:T], data1=LW[:, 0:T],
                                                       initial=0.0, op0=ALU.mult, op1=ALU.add),
                 ["ONES", "LW"], ["CUM"])
            if NCH > 1:
                k.cp(BASE[:, 1:NCH], ch3(CUM[:, 0:T])[:, 0:NCH - 1, C - 1], ["CUM"], ["BASE"])
            k.tt(ch3(CUM[:, 0:T]), ch3(CUM[:, 0:T]), bc(BASE[:, 0:NCH].unsqueeze(2), [128, NCH, C]), ALU.subtract,
                 ["CUM", "BASE"], ["CUM"])
            k.act(TMP2[:, 0:T], psA[b][:, 0:T], AF.Sqrt, [pk], ["TMP2"])
            k.ts(TMP2[:, 0:T], TMP2[:, 0:T], 1e-12, None, ALU.max, None, ["TMP2"], ["TMP2"])
            k.recip(TMP2[:, 0:T], TMP2[:, 0:T], ["TMP2"], ["TMP2"])
            k.tt(KK[:, 0:T], KK[:, 0:T], TMP2[:, 0:T], ALU.mult, ["KK", "TMP2"], ["KK"])
            k.tt(TMP1[:, 0:T], KK[:, 0:T], ASIG[:, 0:T], ALU.mult, ["KK", "ASIG"], ["TMP1"])
            ARv = AR[:, 0:2 * T].rearrange("p (n a c) -> p n a c", n=NCH, a=2)
            k.act(E1[:, 0:T], CUM[:, 0:T], AF.Exp, ["CUM"], ["E1"])
            k.cp(WC[:, 0:NCH], ch3(E1[:, 0:T])[:, :, C - 1], ["E1"], ["WC"])
            if full:
                k.tt(ARv[:, :, 1, :], ch3(Rt[:, 0:T]), ch3(E1[:, 0:T]), ALU.mult, ["Rt", "E1"], ["AR"])
            k.tt(TMP2[:, 0:T], CUM[:, 0:T], LW[:, 0:T], ALU.subtract, ["CUM", "LW"], ["TMP2"])
            k.act(E1[:, 0:T], TMP2[:, 0:T], AF.Exp, ["TMP2"], ["E1"])
            k.stt(ARv[:, :, 0, :], ch3(KK[:, 0:T]), -1.0, ch3(E1[:, 0:T]), ALU.mult, ALU.mult, ["KK", "E1"], ["AR"])
            k.act(E1[:, 0:T], CUM[:, 0:T], AF.Exp, ["CUM"], ["E1"], scale=-1.0)
            k.tt(KT[:, 0:T], Kt[:, 0:T], E1[:, 0:T], ALU.mult, ["Kt", "E1"], ["KT"])
            k.tt(BT[:, 0:T], TMP1[:, 0:T], E1[:, 0:T], ALU.mult, ["TMP1", "E1"], ["BT"])
            for hh_ in range(2):
                hs_ = slice(hh_ * 64, hh_ * 64 + 64)
                k.cp(KTP[hs_, hh_, 0:T], KT[hs_, 0:T], ["KT"], ["KTP"], eng="act")
                k.cp(BTP[hs_, hh_, 0:T], BT[hs_, 0:T], ["BT"], ["BTP"])
            k.cp(VB[:, 0:T], Vt[:, 0:T], ["Vt"], ["VB"], eng="act")
            if stop_phase <= 3.4:
                continue
            if smp:
                def get_state(n, p=p):
                    i2 = n % 2
                    k.dma("sp", SWI[0:64, 0:64], swkv[n, p, 0:64, :], [], ["SWI"])
                    k.dma("sp", SWI[64:128, 64:128], swkv[n, p, 64:128, :], [], ["SWI"])
                    k.tr(psTf[:, 0:128], SWI[:, :], IDF[:, :], ["SWI", "IDF"], ["psTf"])
                    k.cp(SS[i2][0:64, :], psTf[0:64, 0:64], ["psTf"], ["SS%d" % i2])
                    k.cp(SS[i2][64:128, :], psTf[64:128, 64:128], ["psTf"], ["SS%d" % i2])
                    k.cp(SSB[i2][0:64, 0, :], SS[i2][0:64, :], ["SS%d" % i2], ["SS%db" % i2], eng="act")
                    k.cp(SSB[i2][64:128, 1, :], SS[i2][64:128, :], ["SS%d" % i2], ["SS%db" % i2], eng="act")
                    return SS[i2][:, :], SSB[i2][:, :, :], "SS%d" % i2

                def put_state(n, sf, skey, p=p):
                    state_out(sf, skey, wkv_s[n, p])
                wkv_pair(p, get_state, put_state)
            else:
                def get_state(n, p=p):
                    return SST[:, p, :], SBF[:, p, :, :], "SST%d" % p

                def put_state(n, sf, skey):
                    pass
                wkv_pair(p, get_state, put_state)
                if last_main and not (_DBG & 1):
                    state_out(SST[:, p, :], "SST%d" % p, wkv_p[p])
            if not full:
                continue
            k.cp(SQ[:, 0:T], Yt[:, 0:T], ["Yt"], ["SQ"], eng="act")
            b = mmctr[0] % 2
            mmctr[0] += 1
            pk = "psA%d" % b
            k.mm(psA[b][:, 0:T], BONES[:, :], SQ[:, 0:T], True, True, ["BONES", "SQ"], [pk])
            k.stt(Yt[:, 0:T], psA[b][:, 0:T], -1.0 / 64.0, Yt[:, 0:T], ALU.mult, ALU.add, [pk, "Yt"], ["Yt"])
            k.act(SQ[:, 0:T], Yt[:, 0:T], AF.Square, ["Yt"], ["SQ"])
            b = mmctr[0] % 2
            mmctr[0] += 1
            pk = "psA%d" % b
            k.mm(psA[b][:, 0:T], BONES[:, :], SQ[:, 0:T], True, True, ["BONES", "SQ"], [pk])
            k.act(TMP2[:, 0:T], psA[b][:, 0:T], AF.Sqrt, [pk, "EPSC"], ["TMP2"], bias=EPSC[:, 1:2], scale=1.0 / 64.0)
            k.recip(TMP2[:, 0:T], TMP2[:, 0:T], ["TMP2"], ["TMP2"])
            k.tt(Yt[:, 0:T], Yt[:, 0:T], TMP2[:, 0:T], ALU.mult, ["Yt", "TMP2"], ["Yt"])
            k.ts(Yt[:, 0:T], Yt[:, 0:T], LNW[:, p:p + 1], LNB[:, p:p + 1], ALU.mult, ALU.add, ["Yt", "CP"], ["Yt"])
            k.tt(TMP1[:, 0:T], Rt[:, 0:T], Kt[:, 0:T], ALU.mult, ["Rt", "Kt"], ["TMP1"])
            k.ts(SQ[:, 0:T], TMP1[:, 0:T], RKC[:, p:p + 1], None, ALU.mult, None, ["TMP1", "CP"], ["SQ"])
            b = mmctr[0] % 2
            mmctr[0] += 1
            pk = "psA%d" % b
            k.mm(psA[b][:, 0:T], BONES[:, :], SQ[:, 0:T], True, True, ["BONES", "SQ"], [pk])
            k.tt(TMP1[:, 0:T], psA[b][:, 0:T], Vt[:, 0:T], ALU.mult, [pk, "Vt"], ["TMP1"])
            k.tt(Yt[:, 0:T], Yt[:, 0:T], TMP1[:, 0:T], ALU.add, ["Yt", "TMP1"], ["Yt"])
            k.tt(Yt[:, 0:T], Yt[:, 0:T], Gb[:, 0:T], ALU.mult, ["Yt", "Gb"], ["Yt"])
            v, key = load_w_in([(GA0 + p * 128, 128, 0)])
            ps, pk = cm_matmul(v, key, 128, T)
            k.act(SGA[:, 0:T], ps[:, 0:T], AF.Sigmoid, [pk], ["SGA"])
            v, key = load_w_in([(GB0 + p * 128, 128, 0)])
            ps, pk = cm_matmul(v, key, 128, T)
            if smp:
                k.act(SGBS[:, p, :], ps[:, 0:T], AF.Sigmoid, [pk], ["SGBS"])
                k.tt(MIX[:, p, 0:T], Yt[:, 0:T], SGA[:, 0:T], ALU.mult, ["Yt", "SGA"], ["MIX"])
                v, key = load_w_in([(Q0 + p * 128, 128, 0)])
                b = mmctr[0] % 2
                mmctr[0] += 1
                pk = "psA%d" % b
                for kc in range(16):
                    k.mm(psA[b][0:64, 0:128], XN[:, kc, 0:64], v[:, kc, :], kc == 0, kc == 15, [key, "XN"], [pk])
                k.cp(QT[:, p * 128:(p + 1) * 128], psA[b][0:64, 0:128], [pk], ["QT"], eng="act")
                continue
            k.act(SGB[:, 0:T], ps[:, 0:T], AF.Sigmoid, [pk], ["SGB"])
            k.tt(MIXA[:, 0:T], Yt[:, 0:T], SGA[:, 0:T], ALU.mult, ["Yt", "SGA"], ["PEXT"])
            v, key = load_w_in([(Q0 + p * 128, 128, 0)])
            ps, pk = cm_matmul(v, key, 128, T)
            k.cp(QP[0:64, 0, 0:T], ps[0:64, 0:T], [pk], ["QP"], eng="act")
            k.cp(QP[64:128, 1, 0:T], ps[64:128, 0:T], [pk], ["QP"], eng="act")
            g = p // 2
            for hh in range(2):
                h = 2 * p + hh
                k.ts(BH[:, hh, :], DM[:, :], SLOPES[h], None, ALU.mult, None, ["DM"], ["BH"])
                if first_main:
                    k.ts(BH0[:, hh, :], DM0[:, :], SLOPES[h], None, ALU.mult, None, ["DM0"], ["BH0"])
            for n in range(T // 128):
                for hh in range(2):
                    h = 2 * p + hh
                    hs = slice(hh * 64, hh * 64 + 64)
                    bh = BH0 if (first_main and n == 0) else BH
                    bhk = "BH0" if (first_main and n == 0) else "BH"
                    k.mm(psB5[:, 0:256], QP[:, hh, n * 128:(n + 1) * 128], KC[:, g, n * 128:n * 128 + 256], True, True,
                         ["QP", "KC"], ["psB5"])
                    k.stt(TS[:, :], psB5[:, 0:256], 0.125, bh[:, hh, :], ALU.mult, ALU.add, ["psB5", bhk], ["TS"])
                    k.red(ST[:, 4:5], TS[:, :], ALU.max, ["TS"], ["ST4"])
                    k.tt(ST[:, 5:6], ST[:, 4:5], SINKB[:, h:h + 1], ALU.max, ["ST4", "SINKB"], ["ST5"])
                    k.ts(ST[:, 5:6], ST[:, 5:6], -1.0, None, ALU.mult, None, ["ST5"], ["ST5"])
                    k.act(PB[:, :], TS[:, :], AF.Exp, ["TS", "ST5"], ["PB", "ST6"], bias=ST[:, 5:6], accum=ST[:, 6:7])
                    k.act(ST[:, 7:8], ST[:, 5:6], AF.Exp, ["ST5", "SINKB"], ["ST7"], bias=SINKB[:, h:h + 1])
                    k.tt(ST[:, 8:9], ST[:, 6:7], ST[:, 7:8], ALU.add, ["ST6", "ST7"], ["ST8"])
                    k.recip(ST[:, 8:9], ST[:, 8:9], ["ST8"], ["ST8"])
                    k.ts(PN[:, :], PB[:, :], ST[:, 8:9], None, ALU.mult, None, ["PB", "ST8"], ["PN"])
                    k.tr(psTb[:, 0, :], PN[:, 0:128], IDB[:, :], ["PN", "IDB"], ["psTb"])
                    k.tr(psTb[:, 1, :], PN[:, 128:256], IDB[:, :], ["PN", "IDB"], ["psTb"])
                    k.cp(PNT[:, :].rearrange("p (a c) -> p a c", a=2), psTb[:, 0:2, :], ["psTb"], ["PNT"], eng="act")
                    o = psB7[hs, 256:384]
                    k.mm(o, VT[:, n, g * 64:(g + 1) * 64], PNT[:, 0:128], True, False, ["VT", "PNT"], ["psB7"])
                    k.mm(o, VT[:, n + 1, g * 64:(g + 1) * 64], PNT[:, 128:256], False, True, ["VT", "PNT"], ["psB7"])
                k.tt(TMP1[:, 0:128], psB7[:, 256:384], SGB[:, n * 128:(n + 1) * 128], ALU.mult, ["psB7", "SGB"],
                     ["TMP1"])
                k.tt(MIX[:, p, n * 128:(n + 1) * 128], TMP1[:, 0:128], MIXA[:, n * 128:(n + 1) * 128], ALU.add,
                     ["TMP1", "PEXT"], ["MIX"])

        if not full:
            return

        if smp:
            for h in range(32):
                k.tr(psTb[0:64, h % 8, 0:64], QT[:, h * 64:(h + 1) * 64], IDB[0:64, 0:64], ["QT", "IDB"], ["psTb"])
                if h % 8 == 7:
                    k.cp(QC[:, :, h - 7:h + 1, :], psTb[0:64, :, 0:64].rearrange("p h (s t) -> p s h t", s=16),
                         ["psTb"], ["QC"], eng=("act" if h % 16 == 7 else "dve"))
            k.memset(PNF[:, :, :], 0.0, ["PNF"])
            SCa = psB4[0:16, :].rearrange("p (g t) -> p g t", g=4)
            SCb = psB5[0:16, :].rearrange("p (g t) -> p g t", g=4)
            SCn = psB6[0:16, 0:32].rearrange("p (g t) -> p g t", g=8)
            for s in range(16):
                k.dma("sp", CKf[:, :], ckd[s], [], ["CKf"])
                k.dma("sp", CVf[:, :], cvd[s], [], ["CVf"])
                k.cp(CKb[:, :, :], CKf[:, :].rearrange("p (g d) -> p g d", g=8), ["CKf"], ["CKb"])
                k.cp(CVb[:, :, :], CVf[:, :].rearrange("p (g d) -> p g d", g=8), ["CVf"], ["CVb"], eng="act")
                k.dma("sp", kwin_s[s, 0:124, :], ckd[s, 4:128, :], [], ["o_kws"], is_out=True)
                k.dma("sp", vwin_s[s, 0:124, :], cvd[s, 4:128, :], [], ["o_vws"], is_out=True)
                for g in range(8):
                    k.tr(psTb[0:64, g, :], CKb[:, g, :], IDB[:, :], ["CKb", "IDB"], ["psTb"])
                k.cp(CKC[:, :, :], psTb[0:64, :, :], ["psTb"], ["CKC"], eng="act")
                for g in range(8):
                    sc = (SCa if g < 4 else SCb)
                    sk_ = "psB4" if g < 4 else "psB5"
                    lq = QC[:, s, 4 * g:4 * g + 4, :].rearrange("p h t -> p (h t)")
                    k.mm(sc[:, g % 4, :], lq, CKC[:, g, :], True, True, ["QC", "CKC"], [sk_])
                    k.mm(SCn[:, g, :], lq, KNC[:, g, 4 * s:4 * s + 4], True, True, ["QC", "KNC"], ["psB6"])
                k.stt(TSs[:, 0:4, 0:128], SCa, 0.125, BIASC[:, 0:4, :], ALU.mult, ALU.add, ["psB4", "BIASC"], ["TSs"])
                k.stt(TSs[:, 4:8, 0:128], SCb, 0.125, BIASC[:, 4:8, :], ALU.mult, ALU.add, ["psB5", "BIASC"], ["TSs"])
                k.stt(TSs[:, :, 128:132], SCn, 0.125, BIASN[:, :, :], ALU.mult, ALU.add, ["psB6", "BIASN"], ["TSs"])
                k.red(ST[0:16, 0:8], TSs[:, :, :], ALU.max, ["TSs"], ["STs0"])
                k.tt(ST[0:16, 0:8], ST[0:16, 0:8], SINKT[:, :], ALU.max, ["STs0", "SINKT"], ["STs0"])
                k.tt(TSs[:, :, :], TSs[:, :, :], bc(ST[0:16, 0:8].unsqueeze(2), [16, 8, 132]), ALU.subtract,
                     ["TSs", "STs0"], ["TSs"])
                k.act(TSs[:, :, :], TSs[:, :, :], AF.Exp, ["TSs"], ["TSs"])
                k.red(ST[0:16, 8:16], TSs[:, :, :], ALU.add, ["TSs"], ["STs1"])
                k.tt(ST[0:16, 0:8], SINKT[:, :], ST[0:16, 0:8], ALU.subtract,
                     ["STs0", "SINKT"], ["STs0"])
                k.act(ST[0:16, 0:8], ST[0:16, 0:8], AF.Exp, ["STs0"], ["STs0"])
                k.tt(ST[0:16, 8:16], ST[0:16, 8:16], ST[0:16, 0:8], ALU.add, ["STs0", "STs1"], ["STs1"])
                k.recip(ST[0:16, 8:16], ST[0:16, 8:16], ["STs1"], ["STs1"])
                k.tt(PNs[:, :, :], TSs[:, :, :], bc(ST[0:16, 8:16].unsqueeze(2), [16, 8, 132]), ALU.mult,
                     ["TSs", "STs1"], ["PNs"])
                k.cp(PNF[:, :, 4 * s:4 * s + 4], PNs[:, :, 128:132], ["PNs"], ["PNF"])
                for g in range(8):
                    k.tr(psTb[:, 0, g * 16:(g + 1) * 16], PNs[:, g, 0:128], IDB[0:16, 0:16], ["PNs", "IDB"], ["psTb"])
                    k.tr(psTb[0:64, 1, g * 16:(g + 1) * 16], PNF[:, g, :], IDB[0:16, 0:16], ["PNF", "IDB"], ["psTb"])
                k.cp(PNTc[:, :, :], psTb[:, 0, :].rearrange("p (g t) -> p g t", g=8), ["psTb"], ["PNTc"])
                k.cp(PNTn[:, :, :], psTb[0:64, 1, :].rearrange("p (g t) -> p g t", g=8), ["psTb"], ["PNTn"], eng="act")
                k.memset(PNF[:, :, 4 * s:4 * s + 4], 0.0, ["PNF"])
                Op = psB7[:, 256:320].rearrange("p (a t) -> p a t", a=16)
                for pp in range(16):
                    g = pp // 2
                    for hh in range(2):
                        hl = (pp % 2) * 2 + hh
                        hs = slice(hh * 64, hh * 64 + 64)
                        k.mm(Op[hs, pp, :], CVb[:, g, :], PNTc[:, g, hl * 4:(hl + 1) * 4], True, False,
                             ["CVb", "PNTc"], ["psB7"])
                        k.mm(Op[hs, pp, :], VNTb[:, g, :], PNTn[:, g, hl * 4:(hl + 1) * 4], False, True,
                             ["VNTb", "PNTn"], ["psB7"])
                k.cp(YBS[:, :, 4 * s:4 * s + 4], Op, ["psB7"], ["YBS"])
                k.dma("sp", kwin_s[s, 124:128, :], KNT[4 * s:4 * s + 4, :], ["KNT"], ["o_kws"], is_out=True)
                k.dma("sp", vwin_s[s, 124:128, :], VNT[4 * s:4 * s + 4, :], ["VNT"], ["o_vws"], is_out=True)
            k.tt(YBS[:, :, :], YBS[:, :, :], SGBS[:, :, :], ALU.mult, ["YBS", "SGBS"], ["YBS"])
            k.tt(MIX[:, :, 0:64], MIX[:, :, 0:64], YBS[:, :, :], ALU.add, ["MIX", "YBS"], ["MIX"])
            for c8 in range(6):
                for j in range(8):
                    k.tr(psTf[0:16, (j % 4) * 128:(j % 4 + 1) * 128], SHS[:, c8 * 8 + j, :], IDF[:, :],
                         ["SHS", "IDF"], ["psTf"])
                    if j % 4 == 3:
                        k.cp(SSR[:, (j - 3) * 128:(j + 1) * 128], psTf[0:16, 0:512], ["psTf"], ["SSR"])
                k.dma("sp", shift_s[:, c8 * 1024:(c8 + 1) * 1024], SSR[:, :], ["SSR"], ["o_shs"], is_out=True)
            for j, (o, n, blk) in enumerate(((0, 96, 48), (96, 96, 49), (192, 128, 50), (320, 128, 51))):
                k.tr(psTf[0:16, j * 128:j * 128 + n], SHS[0:n, blk, :], IDF[0:n, 0:n], ["SHS", "IDF"], ["psTf"])
                k.cp(SSR[:, o:o + n], psTf[0:16, j * 128:j * 128 + n], ["psTf"], ["SSR"])
            k.dma("sp", shift_s[:, 6144:6592], SSR[:, 0:448], ["SSR"], ["o_shs"], is_out=True)

        if last_main and not (_DBG & 2):
            for blk in range(52):
                if blk < 48:
                    c0, nr = blk * 128, 128
                else:
                    c0, nr = ((6144, 96), (6240, 96), (6336, 128), (6464, 128))[blk - 48]
                k.dma("sp", shift_p[c0:c0 + nr, :], SHIFT[0:nr, blk:blk + 1], ["SHIFT"], ["o_shp"], is_out=True)

        wctr = [0]

        def load_w3(src3, nslots_key):
            s = wctr[0] % 3
            wctr[0] += 1
            keys = ["wa%d" % (2 * s), "wa%d" % (2 * s + 1)]
            return s, keys

        for cc in range(8):
            s, keys = load_w3(None, None)
            v = WA[:, s * 4096:(s + 1) * 4096].rearrange("p (a b) -> p a b", a=16)
            k.dma("pool", v, w_out[:, cc * 256:(cc + 1) * 256].rearrange("(kc p) c -> p kc c", p=128), [], keys)
            for ti in range(ntile):
                b = mmctr[0] % 2
                mmctr[0] += 1
                pk = "psA%d" % b
                for kc in range(16):
                    k.mm(psA[b][0:rows, 0:256], MIX[:, kc, ti * 128:ti * 128 + rows], v[:, kc, :], kc == 0, kc == 15,
                         keys + ["MIX"], [pk])
                hk = "H%d" % ti
                k.tt(H[0:rows, ti, cc * 256:(cc + 1) * 256], H[0:rows, ti, cc * 256:(cc + 1) * 256],
                     psA[b][0:rows, 0:256], ALU.add, [pk, hk], [hk])
        k.dma("sp", NW[:], nmlp.partition_broadcast(128), [], ["NW"])
        for ti in range(ntile):
            hk = "H%d" % ti
            k.act(XNT[0:rows, :], H[0:rows, ti, :], AF.Square, [hk], ["XNT", "ST0"], accum=ST[0:rows, 0:1])
            k.act(ST[0:rows, 1:2], ST[0:rows, 0:1], AF.Sqrt, ["ST0", "EPSC"], ["ST1"], bias=EPSC[0:rows, 0:1],
                  scale=1.0 / D)
            k.recip(ST[0:rows, 1:2], ST[0:rows, 1:2], ["ST1"], ["ST1"])
            k.stt(XNT[0:rows, :], H[0:rows, ti, :], ST[0:rows, 1:2], NW[0:rows, :], ALU.mult, ALU.mult,
                  [hk, "ST1", "NW"], ["XNT"])
            for half in range(2):
                for j in range(8):
                    kc = half * 8 + j
                    k.tr(psTb[:, j, 0:rows], XNT[0:rows, kc * 128:(kc + 1) * 128], IDB[0:rows, 0:rows],
                         ["XNT", "IDB"], ["psTb"])
                k.cp(XN[:, half * 8:(half + 1) * 8, ti * 128:ti * 128 + rows], psTb[:, :, 0:rows],
                     ["psTb"], ["XN"], eng=("act" if half else "dve"))
        UT = [Rt, Kt]
        UTb = [(KT, "KT"), (BT, "BT"), (VB, "VB"), (SQ, "SQ")]
        for sb_ in range(32):
            s, ukeys = load_w3(None, None)
            wu = WA[:, s * 4096:(s + 1) * 4096].rearrange("p (a b) -> p a b", a=16)
            k.dma("pool", wu, w_up[:, sb_ * 256:(sb_ + 1) * 256].rearrange("(kc p) c -> p kc c", p=128), [], ukeys)
            s2, dkeys = load_w3(None, None)
            wd = WA[:, s2 * 4096:(s2 + 1) * 4096].rearrange("p (a b) -> p a b", a=2)
            k.dma("pool", wd, w_down[sb_ * 256:(sb_ + 1) * 256, :].rearrange("(fb p) c -> p fb c", p=128), [], dkeys)
            uts = []
            for fb in range(2):
                b = mmctr[0] % 2
                mmctr[0] += 1
                pk = "psA%d" % b
                for kc in range(16):
                    k.mm(psA[b][:, 0:T], wu[:, kc, fb * 128:(fb + 1) * 128], XN[:, kc, 0:T], kc == 0, kc == 15,
                         ukeys + ["XN"], [pk])
                ut, utk = UTb[(sb_ % 2) * 2 + fb]
                k.act(TMP1[:, 0:T], psA[b][:, 0:T], AF.Relu, [pk], ["TMP1"])
                k.tt(ut[:, 0:T], TMP1[:, 0:T], TMP1[:, 0:T], ALU.mult, ["TMP1"], [utk])
                uts.append((ut, utk))
            for ti in range(ntile):
                hk = "H%d" % ti
                for cc in range(4):
                    b = mmctr[0] % 2
                    mmctr[0] += 1
                    pk = "psA%d" % b
                    for fb in range(2):
                        ut, utk = uts[fb]
                        k.mm(psA[b][0:rows, 0:512], ut[:, ti * 128:ti * 128 + rows], wd[:, fb, cc * 512:(cc + 1) * 512],
                             fb == 0, fb == 1, dkeys + [utk], [pk])
                    k.tt(H[0:rows, ti, cc * 512:(cc + 1) * 512], H[0:rows, ti, cc * 512:(cc + 1) * 512],
                         psA[b][0:rows, 0:512], ALU.add, [pk, hk], [hk])
        k.dma("sp", NW[:], nfin.partition_broadcast(128), [], ["NW"])
        ydst = y_smp if smp else y_main
        for ti in range(ntile):
            hk = "H%d" % ti
            k.act(XNT[0:rows, :], H[0:rows, ti, :], AF.Square, [hk], ["XNT", "ST0"], accum=ST[0:rows, 0:1])
            k.act(ST[0:rows, 1:2], ST[0:rows, 0:1], AF.Sqrt, ["ST0", "EPSC"], ["ST1"], bias=EPSC[0:rows, 0:1],
                  scale=1.0 / D)
            k.recip(ST[0:rows, 1:2], ST[0:rows, 1:2], ["ST1"], ["ST1"])
            k.stt(H[0:rows, ti, :], H[0:rows, ti, :], ST[0:rows, 1:2], NW[0:rows, :], ALU.mult, ALU.mult,
                  [hk, "ST1", "NW"], [hk])
            r0 = (0 if smp else pidx * 512) + ti * 128
            k.dma("sp", ydst[r0:r0 + rows, :], H[0:rows, ti, :], [hk], ["o_y"], is_out=True)

    def fz(e):
        return e.memset(DUM[:, 0:1], 0.0)

    if passes is not None:
        for i_, nm in enumerate(passes.split(",")):
            if i_ > 0:
                S.fence(fz)
            if nm == "p0":
                run_pass("pre", xpre[0:512, :], 512, 1, 512, 64, 0)
            elif nm == "p1":
                run_pass("pre", xpre[512:1024, :], 512, 1, 512, 64, 1, last_pre=True)
            elif nm == "m0":
                run_pass("main", xmain[0:512, :], 512, 1, 512, 64, 0, first_main=True)
            elif nm == "m1":
                run_pass("main", xmain[512:1024, :], 512, 1, 512, 64, 1, last_main=True)
            elif nm == "m1x":
                run_pass("main", xmain[512:1024, :], 512, 1, 512, 64, 1)
            elif nm == "s":
                run_pass("smp", xsmp, 64, 16, 4, 4, 0)
        S.emit()
        k.st.close()
        return nc
    run_pass("pre", xpre[0:512, :], 512, 1, 512, 64, 0)
    if npass > 1:
        S.fence(fz)
        run_pass("pre", xpre[512:1024, :], 512, 1, 512, 64, 1, last_pre=True)
    if npass > 2:
        S.fence(fz)
        run_pass("main", xmain[0:512, :], 512, 1, 512, 64, 0, first_main=True)
    if npass > 3:
        S.fence(fz)
        run_pass("main", xmain[512:1024, :], 512, 1, 512, 64, 1, last_main=True)
    if npass > 4:
        S.fence(fz)
        run_pass("smp", xsmp, 64, 16, 4, 4, 0)
    S.emit()
    k.st.close()
    return nc


def _consts(half):
    c = {}
    c["ident"] = np.eye(128, dtype=np.float32)
    m64 = np.zeros((64, 3, 64), np.float32)
    m64[:, 0, :] = np.triu(np.ones((64, 64), np.float32), 1)
    m64[:, 1, :] = np.triu(np.ones((64, 64), np.float32), 0)
    m64[:, 2, :] = np.tril(np.ones((64, 64), np.float32), -1)
    c["mask64"] = m64
    c["mask4"] = np.ascontiguousarray(m64[0:4, :, 0:4])
    bo = np.zeros((128, 128), np.float32)
    bo[0:64, 0:64] = 1.0
    bo[64:128, 64:128] = 1.0
    c["bones"] = bo
    i = np.arange(128)[:, None]
    kj = np.arange(256)[None, :]
    dist = i - kj + 128
    dm = np.where((dist >= 0) & (dist <= 128), -dist.astype(np.float32), np.float32(NEG)).astype(np.float32)
    c["dm"] = dm
    dm0 = dm.copy()
    if half == 0:
        dm0[:, 0:128] = NEG
    c["dm0"] = dm0
    bc_ = np.zeros((16, 8, 128), np.float32)
    bn_ = np.zeros((16, 8, 4), np.float32)
    for hl in range(4):
        for t in range(4):
            r = hl * 4 + t
            for g in range(8):
                sl = SLOPES[4 * g + hl]
                cc = np.arange(128)
                bc_[r, g, :] = np.where(cc >= t, -sl * (t + 128 - cc), NEG)
                tp = np.arange(4)
                bn_[r, g, :] = np.where(tp <= t, -sl * (t - tp), NEG)
    c["biasc"] = bc_
    c["biasn"] = bn_
    return c


_NC_CACHE = {}


def make_in_maps(inp):
    f = lambda a: np.ascontiguousarray(np.asarray(a, dtype=np.float32))
    x_prompt = f(inp["x_prompt"])
    x_sample = f(inp["x_sample"])
    state_shift = f(inp["state_shift"])[0]
    state_wkv = f(inp["state_wkv"])[0]
    cache_k = f(inp["cache_k_win"])[0]
    cache_v = f(inp["cache_v_win"])[0]
    mu = f(inp["tshift_mu"])[0]
    cp = np.zeros((128, NCP), np.float32)
    cp[:, 0:48] = mu[0:6144].reshape(48, 128).T
    cp[0:96, 48] = mu[6144:6240]
    cp[0:96, 49] = mu[6240:6336]
    cp[:, 50] = mu[6336:6464]
    cp[:, 51] = mu[6464:6592]
    for j, name in enumerate(("w0", "a0", "k_k", "k_a", "r_k", "ln_x_w", "ln_x_b")):
        cp[:, 52 + 16 * j:68 + 16 * j] = f(inp[name])[0].reshape(2048).reshape(16, 128).T
    sinks = f(inp["attn_sinks"])[0]
    sinkb = np.ascontiguousarray(np.tile(sinks[None, :], (128, 1)))
    sinkt = np.zeros((16, 8), np.float32)
    for hl in range(4):
        for t in range(4):
            sinkt[hl * 4 + t, :] = sinks[hl::4]
    shared = dict(
        w_in=f(inp["w_in"])[0], w_out=f(inp["w_out"])[0], w_up=f(inp["w_up"])[0], w_down=f(inp["w_down"])[0],
        w_lora=f(inp["w_lora"])[0], a_lora=f(inp["a_lora"])[0], g_lora=f(inp["g_lora"])[0],
        nmix=f(inp["norm_mix_w"])[0], nmlp=f(inp["norm_mlp_w"])[0], nfin=f(inp["norm_final_w"]),
        cp=cp, sinkb=sinkb, sinkt=sinkt)
    in_maps = []
    for c in range(NCORES):
        b, half = c // 2, c % 2
        m = dict(shared)
        m.update(_consts(half))
        m["xpre"] = x_prompt[b, 0:1024] if half == 1 else np.zeros((1024, D), np.float32)
        m["xmain"] = np.ascontiguousarray(x_prompt[b, half * 1024:(half + 1) * 1024])
        m["xsmp"] = np.ascontiguousarray(x_sample[16 * c:16 * c + 16].reshape(64, D))
        m["sshift"] = np.ascontiguousarray(state_shift[16 * c:16 * c + 16])
        m["swkv"] = np.ascontiguousarray(state_wkv[16 * c:16 * c + 16].reshape(16, 16, 128, 64))
        m["ck"] = np.ascontiguousarray(cache_k[16 * c:16 * c + 16].reshape(16, 128, 512))
        m["cv"] = np.ascontiguousarray(cache_v[16 * c:16 * c + 16].reshape(16, 128, 512))
        in_maps.append(m)
    return in_maps


def kernel(**inp):
    in_maps = make_in_maps(inp)
    if "nc" not in _NC_CACHE:
        _NC_CACHE["nc"] = build_program()
    nc = _NC_CACHE["nc"]
    res = run_bass_kernel_spmd(nc, in_maps, core_ids=list(range(NCORES))).results
    y_prompt = np.zeros((4, 2048, D), np.float32)
    y_sample = np.zeros((128, 4, D), np.float32)
    shift_p = np.zeros((1, 4, RC), np.float32)
    wkv_p = np.zeros((1, 4, 32, 64, 64), np.float32)
    kw_p = np.zeros((1, 4, 128, 8, 64), np.float32)
    vw_p = np.zeros((1, 4, 128, 8, 64), np.float32)
    shift_s = np.zeros((1, 128, RC), np.float32)
    wkv_s = np.zeros((1, 128, 32, 64, 64), np.float32)
    kw_s = np.zeros((1, 128, 128, 8, 64), np.float32)
    vw_s = np.zeros((1, 128, 128, 8, 64), np.float32)
    for c in range(NCORES):
        b, half = c // 2, c % 2
        r = res[c]
        y_prompt[b, half * 1024:(half + 1) * 1024] = r["y_main"]
        y_sample[16 * c:16 * c + 16] = r["y_smp"].reshape(16, 4, D)
        shift_s[0, 16 * c:16 * c + 16] = r["shift_s"]
        wkv_s[0, 16 * c:16 * c + 16] = r["wkv_s"].reshape(16, 32, 64, 64)
        kw_s[0, 16 * c:16 * c + 16] = r["kwin_s"].reshape(16, 128, 8, 64)
        vw_s[0, 16 * c:16 * c + 16] = r["vwin_s"].reshape(16, 128, 8, 64)
        if half == 1:
            shift_p[0, b] = r["shift_p"].reshape(RC)
            wkv_p[0, b] = r["wkv_p"].reshape(32, 64, 64)
            kw_p[0, b] = r["kwin_p"].reshape(128, 8, 64)
            vw_p[0, b] = r["vwin_p"].reshape(128, 8, 64)
    return (y_prompt, y_sample, shift_p, wkv_p, kw_p, vw_p, shift_s, wkv_s, kw_s, vw_s)
```
